# Optimizing a Trainium2 kernel written in Bass

```python
import math
import jax, jax.numpy as jnp
from jax import lax
import numpy as np

D_MODEL = 2048
BATCH = 4
SEQ = 4096
DEPTH = 1

EPS = 1e-6
NEG_INF = -1e30
TINY = 1e-30
FORCE_SCORE = 1e4

NSA_HEADS = 16
NSA_KV_GROUPS = 4
NSA_REP = NSA_HEADS // NSA_KV_GROUPS
NSA_HEAD_DIM = 128
N_NSA_BRANCH = 3
CMP_BLOCK = 32
CMP_STRIDE = 16
SLC_BLOCK = 64
SLC_TOPK = 16
WINDOW = 512
NSA_QBLOCK = 32

GDN_HEADS = 16
GDN_HEAD_DIM = 128
GDN_CONV = 4
GDN_CHUNK = 64

D_FF = -(-(8 * D_MODEL) // (3 * 256)) * 256

NSA_Q_DIM = NSA_HEADS * NSA_HEAD_DIM
NSA_KV_DIM = N_NSA_BRANCH * 2 * NSA_KV_GROUPS * NSA_HEAD_DIM
NSA_GATE_DIM = NSA_HEADS * N_NSA_BRANCH
GDN_QKV_DIM = 3 * GDN_HEADS * GDN_HEAD_DIM
GDN_Z_DIM = GDN_HEADS * GDN_HEAD_DIM
IN_SPLITS = (NSA_Q_DIM, NSA_KV_DIM, NSA_GATE_DIM, GDN_QKV_DIM, GDN_Z_DIM, GDN_HEADS, GDN_HEADS, D_MODEL, D_MODEL)
D_IN = sum(IN_SPLITS)

kernel_name = "hybrid_nsa_gdn_swiglu_block"


def _rms_norm(t, gain):
    tf = t.astype(jnp.float32)
    y = tf * lax.rsqrt(jnp.mean(tf * tf, axis=-1, keepdims=True) + EPS)
    return (y * gain.astype(jnp.float32)).astype(t.dtype)


def _l2norm(t):
    tf = t.astype(jnp.float32)
    return tf * lax.rsqrt(jnp.sum(tf * tf, axis=-1, keepdims=True) + EPS)


def _masked_softmax(s, mask):
    s = jnp.where(mask, s.astype(jnp.float32), NEG_INF)
    e = jnp.where(mask, jnp.exp(s - jnp.max(s, axis=-1, keepdims=True)), 0.0)
    return e / jnp.maximum(jnp.sum(e, axis=-1, keepdims=True), TINY)


def _split_cols(t, sizes):
    out, start = [], 0
    for s in sizes:
        out.append(t[..., start:start + s])
        start += s
    return out


def _nsa_mixer(q, kv, gate_logits, q_gain, k_gain, cmp_pos, w_cmp):
    B_, S_, _ = q.shape
    G, R, dh = NSA_KV_GROUPS, NSA_REP, NSA_HEAD_DIM
    scale = dh ** -0.5
    q = _rms_norm(q.reshape(B_, S_, G, R, dh), q_gain)
    kv = kv.reshape(B_, S_, N_NSA_BRANCH, 2, G, dh)

    n_cmp = (S_ - CMP_BLOCK) // CMP_STRIDE + 1
    cmp_start = np.arange(n_cmp) * CMP_STRIDE
    blk_idx = cmp_start[:, None] + np.arange(CMP_BLOCK)[None, :]
    k_raw, v_raw = kv[:, :, 0, 0], kv[:, :, 0, 1]
    k_cmp = jnp.einsum('bnlgd,lde->bnge', k_raw[:, blk_idx] + cmp_pos[0][:, None, :], w_cmp[0])
    v_cmp = jnp.einsum('bnlgd,lde->bnge', v_raw[:, blk_idx] + cmp_pos[1][:, None, :], w_cmp[1])
    k_cmp = _rms_norm(k_cmp, k_gain[0])
    cmp_end = jnp.asarray(cmp_start + CMP_BLOCK - 1, jnp.int32)

    n_sel = S_ // SLC_BLOCK
    n_topk = min(SLC_TOPK, n_sel)
    k_sel = _rms_norm(kv[:, :, 1, 0], k_gain[1])
    k_sel_blocks = k_sel.reshape(B_, n_sel, SLC_BLOCK, G, dh).transpose(0, 3, 1, 2, 4)
    v_sel_blocks = kv[:, :, 1, 1].reshape(B_, n_sel, SLC_BLOCK, G, dh).transpose(0, 3, 1, 2, 4)
    sel_start = np.arange(n_sel) * SLC_BLOCK
    overlap = np.minimum(cmp_start[:, None] + CMP_BLOCK, sel_start[None, :] + SLC_BLOCK) - np.maximum(cmp_start[:, None], sel_start[None, :])
    agg = jnp.asarray(np.clip(overlap, 0, None) / CMP_BLOCK, jnp.float32)
    b_ix = jnp.arange(B_)[:, None, None, None]
    g_ix = jnp.arange(G)[None, None, :, None]

    k_win = jnp.pad(_rms_norm(kv[:, :, 2, 0], k_gain[2]), ((0, 0), (WINDOW, 0), (0, 0), (0, 0)))
    v_win = jnp.pad(kv[:, :, 2, 1], ((0, 0), (WINDOW, 0), (0, 0), (0, 0)))

    gates = jax.nn.sigmoid(gate_logits.astype(jnp.float32)).reshape(B_, S_, G, R, N_NSA_BRANCH)
    QC = NSA_QBLOCK
    nq = S_ // QC

    def block(args):
        qc, gc, start = args
        t = start + jnp.arange(QC, dtype=jnp.int32)
        s = jnp.einsum('bqgrd,bngd->bqgrn', qc, k_cmp) * scale
        p_cmp = _masked_softmax(s, (cmp_end[None, :] <= t[:, None])[None, :, None, None, :])
        o_cmp = jnp.einsum('bqgrn,bngd->bqgrd', p_cmp, v_cmp)
        imp = jnp.einsum('bqgrn,nj->bqgj', p_cmp, agg)
        blk = jnp.arange(n_sel, dtype=jnp.int32)[None, :]
        cur = t[:, None] // SLC_BLOCK
        forced = (blk == 0) | (blk == cur) | (blk == cur - 1)
        causal = blk * SLC_BLOCK <= t[:, None]
        score = jnp.where(forced[None, :, None, :], FORCE_SCORE, jnp.where(causal[None, :, None, :], imp, -FORCE_SCORE))
        _, sel = lax.top_k(score, n_topk)
        ks = k_sel_blocks[b_ix, g_ix, sel]
        vs = v_sel_blocks[b_ix, g_ix, sel]
        key_pos = sel[..., None] * SLC_BLOCK + jnp.arange(SLC_BLOCK, dtype=jnp.int32)
        m_sel = (key_pos <= t[None, :, None, None, None])[:, :, :, None].reshape(B_, QC, G, 1, n_topk * SLC_BLOCK)
        s = jnp.einsum('bqgrd,bqgkld->bqgrkl', qc, ks).reshape(B_, QC, G, R, n_topk * SLC_BLOCK) * scale
        p = _masked_softmax(s, m_sel).reshape(B_, QC, G, R, n_topk, SLC_BLOCK)
        o_slc = jnp.einsum('bqgrkl,bqgkld->bqgrd', p, vs)
        kw = lax.dynamic_slice_in_dim(k_win, start, WINDOW + QC, axis=1)
        vw = lax.dynamic_slice_in_dim(v_win, start, WINDOW + QC, axis=1)
        pos = start - WINDOW + jnp.arange(WINDOW + QC, dtype=jnp.int32)
        rel = t[:, None] - pos[None, :]
        m_win = ((pos[None, :] >= 0) & (rel >= 0) & (rel < WINDOW))[None, :, None, None, :]
        s = jnp.einsum('bqgrd,bkgd->bqgrk', qc, kw) * scale
        o_win = jnp.einsum('bqgrk,bkgd->bqgrd', _masked_softmax(s, m_win), vw)
        o = gc[..., 0:1] * o_cmp + gc[..., 1:2] * o_slc + gc[..., 2:3] * o_win
        return o.astype(qc.dtype)

    q_chunks = jnp.moveaxis(q.reshape(B_, nq, QC, G, R, dh), 1, 0)
    g_chunks = jnp.moveaxis(gates.reshape(B_, nq, QC, G, R, N_NSA_BRANCH), 1, 0)
    starts = jnp.arange(nq, dtype=jnp.int32) * QC
    out = lax.map(block, (q_chunks, g_chunks, starts))
    return jnp.moveaxis(out, 0, 1).reshape(B_, S_, G * R * dh)


def _gated_delta_chunked(q, k, v, g, beta):
    B_, S_, H, dk = q.shape
    dv = v.shape[-1]
    C = GDN_CHUNK
    N = S_ // C
    out_dtype = v.dtype

    def chunks(t):
        return jnp.moveaxis(t.astype(jnp.float32).reshape(B_, N, C, H, -1), 3, 2)

    q, k, v = chunks(q), chunks(k), chunks(v)
    g = jnp.cumsum(chunks(g[..., None])[..., 0], axis=-1)
    beta = chunks(beta[..., None])[..., 0]
    k_beta = k * beta[..., None]
    v_beta = v * beta[..., None]
    tril = np.tril(np.ones((C, C), bool))
    strict = np.tril(np.ones((C, C), bool), -1)
    diff = g[..., :, None] - g[..., None, :]
    decay = jnp.where(tril, jnp.exp(jnp.where(tril, diff, 0.0)), 0.0)
    a_mat = jnp.eye(C, dtype=jnp.float32) + jnp.where(strict, jnp.einsum('bnhid,bnhjd->bnhij', k_beta, k) * decay, 0.0)
    rhs = jnp.concatenate([v_beta, k_beta * jnp.exp(g)[..., None]], axis=-1)
    sol = lax.linalg.triangular_solve(a_mat, rhs, left_side=True, lower=True, unit_diagonal=True)
    u, w = sol[..., :dv], sol[..., dv:]
    qk = jnp.einsum('bnhid,bnhjd->bnhij', q, k) * decay

    def step(state, xs):
        q_c, k_c, u_c, w_c, g_c, qk_c = xs
        v_new = u_c - jnp.einsum('bhcd,bhde->bhce', w_c, state)
        o = jnp.einsum('bhcd,bhde->bhce', q_c * jnp.exp(g_c)[..., None], state) + jnp.einsum('bhij,bhje->bhie', qk_c, v_new)
        g_last = g_c[..., -1:]
        state = state * jnp.exp(g_last)[..., None] + jnp.einsum('bhcd,bhce->bhde', k_c * jnp.exp(g_last - g_c)[..., None], v_new)
        return state, o

    xs = tuple(jnp.moveaxis(t, 1, 0) for t in (q, k, u, w, g, qk))
    state0 = jnp.zeros((B_, H, dk, dv), jnp.float32)
    _, o = lax.scan(step, state0, xs)
    o = jnp.transpose(o, (1, 0, 3, 2, 4)).reshape(B_, S_, H, dv)
    return o.astype(out_dtype)


def _gdn_mixer(qkv, z, a, b, conv_w, a_log, dt_bias, out_gain):
    B_, S_, _ = qkv.shape
    H, d = GDN_HEADS, GDN_HEAD_DIM
    pad = jnp.pad(qkv, ((0, 0), (GDN_CONV - 1, 0), (0, 0)))
    conv = conv_w[0] * pad[:, 0:S_]
    for i in range(1, GDN_CONV):
        conv = conv + conv_w[i] * pad[:, i:i + S_]
    conv = jax.nn.silu(conv)
    q, k, v = jnp.split(conv, 3, axis=-1)
    q = _l2norm(q.reshape(B_, S_, H, d)) * (d ** -0.5)
    k = _l2norm(k.reshape(B_, S_, H, d))
    v = v.reshape(B_, S_, H, d)
    beta = jax.nn.sigmoid(b.astype(jnp.float32))
    g = -jnp.exp(a_log.astype(jnp.float32)) * jax.nn.softplus(a.astype(jnp.float32) + dt_bias.astype(jnp.float32))
    o = _gated_delta_chunked(q, k, v, g, beta)
    o = _rms_norm(o, out_gain) * jax.nn.silu(z.reshape(B_, S_, H, d))
    return o.reshape(B_, S_, H * d)


def setup_inputs(seed: int = 0) -> dict:
    key = jax.random.key(seed)
    ks = jax.random.split(key, 20)
    L, dh = DEPTH, NSA_HEAD_DIM

    def nrm(k, shape, scale):
        return jax.random.normal(k, shape, jnp.float32) * scale

    def gain(k, shape):
        return 1.0 + 0.02 * jax.random.normal(k, shape, jnp.float32)

    dt = jnp.exp(jax.random.uniform(ks[9], (L, GDN_HEADS), jnp.float32, minval=math.log(1e-3), maxval=math.log(1e-1)))
    return {
        "x": nrm(ks[0], (BATCH, SEQ, D_MODEL), 1.0),
        "attn_norm": gain(ks[1], (L, D_MODEL)),
        "w_in": nrm(ks[2], (L, D_MODEL, D_IN), D_MODEL ** -0.5),
        "nsa_q_norm": gain(ks[3], (L, dh)),
        "nsa_k_norm": gain(ks[4], (L, N_NSA_BRANCH, dh)),
        "cmp_pos": nrm(ks[5], (L, 2, CMP_BLOCK, dh), 0.1),
        "w_cmp": nrm(ks[6], (L, 2, CMP_BLOCK, dh, dh), (CMP_BLOCK * dh) ** -0.5),
        "gdn_conv": nrm(ks[7], (L, GDN_CONV, GDN_QKV_DIM), GDN_CONV ** -0.5),
        "gdn_a_log": jnp.log(jax.random.uniform(ks[8], (L, GDN_HEADS), jnp.float32, minval=1.0, maxval=16.0)),
        "gdn_dt_bias": dt + jnp.log(-jnp.expm1(-dt)),
        "gdn_out_norm": gain(ks[10], (L, GDN_HEAD_DIM)),
        "w_branch_a": nrm(ks[11], (L, NSA_Q_DIM, D_MODEL), NSA_Q_DIM ** -0.5),
        "w_branch_b": nrm(ks[12], (L, GDN_Z_DIM, D_MODEL), GDN_Z_DIM ** -0.5),
        "w_out": nrm(ks[13], (L, D_MODEL, D_MODEL), D_MODEL ** -0.5),
        "ffn_norm": gain(ks[14], (L, D_MODEL)),
        "w_gate": nrm(ks[15], (L, D_MODEL, D_FF), D_MODEL ** -0.5),
        "w_up": nrm(ks[16], (L, D_MODEL, D_FF), D_MODEL ** -0.5),
        "w_down": nrm(ks[17], (L, D_FF, D_MODEL), D_FF ** -0.5),
    }


def reference(x, attn_norm, w_in, nsa_q_norm, nsa_k_norm, cmp_pos, w_cmp, gdn_conv, gdn_a_log, gdn_dt_bias,
              gdn_out_norm, w_branch_a, w_branch_b, w_out, ffn_norm, w_gate, w_up, w_down):
    for l in range(DEPTH):
        h = _rms_norm(x, attn_norm[l])
        proj = h @ w_in[l]
        q_a, kv_a, gate_a, qkv_b, z_b, a_b, b_b, m_a, m_b = _split_cols(proj, IN_SPLITS)
        o_a = _nsa_mixer(q_a, kv_a, gate_a, nsa_q_norm[l], nsa_k_norm[l], cmp_pos[l], w_cmp[l])
        o_b = _gdn_mixer(qkv_b, z_b, a_b, b_b, gdn_conv[l], gdn_a_log[l], gdn_dt_bias[l], gdn_out_norm[l])
        mix = jax.nn.sigmoid(m_a) * (o_a @ w_branch_a[l]) + jax.nn.sigmoid(m_b) * (o_b @ w_branch_b[l])
        x = x + mix @ w_out[l]
        h2 = _rms_norm(x, ffn_norm[l])
        x = x + (jax.nn.silu(h2 @ w_gate[l]) * (h2 @ w_up[l])) @ w_down[l]
    return x
```

```python
import os
import numpy as np
import ml_dtypes
from contextlib import ExitStack
import concourse.bass as bass
import concourse.mybir as mybir
from concourse.bass_utils import run_bass_kernel_spmd

F32 = mybir.dt.float32
BF16 = mybir.dt.bfloat16
AF = mybir.ActivationFunctionType
ALU = mybir.AluOpType

S = 4096
D = 2048
NT = S // 128
DFF = 5632
EPS = 1e-6
EPOCH = 30000
NEG = -4096.0

PHASES = os.environ.get("MK_PHASES", "ABCDE")
EXT_IN = set(filter(None, os.environ.get("MK_EXT_IN", "").split(",")))
EXT_OUT = set(filter(None, os.environ.get("MK_EXT_OUT", "").split(",")))


class Buf:
    __slots__ = ("w", "r", "excl")

    def __init__(self):
        self.w = None
        self.r = []
        self.excl = False


class Tl:
    def __init__(self, t):
        self.t = t
        self.b = Buf()


class Prog:
    ENGS = ("pe", "act", "dve", "pool", "sp")

    def __init__(self, nc, es):
        self.nc = nc
        self.es = es
        self.streams = {e: [] for e in self.ENGS}
        self.count = {e: 0 for e in self.ENGS}
        self.seen = {e: {} for e in self.ENGS}
        self.sems = {}
        self.dcount = {}
        self.dgen = {}
        self.ninstr = 0

    def sem(self, key):
        if key not in self.sems:
            self.sems[key] = self.es.enter_context(self.nc.semaphore("s%d" % len(self.sems)))
        return self.sems[key]

    def _deps(self, eng, reads, writes, ident=None):
        ident = ident or eng
        deps = {}

        def add(tok, raw):
            if tok is None:
                return
            key, val, teng = tok
            if teng == ident and not raw:
                return
            if deps.get(key, 0) < val:
                deps[key] = val

        for b in reads:
            add(b.w, True)
            if b.excl:
                for t in b.r:
                    add(t, False)
        for b in writes:
            add(b.w, False)
            for t in b.r:
                add(t, False)
        waits = []
        seen = self.seen[eng]
        for key, val in deps.items():
            if seen.get(key, 0) < val:
                seen[key] = val
                waits.append((key, val))
        return waits

    def _commit(self, tok, reads, writes):
        for b in reads:
            if len(b.r) > 24:
                best = {}
                for t in b.r:
                    if best.get(t[0], (0,))[0] < t[1]:
                        best[t[0]] = (t[1], t)
                b.r = [v[1] for v in best.values()]
            b.r.append(tok)
        for b in writes:
            b.w = tok
            b.r = []

    def op(self, eng, fn, reads=(), writes=()):
        reads = [x.b if isinstance(x, Tl) else x for x in reads]
        writes = [x.b if isinstance(x, Tl) else x for x in writes]
        waits = self._deps(eng, reads, writes)
        c = self.count[eng]
        key = (eng, c // EPOCH)
        val = c % EPOCH + 1
        self.count[eng] = c + 1
        self.sem(key)
        self.streams[eng].append((waits, fn, key, 1))
        tok = (key, val, eng)
        self._commit(tok, reads, writes)
        self.ninstr += 1
        return tok

    def dma(self, eng, fn, reads=(), writes=(), key=None):
        reads = [x.b if isinstance(x, Tl) else x for x in reads]
        writes = [x.b if isinstance(x, Tl) else x for x in writes]
        w2 = self._deps(eng, reads, writes, ident="dma")
        gen = self.dgen.get(key, 0)
        k = ("dma", key, gen)
        if self.dcount.get(k, 0) + 16 > EPOCH:
            gen += 1
            self.dgen[key] = gen
            k = ("dma", key, gen)
        self.dcount[k] = self.dcount.get(k, 0) + 16
        self.sem(k)
        self.streams[eng].append((w2, fn, k, 16))
        tok = (k, self.dcount[k], "dma")
        self._commit(tok, reads, writes)
        self.ninstr += 1
        return tok

    def barrier(self):
        toks = []
        for e in self.ENGS:
            c = self.count[e]
            if c > 0:
                toks.append(((e, (c - 1) // EPOCH), (c - 1) % EPOCH + 1))
        for k, v in self.dcount.items():
            toks.append((k, v))
        for e in self.ENGS:
            seen = self.seen[e]
            waits = []
            for k, v in toks:
                if seen.get(k, 0) < v:
                    seen[k] = v
                    waits.append((k, v))
            self.streams[e].append((waits, None, None, 0))

    def emit(self):
        nc = self.nc

        def run(engname):
            stream = self.streams[engname]

            def body(e):
                for waits, fn, key, inc in stream:
                    for k, v in waits:
                        e.wait_ge(self.sems[k], v)
                    if fn is not None:
                        fn(e).then_inc(self.sems[key], inc)
            return body

        with nc.Block() as block:
            block.tensor(run("pe"))
            block.scalar(run("act"))
            block.vector(run("dve"))
            block.gpsimd(run("pool"))
            block.sync(run("sp"))
        self.streams = {e: [] for e in self.ENGS}

    def mm(self, out, lhsT, rhs, start=True, stop=True, r=(), w=()):
        return self.op("pe", lambda e: e.matmul(out, lhsT=lhsT, rhs=rhs, start=start, stop=stop), r, w)

    def tr(self, out, in_, ident, r=(), w=()):
        return self.op("pe", lambda e: e.transpose(out=out, in_=in_, identity=ident), r, w)

    def act(self, out, in_, func, r=(), w=(), bias=0.0, scale=1.0, accum=None):
        if accum is None:
            return self.op("act", lambda e: e.activation(out=out, in_=in_, func=func, bias=bias, scale=scale), r, w)
        return self.op("act", lambda e: e.activation(out=out, in_=in_, func=func, bias=bias, scale=scale, accum_out=accum), r, w)

    def copy(self, eng, out, in_, r=(), w=()):
        if eng == "act":
            return self.op("act", lambda e: e.copy(out=out, in_=in_), r, w)
        return self.op(eng, lambda e: e.tensor_copy(out=out, in_=in_), r, w)

    def ts(self, eng, out, in0, s1, s2, op0, op1=None, r=(), w=()):
        if op1 is None:
            return self.op(eng, lambda e: e.tensor_scalar(out=out, in0=in0, scalar1=s1, scalar2=None, op0=op0), r, w)
        return self.op(eng, lambda e: e.tensor_scalar(out=out, in0=in0, scalar1=s1, scalar2=s2, op0=op0, op1=op1), r, w)

    def stt(self, eng, out, in0, scalar, in1, op0, op1, r=(), w=()):
        return self.op(eng, lambda e: e.scalar_tensor_tensor(out=out, in0=in0, scalar=scalar, in1=in1, op0=op0, op1=op1), r, w)

    def tt(self, eng, out, in0, in1, op, r=(), w=()):
        return self.op(eng, lambda e: e.tensor_tensor(out=out, in0=in0, in1=in1, op=op), r, w)

    def recip(self, out, in_, r=(), w=()):
        return self.op("dve", lambda e: e.reciprocal(out=out, in_=in_), r, w)

    def memset(self, eng, ap, val, w=()):
        return self.op(eng, lambda e: e.memset(ap, val), (), w)

    def ld(self, eng, out, in_, w=(), key=None, r=()):
        return self.dma(eng, lambda e: e.dma_start(out=out, in_=in_), r, w, key)

    def st(self, eng, out, in_, r=(), key=None):
        return self.dma(eng, lambda e: e.dma_start(out=out, in_=in_), r, (), key)


_UID = [0]


def _uid():
    _UID[0] += 1
    return _UID[0]


class Ctx:
    def __init__(self, nc, P):
        self.nc = nc
        self.P = P
        self.es = ExitStack()
        self.n = 0
        self.banks = []
        self.bi = 0

    def sb(self, shape, dt=F32):
        self.n += 1
        return Tl(self.es.enter_context(self.nc.sbuf_tensor("t%d" % _uid(), shape, dt)))

    def sbs(self, n, shape, dt=F32):
        return [self.sb(shape, dt) for _ in range(n)]

    def init_psum(self):
        for i in range(8):
            self.n += 1
            self.banks.append(Tl(self.es.enter_context(self.nc.psum_tensor("p%d" % _uid(), [128, 2048], mybir.dt.uint8))))
            self.banks[-1].b.excl = True

    def take(self, dt=F32):
        b = self.banks.pop()
        return b, b.t[:].bitcast(dt)

    def ps(self, dt=F32):
        b = self.banks[self.bi % len(self.banks)]
        self.bi += 1
        return b, b.t[:].bitcast(dt)

    def close(self):
        self.es.close()


class Rot:
    def __init__(self, items):
        self.items = items
        self.i = 0

    def next(self):
        x = self.items[self.i % len(self.items)]
        self.i += 1
        return x


def pretile(w, gc):
    K, N = w.shape
    assert K % 128 == 0 and N % gc == 0
    return np.ascontiguousarray(w.reshape(K // 128, 128, N // gc, gc).transpose(2, 1, 0, 3))


def make_consts():
    c = {}
    n_cmp = 255
    cmp_start = np.arange(n_cmp) * 16
    sel_start = np.arange(64) * 64
    overlap = np.minimum(cmp_start[:, None] + 32, sel_start[None, :] + 64) - np.maximum(cmp_start[:, None], sel_start[None, :])
    agg = np.zeros((256, 64), np.float32)
    agg[:255] = np.clip(overlap, 0, None) / 32.0
    c["agg"] = agg
    n = np.arange(256)
    t = np.arange(S)
    valid = (16 * n[:, None] + 31 <= t[None, :]) & (n[:, None] < 255)
    cb = np.where(valid, 0.0, NEG).astype(np.float32).reshape(2, 128, NT, 128)
    cb = np.broadcast_to(cb.transpose(2, 0, 1, 3)[:, :, :, None, :], (NT, 2, 128, 4, 128))
    c["cmpbias"] = np.ascontiguousarray(cb).reshape(NT, 2, 128, 512).astype(ml_dtypes.bfloat16)
    kl = np.arange(128)[:, None]
    ql = np.arange(128)[None, :]
    tria = np.where(kl > ql, NEG, 0.0).astype(np.float32)
    trib = np.where(kl <= ql, NEG, 0.0).astype(np.float32)
    c["tria"] = np.ascontiguousarray(np.broadcast_to(tria[:, None, :], (128, 4, 128))).reshape(128, 512).astype(ml_dtypes.bfloat16)
    c["trib"] = np.ascontiguousarray(np.broadcast_to(trib[:, None, :], (128, 4, 128))).reshape(128, 512).astype(ml_dtypes.bfloat16)
    ex = np.zeros((64, NT, 128), np.float32)
    for kt in range(NT):
        for k in range(128):
            ex[2 * kt + k // 64, kt, k] = 1.0
    c["expand"] = ex.astype(ml_dtypes.bfloat16)
    i = np.arange(128)
    same = (i[:, None] // 64) == (i[None, :] // 64)
    g = {}
    g["trit"] = (same & (i[:, None] <= i[None, :])).astype(np.float32)
    g["ustr"] = (same & (i[:, None] > i[None, :])).astype(np.float32)
    g["bl"] = np.where(same & (i[None, :] < i[:, None]), 0.0, -30000.0).astype(np.float32)
    g["bu"] = np.where(same & (i[:, None] <= i[None, :]), 0.0, -30000.0).astype(np.float32)
    g["ci0"] = np.broadcast_to((i[:, None] < 64), (128, 128)).astype(np.float32)
    g["ci1"] = np.broadcast_to((i[:, None] >= 64), (128, 128)).astype(np.float32)
    g["ident"] = np.eye(128, dtype=np.float32)
    g["ones"] = np.ones((128, 128), np.float32)
    selmul = np.zeros((S, 64), np.float32)
    seladd = np.zeros((S, 64), np.float32)
    tt_ = np.arange(S)
    cur = tt_ // 64
    jb = np.arange(64)[None, :]
    noncausal = jb > cur[:, None]
    forced0 = (jb == 0) & ~noncausal
    forced1 = (jb == cur[:, None] - 1)
    forced2 = (jb == cur[:, None])
    free = ~(noncausal | forced0 | forced1 | forced2)
    selmul[free] = 1.0
    seladd = np.where(noncausal, -1e4 - jb, 0.0).astype(np.float32)
    seladd = np.where(forced0, 1e4, seladd)
    seladd = np.where(forced1, 1e4 + 1, seladd)
    seladd = np.where(forced2, 1e4 + 2, seladd).astype(np.float32)
    c["selmul"] = np.ascontiguousarray(selmul.reshape(NT, 128, 64))
    c["seladd"] = np.ascontiguousarray(seladd.reshape(NT, 128, 64))
    c["gc"] = np.ascontiguousarray(np.stack([g[k] for k in ("trit", "ustr", "bl", "bu", "ci0", "ci1", "ident", "ones")], axis=1))
    return c


GC_TRIT, GC_USTR, GC_BL, GC_BU, GC_CI0, GC_CI1, GC_ID, GC_ONES = range(8)

FM_GROUPS = 56
TM_GROUPS = 13


def split_w_in(w):
    q = w[:, 0:2048]
    kv = w[:, 2048:5120]
    gate = w[:, 5120:5168]
    gq = w[:, 5168:11312]
    z = w[:, 11312:13360]
    a = w[:, 13360:13376]
    b = w[:, 13376:13392]
    mA = w[:, 13392:15440]
    mB = w[:, 15440:17488]
    kc, vc, ks, vs, kw, vw = [kv[:, i * 512:(i + 1) * 512] for i in range(6)]
    fm = np.concatenate([q, kc, vc, ks, kw, gq, mA, mB], axis=1)
    misc = np.zeros((D, 256), np.float32)
    misc[:, 0:48] = gate
    misc[:, 48:64] = a
    misc[:, 64:80] = b
    tm = np.concatenate([vs, vw, z, misc], axis=1)
    return pretile(fm, 256), pretile(tm, 256)


def build_program():
    nc = bass.Bass("TRN2", target_bir_lowering=False)
    ins = {}

    def inp(name, shape, dt=F32):
        ins[name] = nc.dram_tensor(name, list(shape), dt, kind="ExternalInput").ap()
        return ins[name]

    scr = {}

    def scratch(name, shape, dt=F32):
        kind = "ExternalInput" if name in EXT_IN else ("ExternalOutput" if name in EXT_OUT else "Internal")
        scr[name] = nc.dram_tensor(name, list(shape), dt, kind=kind).ap()
        return scr[name]

    inp("x", [S, D])
    inp("attn_norm", [1, D])
    inp("ffn_norm", [1, D])
    inp("wfm", [FM_GROUPS, 128, 16, 256])
    inp("wtm", [TM_GROUPS, 128, 16, 256])
    inp("qgain", [128, 1])
    inp("kgain", [128, 3])
    inp("posT", [128, 2, 32])
    inp("wcmp", [128, 2, 32, 128])
    inp("convT", [128, 48, 4])
    inp("alog", [1, 16])
    inp("dtb", [1, 16])
    inp("ogain", [1, 128])
    inp("wbra", [8, 128, 16, 256])
    inp("wbrb", [8, 128, 16, 256])
    inp("wout", [8, 128, 16, 256])
    inp("wgate", [22, 128, 16, 256])
    inp("wup", [22, 128, 16, 256])
    inp("wdown", [16, 128, 44, 128])
    inp("agg", [256, 64])
    inp("cmpbias", [NT, 2, 128, 512], BF16)
    inp("tria", [128, 512], BF16)
    inp("trib", [128, 512], BF16)
    inp("expand", [64, NT, 128], BF16)
    inp("gc", [128, 8, 128])
    inp("selmul", [NT, 128, 64])
    inp("seladd", [NT, 128, 64])
    out = nc.dram_tensor("out", [S, D], F32, kind="ExternalOutput").ap()

    scratch("qn", [16, 128, S], BF16)
    scratch("kc", [4, 128, S], BF16)
    scratch("vc", [4, 128, S], BF16)
    scratch("ks", [4, 128, S], BF16)
    scratch("kw", [4, 128, S], BF16)
    scratch("gq", [48, 128, S], F32)
    scratch("sgA", [16, 128, S], F32)
    scratch("sgB", [16, 128, S], F32)
    scratch("vs", [4, S, 128], BF16)
    scratch("vw", [4, S, 128], BF16)
    scratch("z", [S, D], F32)
    scratch("misc", [S, 80], F32)
    scratch("oaT", [16, 128, S], BF16)
    scratch("obT", [16, 128, S], BF16)
    scratch("gqT", [NT, 128, 16, 128], BF16)
    scratch("gkT", [NT, 128, 16, 128], BF16)
    scratch("gktm", [NT, 128, 16, 128], F32)
    scratch("gvtm", [NT, 128, 16, 128], F32)
    scratch("x1", [S, D], F32)

    with ExitStack() as es:
        P = Prog(nc, es)
        if "A" in PHASES:
            phase_A(nc, P, ins, scr)
        if "B" in PHASES:
            phase_B(nc, P, ins, scr)
        if "C" in PHASES:
            phase_C(nc, P, ins, scr)
        if "D" in PHASES:
            phase_D(nc, P, ins, scr)
        if "E" in PHASES:
            phase_E(nc, P, ins, scr, out)
        print("instructions:", P.ninstr, "sems:", len(P.sems))
    return nc


def norm_transpose(P, C, src_rows, gain_bc, ident, hT, col0, ntiles, xts, hbs, junk, ssr, evrot):
    for tt in range(ntiles):
        xt = xts.next()
        hb = hbs.next()
        ss = ssr.next()
        P.ld("sp", xt.t[:], src_rows[tt * 128:(tt + 1) * 128, :], w=[xt], key="xt%d" % (xts.i % 2))
        P.act(junk.t[:], xt.t[:], AF.Square, r=[xt], w=[junk, ss], accum=ss.t[:, 0:1])
        P.act(ss.t[:, 1:2], ss.t[:, 0:1], AF.Sqrt, r=[ss], w=[ss], bias=EPS, scale=1.0 / D)
        P.recip(ss.t[:, 2:3], ss.t[:, 1:2], r=[ss], w=[ss])
        P.stt("dve", hb.t[:], xt.t[:], ss.t[:, 2:3], gain_bc.t[:], ALU.mult, ALU.mult, r=[xt, ss, gain_bc], w=[hb])
        for k4 in range(4):
            pb, pap = C.ps(BF16)
            for kk in range(4):
                k = k4 * 4 + kk
                P.tr(pap[:, kk * 128:(kk + 1) * 128], hb.t[:, k * 128:(k + 1) * 128], ident.t[:], r=[hb, ident], w=[pb])
            eng = evrot.next()
            P.copy(eng, hT.t[:, k4 * 4:(k4 + 1) * 4, col0 + tt * 128: col0 + (tt + 1) * 128],
                   pap[:, 0:512].rearrange("p (a b) -> p a b", a=4), r=[pb], w=[hT])


class WStream:
    def __init__(self, P, C, bf_elems, n_stage=3, n_bf=3, tag="w"):
        self.P = P
        self.st = Rot(C.sbs(n_stage, [128, 2048], F32))
        self.bf = Rot(C.sbs(n_bf, [128, bf_elems], BF16))
        self.tag = tag
        self.n = 0

    def get(self, src, nk, gc):
        P = self.P
        b = self.bf.next()
        bv = b.t[:, 0:nk * gc].rearrange("p (k c) -> p k c", k=nk)
        kstep = max(1, min(2048 // gc, 8))
        k0 = 0
        while k0 < nk:
            k1 = min(nk, k0 + kstep)
            s = self.st.next()
            self.n += 1
            sv = s.t[:, 0:(k1 - k0) * gc].rearrange("p (k c) -> p k c", k=k1 - k0)
            P.ld("sp", sv, src[:, k0:k1, :], w=[s], key="%s%d" % (self.tag, self.n % len(self.st.items)))
            P.copy("pool", bv[:, k0:k1, :], sv, r=[s], w=[b])
            k0 = k1
        return b, bv


def phase_A(nc, P, ins, scr):
    C = Ctx(nc, P)
    C.init_psum()
    hT = C.sb([128, 16, 2048], BF16)
    xts = Rot(C.sbs(2, [128, D], F32))
    hbs = Rot(C.sbs(2, [128, D], BF16))
    junk = C.sb([128, D], BF16)
    gain = C.sb([128, D], F32)
    ssr = Rot(C.sbs(2, [128, 4], F32))
    ident = C.sb([128, 128], BF16)
    ones = C.sb([128, 128], BF16)
    gcf = C.sb([128, 8, 128], F32)
    qg = C.sb([128, 1], F32)
    kg = C.sb([128, 3], F32)
    ws = WStream(P, C, 4096, 3, 3, "wa")
    evf = Rot(C.sbs(3, [128, 512], F32))
    evb = Rot(C.sbs(3, [128, 512], BF16))
    sqs = Rot(C.sbs(2, [128, 512], BF16))
    rts = Rot(C.sbs(2, [128, 512], F32))
    evrot = Rot(["act", "dve"])

    P.ld("sp", gain.t[:], ins["attn_norm"][0:1, :].to_broadcast([128, D]), w=[gain], key="c0")
    P.ld("sp", gcf.t[:], ins["gc"], w=[gcf], key="c1")
    P.ld("sp", qg.t[:], ins["qgain"], w=[qg], key="c2")
    P.ld("sp", kg.t[:], ins["kgain"], w=[kg], key="c3")
    P.copy("dve", ident.t[:], gcf.t[:, GC_ID, :], r=[gcf], w=[ident])
    P.copy("dve", ones.t[:], gcf.t[:, GC_ONES, :], r=[gcf], w=[ones])

    x = ins["x"]
    for half in range(2):
        t0 = half * 2048
        norm_transpose(P, C, x[t0:t0 + 2048, :], gain, ident, hT, 0, 16, xts, hbs, junk, ssr, evrot)
        for g in range(FM_GROUPS):
            wb, wv = ws.get(ins["wfm"][g], 16, 256)
            for sub in range(2):
                ct = g * 2 + sub
                for tg in range(4):
                    pb, pap = C.ps()
                    for k in range(16):
                        P.mm(pap[:, 0:512], wv[:, k, sub * 128:(sub + 1) * 128], hT.t[:, k, tg * 512:(tg + 1) * 512],
                             start=(k == 0), stop=(k == 15), r=[wb, hT], w=[pb])
                    tok = slice(t0 + tg * 512, t0 + (tg + 1) * 512)
                    if ct < 16 or 24 <= ct < 32:
                        if ct < 16:
                            gcol = qg.t[:, 0:1]; gt = qg
                            dst = scr["qn"][ct, :, tok]
                        elif ct < 28:
                            gcol = kg.t[:, 1:2]; gt = kg
                            dst = scr["ks"][ct - 24, :, tok]
                        else:
                            gcol = kg.t[:, 2:3]; gt = kg
                            dst = scr["kw"][ct - 28, :, tok]
                        sq = sqs.next(); rt = rts.next(); eb = evb.next()
                        P.act(sq.t[:], pap[:, 0:512], AF.Square, r=[pb], w=[sq])
                        p2, p2ap = C.ps()
                        P.mm(p2ap[:, 0:512], ones.t[:], sq.t[:], r=[ones, sq], w=[p2])
                        P.act(rt.t[:], p2ap[:, 0:512], AF.Sqrt, r=[p2], w=[rt], bias=EPS, scale=1.0 / 128)
                        P.recip(rt.t[:], rt.t[:], r=[rt], w=[rt])
                        P.stt("dve", eb.t[:], pap[:, 0:512], gcol, rt.t[:], ALU.mult, ALU.mult, r=[pb, gt, rt], w=[eb])
                        P.st("sp", dst, eb.t[:], r=[eb], key="eb%d" % (evb.i % 3))
                    elif ct < 24:
                        eb = evb.next()
                        P.copy(evrot.next(), eb.t[:], pap[:, 0:512], r=[pb], w=[eb])
                        dst = scr["kc"][ct - 16, :, tok] if ct < 20 else scr["vc"][ct - 20, :, tok]
                        P.st("sp", dst, eb.t[:], r=[eb], key="eb%d" % (evb.i % 3))
                    elif ct < 80:
                        ef = evf.next()
                        P.copy(evrot.next(), ef.t[:], pap[:, 0:512], r=[pb], w=[ef])
                        P.st("sp", scr["gq"][ct - 32, :, tok], ef.t[:], r=[ef], key="ef%d" % (evf.i % 3))
                    else:
                        ef = evf.next()
                        P.act(ef.t[:], pap[:, 0:512], AF.Sigmoid, r=[pb], w=[ef])
                        dst = scr["sgA"][ct - 80, :, tok] if ct < 96 else scr["sgB"][ct - 96, :, tok]
                        P.st("sp", dst, ef.t[:], r=[ef], key="ef%d" % (evf.i % 3))
        for g in range(TM_GROUPS):
            wb, wv = ws.get(ins["wtm"][g], 16, 256)
            for tt in range(16):
                pb, pap = C.ps()
                for k in range(16):
                    P.mm(pap[:, 0:256], hT.t[:, k, tt * 128:(tt + 1) * 128], wv[:, k, :],
                         start=(k == 0), stop=(k == 15), r=[wb, hT], w=[pb])
                rows = slice(t0 + tt * 128, t0 + (tt + 1) * 128)
                if g < 4:
                    eb = evb.next()
                    P.copy(evrot.next(), eb.t[:, 0:256], pap[:, 0:256], r=[pb], w=[eb])
                    name = "vs" if g < 2 else "vw"
                    g0 = (g % 2) * 2
                    P.st("sp", scr[name][g0:g0 + 2, rows, :].rearrange("g t d -> t g d"),
                         eb.t[:, 0:256].rearrange("p (g d) -> p g d", g=2), r=[eb], key="eb%d" % (evb.i % 3))
                elif g < 12:
                    ef = evf.next()
                    P.copy(evrot.next(), ef.t[:, 0:256], pap[:, 0:256], r=[pb], w=[ef])
                    P.st("sp", scr["z"][rows, (g - 4) * 256:(g - 3) * 256], ef.t[:, 0:256], r=[ef], key="ef%d" % (evf.i % 3))
                else:
                    ef = evf.next()
                    P.act(ef.t[:, 0:48], pap[:, 0:48], AF.Sigmoid, r=[pb], w=[ef])
                    P.copy("dve", ef.t[:, 48:80], pap[:, 48:80], r=[pb], w=[ef])
                    P.st("sp", scr["misc"][rows, :], ef.t[:, 0:80], r=[ef], key="ef%d" % (evf.i % 3))
    P.barrier()
    P.emit()
    C.close()


def phase_B(nc, P, ins, scr):
    C = Ctx(nc, P)
    C.init_psum()
    SC = 128.0 ** -0.5
    gcf = C.sb([128, 8, 128], F32)
    identb = C.sb([128, 128], BF16)
    onesb = C.sb([128, 128], BF16)
    tria = C.sb([128, 512], BF16)
    trib = C.sb([128, 512], BF16)
    expand = C.sb([64, NT, 128], BF16)
    wck = C.sb([128, 32, 128], BF16)
    wcv = C.sb([128, 32, 128], BF16)
    wst = Rot(C.sbs(2, [128, 8, 128], F32))
    posT = C.sb([128, 2, 32], F32)
    kg = C.sb([128, 3], F32)
    aggf = C.sb([128, 2, 64], F32)
    gates = C.sb([128, NT, 48], F32)
    selmul = C.sb([128, NT, 64], F32)
    seladd = C.sb([128, NT, 64], F32)
    ksT = C.sb([128, S], BF16)
    kwT = C.sb([128, S], BF16)
    vsa = C.sb([128, NT, 129], BF16)
    vwa = C.sb([128, NT, 129], BF16)
    kcr = C.sb([128, S], BF16)
    vcr = C.sb([128, S], BF16)
    kaug = C.sb([128, 32, 256], BF16)
    vaug = C.sb([128, 32, 256], BF16)
    kcmpT = C.sb([128, 256], BF16)
    vcmpa = C.sb([128, 2, 193], BF16)
    sqc = C.sb([128, 256], BF16)
    rtc = C.sb([128, 256], F32)
    qTs = Rot(C.sbs(2, [128, 512], BF16))
    cbs = Rot(C.sbs(2, [128, 2, 512], BF16))
    pTs = Rot(C.sbs(3, [128, 512], BF16))
    oaccs = Rot(C.sbs(2, [128, 4, 128], F32))
    oabs = Rot(C.sbs(2, [128, 4, 128], BF16))
    oTs = Rot(C.sbs(2, [128, 4, 128], BF16))
    smalls = Rot(C.sbs(4, [128, 16], F32))
    imps = Rot(C.sbs(2, [128, 64], F32))
    scs = Rot(C.sbs(2, [128, 64], F32))
    sc2s = Rot(C.sbs(2, [128, 64], F32))
    m8s = Rot(C.sbs(2, [128, 16], F32))
    brows = Rot(C.sbs(2, [128, 64], BF16))
    biasTs = Rot(C.sbs(2, [64, 512], BF16))
    evrot = Rot(["act", "dve"])
    dprot = Rot(["dve", "pool"])

    P.ld("sp", gcf.t[:], ins["gc"], w=[gcf], key="c0")
    P.copy("dve", identb.t[:], gcf.t[:, GC_ID, :], r=[gcf], w=[identb])
    P.copy("dve", onesb.t[:], gcf.t[:, GC_ONES, :], r=[gcf], w=[onesb])
    P.ld("sp", tria.t[:], ins["tria"], w=[tria], key="c1")
    P.ld("sp", trib.t[:], ins["trib"], w=[trib], key="c2")
    P.ld("sp", expand.t[:], ins["expand"], w=[expand], key="c3")
    P.ld("sp", posT.t[:], ins["posT"], w=[posT], key="c4")
    P.ld("sp", kg.t[:], ins["kgain"], w=[kg], key="c5")
    P.ld("sp", aggf.t[:], ins["agg"].rearrange("(j p) c -> p j c", p=128), w=[aggf], key="c6")
    for t8 in range(4):
        ts_ = slice(t8 * 8, (t8 + 1) * 8)
        P.ld("sp", gates.t[:, ts_, :], scr["misc"][t8 * 1024:(t8 + 1) * 1024, 0:48].rearrange("(t p) c -> p t c", p=128), w=[gates], key="c7")
        P.ld("sp", selmul.t[:, ts_, :], ins["selmul"][ts_].rearrange("t p c -> p t c"), w=[selmul], key="c8")
        P.ld("sp", seladd.t[:, ts_, :], ins["seladd"][ts_].rearrange("t p c -> p t c"), w=[seladd], key="c9")
    for kv, dstw in ((0, wck), (1, wcv)):
        for l4 in range(4):
            st = wst.next()
            P.ld("sp", st.t[:], ins["wcmp"][:, kv, l4 * 8:(l4 + 1) * 8, :], w=[st], key="wc%d" % (wst.i % 2))
            P.copy("pool", dstw.t[:, l4 * 8:(l4 + 1) * 8, :], st.t[:], r=[st], w=[dstw])
    P.memset("pool", kaug.t[:, :, 255:256], 0.0, w=[kaug])
    P.memset("pool", vaug.t[:, :, 255:256], 0.0, w=[vaug])
    P.memset("pool", vsa.t[:, :, 128:129], 1.0, w=[vsa])
    P.memset("pool", vwa.t[:, :, 128:129], 1.0, w=[vwa])
    P.memset("pool", vcmpa.t[:, :, 128:129], 1.0, w=[vcmpa])
    P.copy("dve", vcmpa.t[:, :, 129:193], aggf.t[:], r=[aggf], w=[vcmpa])

    oset = []
    for i in range(2):
        b0, a0 = C.take()
        b1, a1 = C.take()
        oset.append(((b0, a0), (b1, a1)))
    orot = Rot(oset)

    for g in range(4):
        P.ld("sp", kcr.t[:], scr["kc"][g], w=[kcr], key="g0")
        P.ld("sp", vcr.t[:], scr["vc"][g], w=[vcr], key="g1")
        P.ld("sp", ksT.t[:], scr["ks"][g], w=[ksT], key="g2")
        P.ld("sp", kwT.t[:], scr["kw"][g], w=[kwT], key="g3")
        for t8 in range(4):
            ts_ = slice(t8 * 8, (t8 + 1) * 8)
            P.ld("sp", vsa.t[:, ts_, 0:128], scr["vs"][g, t8 * 1024:(t8 + 1) * 1024, :].rearrange("(t p) d -> p t d", p=128), w=[vsa], key="g4")
            P.ld("sp", vwa.t[:, ts_, 0:128], scr["vw"][g, t8 * 1024:(t8 + 1) * 1024, :].rearrange("(t p) d -> p t d", p=128), w=[vwa], key="g5")
        for l in range(32):
            P.ts(dprot.next(), kaug.t[:, l, 0:255], kcr.t[:, l:l + 4065:16], posT.t[:, 0, l:l + 1], None, ALU.add, r=[kcr, posT], w=[kaug])
            P.ts(dprot.next(), vaug.t[:, l, 0:255], vcr.t[:, l:l + 4065:16], posT.t[:, 1, l:l + 1], None, ALU.add, r=[vcr, posT], w=[vaug])
        pk, pkap = C.ps()
        for l in range(32):
            P.mm(pkap[:, 0:256], wck.t[:, l, :], kaug.t[:, l, :], start=(l == 0), stop=(l == 31), r=[wck, kaug], w=[pk])
        P.act(sqc.t[:], pkap[:, 0:256], AF.Square, r=[pk], w=[sqc])
        p2, p2ap = C.ps()
        P.mm(p2ap[:, 0:256], onesb.t[:], sqc.t[:], r=[onesb, sqc], w=[p2])
        P.act(rtc.t[:], p2ap[:, 0:256], AF.Sqrt, r=[p2], w=[rtc], bias=EPS, scale=1.0 / 128)
        P.recip(rtc.t[:], rtc.t[:], r=[rtc], w=[rtc])
        P.stt("dve", kcmpT.t[:], pkap[:, 0:256], kg.t[:, 0:1], rtc.t[:], ALU.mult, ALU.mult, r=[pk, kg, rtc], w=[kcmpT])
        for j in range(2):
            pv, pvap = C.ps()
            for l in range(32):
                P.mm(pvap[:, 0:128], vaug.t[:, l, j * 128:(j + 1) * 128], wcv.t[:, l, :], start=(l == 0), stop=(l == 31), r=[wcv, vaug], w=[pv])
            P.copy("act", vcmpa.t[:, j, 0:128], pvap[:, 0:128], r=[pv], w=[vcmpa])

        for qt in range(NT):
            qT = qTs.next()
            cb = cbs.next()
            P.ld("sp", qT.t[:].rearrange("p (h t) -> p h t", h=4), scr["qn"][4 * g:4 * g + 4, :, qt * 128:(qt + 1) * 128].rearrange("h d t -> d h t"), w=[qT], key="q%d" % (qTs.i % 2))
            P.ld("sp", cb.t[:], ins["cmpbias"][qt].rearrange("j n c -> n j c"), w=[cb], key="cb%d" % (cbs.i % 2))
            oacc = oaccs.next()
            sm = smalls.next()
            (ob0, oa0), (ob1, oa1) = orot.next()
            obk = (ob0, ob1); oap = (oa0, oa1)
            njt = 1 if qt < 16 else 2
            for j in range(njt):
                sb_, sap = C.ps()
                P.mm(sap[:, 0:512], kcmpT.t[:, j * 128:(j + 1) * 128], qT.t[:], start=True, stop=False, r=[kcmpT, qT], w=[sb_])
                P.mm(sap[:, 0:512], identb.t[:], cb.t[:, j, :], start=False, stop=True, r=[identb, cb], w=[sb_])
                pt = pTs.next()
                P.act(pt.t[:], sap[:, 0:512], AF.Exp, r=[sb_], w=[pt], scale=SC)
                for r in range(4):
                    P.mm(oap[r // 2][:, (r % 2) * 193:(r % 2) * 193 + 193], pt.t[:, r * 128:(r + 1) * 128], vcmpa.t[:, j, :],
                         start=(j == 0 and r % 2 == 0), stop=(j == njt - 1), r=[pt, vcmpa], w=[obk[r // 2]])
            for r in range(4):
                zc = (r % 2) * 193 + 128
                P.ts("dve", sm.t[:, r:r + 1], oap[r // 2][:, zc:zc + 1], 1e-30, None, ALU.max, r=[obk[r // 2]], w=[sm])
            P.recip(sm.t[:, 4:8], sm.t[:, 0:4], r=[sm], w=[sm])
            P.tt("dve", sm.t[:, 8:12], sm.t[:, 4:8], gates.t[:, qt, 12 * g + 0:12 * g + 12:3], ALU.mult, r=[sm, gates], w=[sm])
            imp = imps.next()
            for r in range(4):
                ic = (r % 2) * 193 + 129
                if r == 0:
                    P.ts("dve", imp.t[:], oap[0][:, ic:ic + 64], sm.t[:, 4:5], None, ALU.mult, r=[obk[0], sm], w=[imp])
                else:
                    P.stt("dve", imp.t[:], oap[r // 2][:, ic:ic + 64], sm.t[:, 4 + r:5 + r], imp.t[:], ALU.mult, ALU.add, r=[obk[r // 2], sm, imp], w=[imp])
                oc = (r % 2) * 193
                P.ts("dve", oacc.t[:, r, :], oap[r // 2][:, oc:oc + 128], sm.t[:, 8 + r:9 + r], None, ALU.mult, r=[obk[r // 2], sm], w=[oacc])
            sc = scs.next(); sc2 = sc2s.next(); m8 = m8s.next(); brow = brows.next(); biasT = biasTs.next()
            P.tt("dve", sc.t[:], imp.t[:], selmul.t[:, qt, :], ALU.mult, r=[imp, selmul], w=[sc])
            P.tt("dve", sc.t[:], sc.t[:], seladd.t[:, qt, :], ALU.add, r=[sc, seladd], w=[sc])
            P.op("dve", lambda e, o=m8.t[:, 0:8], i=sc.t[:]: e.max(out=o, in_=i), [sc.b], [m8.b])
            P.op("dve", lambda e, o=sc2.t[:], a=m8.t[:, 0:8], v=sc.t[:]: e.match_replace(out=o, in_to_replace=a, in_values=v, imm_value=-1e30), [sc.b, m8.b], [sc2.b])
            P.op("dve", lambda e, o=m8.t[:, 8:16], i=sc2.t[:]: e.max(out=o, in_=i), [sc2.b], [m8.b])
            P.ts("dve", sc2.t[:], sc.t[:], m8.t[:, 15:16], None, ALU.is_ge, r=[sc, m8], w=[sc2])
            P.ts("dve", brow.t[:], sc2.t[:], -NEG, NEG, ALU.mult, ALU.add, r=[sc2], w=[brow])
            pbt, pbtap = C.ps(BF16)
            P.tr(pbtap[0:64, 0:128], brow.t[:], identb.t[:], r=[brow, identb], w=[pbt])
            for r in range(4):
                P.copy(evrot.next(), biasT.t[:, r * 128:(r + 1) * 128], pbtap[0:64, 0:128], r=[pbt], w=[biasT])
            for br in (1, 2):
                (ob0, oa0), (ob1, oa1) = orot.next()
                obk = (ob0, ob1); oap = (oa0, oa1)
                kT = ksT if br == 1 else kwT
                va = vsa if br == 1 else vwa
                kts = list(range(0, qt + 1)) if br == 1 else list(range(max(0, qt - 4), qt + 1))
                for idx, kt in enumerate(kts):
                    sb_, sap = C.ps()
                    extra = []
                    if br == 1:
                        extra.append((expand.t[:, kt, :], biasT.t[:], [expand, biasT]))
                    if kt == qt:
                        extra.append((identb.t[:], tria.t[:], [identb, tria]))
                    if br == 2 and kt == qt - 4:
                        extra.append((identb.t[:], trib.t[:], [identb, trib]))
                    P.mm(sap[:, 0:512], kT.t[:, kt * 128:(kt + 1) * 128], qT.t[:], start=True, stop=(len(extra) == 0), r=[kT, qT], w=[sb_])
                    for ei, (l_, r_, rd) in enumerate(extra):
                        P.mm(sap[:, 0:512], l_, r_, start=False, stop=(ei == len(extra) - 1), r=rd, w=[sb_])
                    pt = pTs.next()
                    P.act(pt.t[:], sap[:, 0:512], AF.Exp, r=[sb_], w=[pt], scale=SC)
                    for r in range(4):
                        P.mm(oap[r // 2][:, (r % 2) * 193:(r % 2) * 193 + 129], pt.t[:, r * 128:(r + 1) * 128], va.t[:, kt, :],
                             start=(idx == 0 and r % 2 == 0), stop=(idx == len(kts) - 1), r=[pt, va], w=[obk[r // 2]])
                sm2 = smalls.next()
                for r in range(4):
                    zc = (r % 2) * 193 + 128
                    P.copy("dve", sm2.t[:, r:r + 1], oap[r // 2][:, zc:zc + 1], r=[obk[r // 2]], w=[sm2])
                P.recip(sm2.t[:, 4:8], sm2.t[:, 0:4], r=[sm2], w=[sm2])
                P.tt("dve", sm2.t[:, 8:12], sm2.t[:, 4:8], gates.t[:, qt, 12 * g + br:12 * g + 12:3], ALU.mult, r=[sm2, gates], w=[sm2])
                for r in range(4):
                    oc = (r % 2) * 193
                    P.stt("dve", oacc.t[:, r, :], oap[r // 2][:, oc:oc + 128], sm2.t[:, 8 + r:9 + r], oacc.t[:, r, :], ALU.mult, ALU.add,
                          r=[obk[r // 2], sm2, oacc], w=[oacc])
            oab = oabs.next(); oT = oTs.next()
            P.copy("pool", oab.t[:], oacc.t[:], r=[oacc], w=[oab])
            po, poap = C.ps(BF16)
            for r in range(4):
                P.tr(poap[:, r * 128:(r + 1) * 128], oab.t[:, r, :], identb.t[:], r=[oab, identb], w=[po])
            P.copy("act", oT.t[:], poap[:, 0:512].rearrange("p (h t) -> p h t", h=4), r=[po], w=[oT])
            P.st("sp", scr["oaT"][4 * g:4 * g + 4, :, qt * 128:(qt + 1) * 128].rearrange("h d t -> d h t"), oT.t[:], r=[oT], key="oT%d" % (oTs.i % 2))
    P.barrier()
    P.emit()
    C.close()


CSUB = os.environ.get("MK_CSUB", "12")


def phase_C(nc, P, ins, scr):
    if "1" in CSUB:
        phase_C1(nc, P, ins, scr)
    if "2" in CSUB:
        phase_C2(nc, P, ins, scr)


def phase_C1(nc, P, ins, scr):
    C = Ctx(nc, P)
    C.init_psum()
    gcf = C.sb([128, 8, 128], F32)
    onesb = C.sb([128, 128], BF16)
    convw = C.sb([128, 48, 4], F32)
    ctmp = C.sb([128, S], F32)
    raws = Rot(C.sbs(2, [128, S], F32))
    ys = Rot(C.sbs(2, [128, S], F32))
    fbs = Rot(C.sbs(2, [128, S], BF16))
    tms = Rot(C.sbs(2, [128, NT, 128], F32))
    sqs = Rot(C.sbs(2, [128, 512], BF16))
    rts = Rot(C.sbs(2, [128, 512], F32))
    evrot = Rot(["act", "dve"])
    P.ld("sp", gcf.t[:], ins["gc"], w=[gcf], key="c0")
    P.ld("sp", convw.t[:], ins["convT"], w=[convw], key="c1")
    P.copy("dve", onesb.t[:], gcf.t[:, GC_ONES, :], r=[gcf], w=[onesb])
    identf = gcf.t[:, GC_ID, :]
    for ct in range(48):
        kind, h = ct // 16, ct % 16
        raw = raws.next(); y = ys.next()
        eng = "dve" if ct % 2 == 0 else "pool"
        P.ld("sp", raw.t[:], scr["gq"][ct], w=[raw], key="r%d" % (raws.i % 2))
        P.ts(eng, y.t[:], raw.t[:], convw.t[:, ct, 3:4], None, ALU.mult, r=[raw, convw], w=[y])
        for sh in (1, 2, 3):
            if eng == "dve":
                P.stt(eng, y.t[:, sh:S], raw.t[:, 0:S - sh], convw.t[:, ct, 3 - sh:4 - sh], y.t[:, sh:S], ALU.mult, ALU.add, r=[raw, convw, y], w=[y])
            else:
                P.ts(eng, ctmp.t[:, 0:S - sh], raw.t[:, 0:S - sh], convw.t[:, ct, 3 - sh:4 - sh], None, ALU.mult, r=[raw, convw], w=[ctmp])
                P.tt(eng, y.t[:, sh:S], y.t[:, sh:S], ctmp.t[:, 0:S - sh], ALU.add, r=[y, ctmp], w=[y])
        P.act(y.t[:], y.t[:], AF.Silu, r=[y], w=[y])
        if kind < 2:
            fb = fbs.next()
            for tg in range(8):
                sl = slice(tg * 512, (tg + 1) * 512)
                sq = sqs.next(); rt = rts.next()
                P.act(sq.t[:], y.t[:, sl], AF.Square, r=[y], w=[sq])
                pb, pap = C.ps()
                P.mm(pap[:, 0:512], onesb.t[:], sq.t[:], r=[onesb, sq], w=[pb])
                P.act(rt.t[:], pap[:, 0:512], AF.Sqrt, r=[pb], w=[rt], bias=EPS, scale=1.0)
                P.recip(rt.t[:], rt.t[:], r=[rt], w=[rt])
                if kind == 0:
                    P.stt("dve", fb.t[:, sl], y.t[:, sl], 128.0 ** -0.5, rt.t[:], ALU.mult, ALU.mult, r=[y, rt], w=[fb])
                else:
                    P.tt("dve", y.t[:, sl], y.t[:, sl], rt.t[:], ALU.mult, r=[y, rt], w=[y])
                    P.copy("pool", fb.t[:, sl], y.t[:, sl], r=[y], w=[fb])
            dst = scr["gqT"] if kind == 0 else scr["gkT"]
            for t8 in range(4):
                P.st("sp", dst[t8 * 8:(t8 + 1) * 8, :, h, :].rearrange("t d c -> d t c"), fb.t[:, t8 * 1024:(t8 + 1) * 1024].rearrange("p (t c) -> p t c", t=8), r=[fb], key="fb%d" % (fbs.i % 2))
        if kind >= 1:
            tm = tms.next()
            for t4 in range(NT // 4):
                pb, pap = C.ps()
                for tt in range(4):
                    tile = t4 * 4 + tt
                    P.tr(pap[:, tt * 128:(tt + 1) * 128], y.t[:, tile * 128:(tile + 1) * 128], identf, r=[y, gcf], w=[pb])
                P.copy(evrot.next(), tm.t[:, t4 * 4:(t4 + 1) * 4, :], pap[:, 0:512].rearrange("p (a b) -> p a b", a=4), r=[pb], w=[tm])
            dst = scr["gktm"] if kind == 1 else scr["gvtm"]
            for t8 in range(4):
                P.st("sp", dst[t8 * 8:(t8 + 1) * 8, :, h, :].rearrange("t c d -> c t d"), tm.t[:, t8 * 8:(t8 + 1) * 8, :], r=[tm], key="tm%d" % (tms.i % 2))
    P.barrier()
    P.emit()
    C.close()


def phase_C2(nc, P, ins, scr):
    C = Ctx(nc, P)
    C.init_psum()
    gcf = C.sb([128, 8, 128], F32)
    identb = C.sb([128, 128], BF16)
    misc = C.sb([128, NT, 32], F32)
    alog = C.sb([128, 16], F32)
    dtb = C.sb([128, 16], F32)
    og16 = C.sb([128, 16, 128], F32)
    gall = C.sb([128, NT, 16], F32)
    ball = C.sb([128, NT, 16], F32)
    nball = C.sb([128, NT, 16], F32)
    St = C.sb([128, 16, 128], F32)
    Sb = [C.sb([128, 128], BF16) for _ in range(16)]
    qTs = Rot(C.sbs(2, [128, 16, 128], BF16))
    kTs = Rot(C.sbs(2, [128, 16, 128], BF16))
    ktms = Rot(C.sbs(2, [128, 16, 128], F32))
    vtms = Rot(C.sbs(2, [128, 16, 128], F32))
    zts = Rot(C.sbs(2, [128, 16, 128], F32))
    scal = Rot(C.sbs(2, [128, 5, 16], F32))
    GUs = Rot(C.sbs(3, [128, 128], F32))
    decs = Rot(C.sbs(3, [128, 256], F32))
    Ms = Rot(C.sbs(4, [128, 2, 128], F32))
    Rs = Rot(C.sbs(4, [128, 128], F32))
    Rbs = Rot(C.sbs(3, [128, 128], BF16))
    vbs = Rot(C.sbs(3, [128, 128], BF16))
    kbgs = Rot(C.sbs(3, [128, 128], BF16))
    u16 = C.sb([128, 16, 128], F32)
    wT16 = [C.sb([128, 128], BF16) for _ in range(16)]
    qkd16 = [C.sb([128, 128], BF16) for _ in range(16)]
    kd16 = [C.sb([128, 128], BF16) for _ in range(16)]
    vn16 = [C.sb([128, 128], BF16) for _ in range(16)]
    o1s16 = [C.sb([128, 128], F32) for _ in range(16)]
    ots = Rot(C.sbs(2, [128, 16, 128], F32))
    obs = Rot(C.sbs(2, [128, 16, 128], BF16))
    obTs = Rot(C.sbs(2, [128, 16, 128], BF16))
    sss = Rot(C.sbs(2, [128, 48], F32))
    junk = C.sb([128, 128], F32)
    tmp16 = C.sb([128, NT, 16], F32)
    evrot = Rot(["act", "dve"])

    P.ld("sp", gcf.t[:], ins["gc"], w=[gcf], key="c0")
    P.copy("dve", identb.t[:], gcf.t[:, GC_ID, :], r=[gcf], w=[identb])
    for t8 in range(4):
        P.ld("sp", misc.t[:, t8 * 8:(t8 + 1) * 8, :], scr["misc"][t8 * 1024:(t8 + 1) * 1024, 48:80].rearrange("(t p) c -> p t c", p=128), w=[misc], key="c1")
    P.ld("sp", alog.t[:], ins["alog"][0:1, :].to_broadcast([128, 16]), w=[alog], key="c2")
    P.ld("sp", dtb.t[:], ins["dtb"][0:1, :].to_broadcast([128, 16]), w=[dtb], key="c3")
    for h in range(16):
        P.ld("sp", og16.t[:, h, :], ins["ogain"][0:1, :].to_broadcast([128, 128]), w=[og16], key="c4")
    P.act(alog.t[:], alog.t[:], AF.Exp, r=[alog], w=[alog])
    for t in range(NT):
        P.tt("dve", tmp16.t[:, t, :], misc.t[:, t, 0:16], dtb.t[:], ALU.add, r=[misc, dtb], w=[tmp16])
    P.act(tmp16.t[:], tmp16.t[:], AF.Exp, r=[tmp16], w=[tmp16])
    P.act(tmp16.t[:], tmp16.t[:], AF.Ln, r=[tmp16], w=[tmp16], bias=1.0, scale=1.0)
    for t in range(NT):
        P.stt("dve", gall.t[:, t, :], tmp16.t[:, t, :], -1.0, alog.t[:], ALU.mult, ALU.mult, r=[tmp16, alog], w=[gall])
    P.act(ball.t[:], misc.t[:, :, 16:32], AF.Sigmoid, r=[misc], w=[ball])
    P.ts("dve", nball.t[:], ball.t[:], -1.0, None, ALU.mult, r=[ball], w=[nball])
    P.memset("pool", St.t[:], 0.0, w=[St])
    for h in range(16):
        P.memset("pool", Sb[h].t[:], 0.0, w=[Sb[h]])

    TRIT = gcf.t[:, GC_TRIT, :]; USTR = gcf.t[:, GC_USTR, :]; BL = gcf.t[:, GC_BL, :]; BU = gcf.t[:, GC_BU, :]
    CI0 = gcf.t[:, GC_CI0, :]; CI1 = gcf.t[:, GC_CI1, :]; IDF = gcf.t[:, GC_ID, :]

    C2STAGE = int(os.environ.get("MK_C2STAGE", "3"))
    C2TILES = int(os.environ.get("MK_C2TILES", str(NT)))
    C2SUB = int(os.environ.get("MK_C2SUB", "9"))
    for tile in range(C2TILES):
        qT = qTs.next(); kT = kTs.next(); ktm = ktms.next(); vtm = vtms.next(); zt = zts.next()
        si = qTs.i % 2
        P.ld("sp", qT.t[:], scr["gqT"][tile], w=[qT], key="lq%d" % si)
        P.ld("sp", kT.t[:], scr["gkT"][tile], w=[kT], key="lk%d" % si)
        P.ld("sp", ktm.t[:], scr["gktm"][tile], w=[ktm], key="lkt%d" % si)
        P.ld("sp", vtm.t[:], scr["gvtm"][tile], w=[vtm], key="lvt%d" % si)
        P.ld("sp", zt.t[:].rearrange("p h d -> p (h d)"), scr["z"][tile * 128:(tile + 1) * 128, :], w=[zt], key="lz%d" % si)
        if C2SUB < 2:
            continue
        sc = scal.next()
        g_t = gall.t[:, tile, :]
        pb, pap = C.ps()
        P.mm(pap[:, 0:16], TRIT, g_t, r=[gcf, gall], w=[pb])
        P.mm(pap[:, 16:32], USTR, g_t, r=[gcf, gall], w=[pb])
        P.mm(pap[:, 32:48], CI0, g_t, r=[gcf, gall], w=[pb])
        P.mm(pap[:, 48:64], CI1, g_t, r=[gcf, gall], w=[pb])
        P.act(sc.t[:, 0:4, :].rearrange("p a b -> p (a b)"), pap[:, 0:64], AF.Exp, r=[pb], w=[sc])
        P.tt("dve", sc.t[:, 4, :], sc.t[:, 0, :], ball.t[:, tile, :], ALU.mult, r=[sc, ball], w=[sc])
        for h in range(16 if C2SUB >= 3 else 0):
            GU = GUs.next(); dec = decs.next()
            P.ts("pool", GU.t[:], USTR, gall.t[:, tile, h:h + 1], None, ALU.mult, r=[gcf, gall], w=[GU])
            pd, pdap = C.ps()
            P.mm(pdap[:, 0:128], TRIT, GU.t[:], start=True, stop=False, r=[gcf, GU], w=[pd])
            P.mm(pdap[:, 0:128], IDF, BL, start=False, stop=True, r=[gcf], w=[pd])
            P.mm(pdap[:, 128:256], GU.t[:], TRIT, start=True, stop=False, r=[gcf, GU], w=[pd])
            P.mm(pdap[:, 128:256], IDF, BU, start=False, stop=True, r=[gcf], w=[pd])
            P.act(dec.t[:], pdap[:, 0:256], AF.Exp, r=[pd], w=[dec])
            if C2SUB < 4:
                continue
            pk, pkap = C.ps()
            P.mm(pkap[:, 0:128], kT.t[:, h, :], kT.t[:, h, :], r=[kT], w=[pk])
            P.mm(pkap[:, 128:256], kT.t[:, h, :], qT.t[:, h, :], r=[kT, qT], w=[pk])
            MN = Ms.next()
            P.stt("dve", MN.t[:, 0, :], pkap[:, 0:128], nball.t[:, tile, h:h + 1], dec.t[:, 0:128], ALU.mult, ALU.mult, r=[pk, nball, dec], w=[MN])
            P.tt("dve", qkd16[h].t[:], pkap[:, 128:256], dec.t[:, 128:256], ALU.mult, r=[pk, dec], w=[qkd16[h]])
            if C2SUB < 5:
                continue
            pn, pnap = C.ps()
            P.tr(pnap[:, 0:128], MN.t[:, 0, :], IDF, r=[MN, gcf], w=[pn])
            P.copy("act", MN.t[:, 1, :], pnap[:, 0:128], r=[pn], w=[MN])
            R = Rs.next()
            P.tt("pool", R.t[:], MN.t[:, 1, :], IDF, ALU.add, r=[MN, gcf], w=[R])
            for k in range(1, 6 if C2SUB >= 6 else 1):
                MN2 = Ms.next()
                pm, pmap = C.ps()
                P.mm(pmap[:, 0:128], MN.t[:, 1, :], MN.t[:, 0, :], r=[MN], w=[pm])
                if k < 5:
                    P.mm(pmap[:, 128:256], MN.t[:, 0, :], MN.t[:, 1, :], r=[MN], w=[pm])
                    P.copy("act", MN2.t[:].rearrange("p a b -> p (a b)"), pmap[:, 0:256], r=[pm], w=[MN2])
                else:
                    P.copy("act", MN2.t[:, 0, :], pmap[:, 0:128], r=[pm], w=[MN2])
                pr, prap = C.ps()
                P.mm(prap[:, 0:128], MN2.t[:, 0, :], R.t[:], r=[MN2, R], w=[pr])
                R2 = Rs.next()
                P.tt("dve", R2.t[:], prap[:, 0:128], R.t[:], ALU.add, r=[pr, R], w=[R2])
                MN = MN2; R = R2
            if C2SUB < 7:
                continue
            Rb = Rbs.next(); vb = vbs.next(); kbg = kbgs.next()
            P.copy("pool", Rb.t[:], R.t[:], r=[R], w=[Rb])
            P.ts("dve", vb.t[:], vtm.t[:, h, :], ball.t[:, tile, h:h + 1], None, ALU.mult, r=[vtm, ball], w=[vb])
            P.ts("dve", kbg.t[:], ktm.t[:, h, :], sc.t[:, 4, h:h + 1], None, ALU.mult, r=[ktm, sc], w=[kbg])
            P.ts("dve", kd16[h].t[:], ktm.t[:, h, :], sc.t[:, 1, h:h + 1], None, ALU.mult, r=[ktm, sc], w=[kd16[h]])
            if C2SUB < 8:
                continue
            pu, puap = C.ps()
            P.mm(puap[:, 0:128], Rb.t[:], vb.t[:], r=[Rb, vb], w=[pu])
            P.mm(puap[:, 128:256], kbg.t[:], Rb.t[:], r=[Rb, kbg], w=[pu])
            if C2SUB < 9:
                continue
            P.copy("act", u16.t[:, h, :], puap[:, 0:128], r=[pu], w=[u16])
            P.copy("dve", wT16[h].t[:], puap[:, 128:256], r=[pu], w=[wT16[h]])
        for ci in range(2 if C2STAGE >= 2 else 0):
            rows = slice(ci * 64, (ci + 1) * 64)
            pend = None
            for i in range(17):
                if i < 16:
                    h = i
                    p1, p1ap = C.ps()
                    P.mm(p1ap[:, 0:128], wT16[h].t[:], Sb[h].t[:], r=[wT16[h], Sb[h]], w=[p1])
                    P.mm(p1ap[:, 128:256], qT.t[:, h, :], Sb[h].t[:], r=[qT, Sb[h]], w=[p1])
                    P.tt("dve", vn16[h].t[rows, :], u16.t[rows, h, :], p1ap[rows, 0:128], ALU.subtract, r=[u16, p1], w=[vn16[h]])
                    P.act(o1s16[h].t[rows, :], p1ap[rows, 128:256], AF.Identity, r=[p1, sc], w=[o1s16[h]], scale=sc.t[rows, 0, h:h + 1])
                if i >= 1:
                    h = i - 1
                    p3, p3ap = C.ps()
                    P.mm(p3ap[:, 0:128], kd16[h].t[rows, :], vn16[h].t[rows, :], r=[kd16[h], vn16[h]], w=[p3])
                    P.stt("dve", St.t[:, h, :], St.t[:, h, :], sc.t[:, 2 + ci, h:h + 1], p3ap[:, 0:128], ALU.mult, ALU.add, r=[St, sc, p3], w=[St])
                    P.copy("pool", Sb[h].t[:], St.t[:, h, :], r=[St], w=[Sb[h]])
        if C2STAGE < 3:
            continue
        ot = ots.next(); ob = obs.next(); obT = obTs.next(); ss = sss.next()
        for h in range(16):
            p2, p2ap = C.ps()
            P.mm(p2ap[:, 0:128], qkd16[h].t[:], vn16[h].t[:], r=[qkd16[h], vn16[h]], w=[p2])
            P.tt("dve", ot.t[:, h, :], p2ap[:, 0:128], o1s16[h].t[:], ALU.add, r=[p2, o1s16[h]], w=[ot])
            P.act(junk.t[:], ot.t[:, h, :], AF.Square, r=[ot], w=[junk, ss], accum=ss.t[:, h:h + 1])
        P.act(ss.t[:, 16:32], ss.t[:, 0:16], AF.Sqrt, r=[ss], w=[ss], bias=EPS, scale=1.0 / 128)
        P.recip(ss.t[:, 32:48], ss.t[:, 16:32], r=[ss], w=[ss])
        P.act(zt.t[:], zt.t[:], AF.Silu, r=[zt], w=[zt])
        P.tt("pool", zt.t[:], zt.t[:], og16.t[:], ALU.mult, r=[zt, og16], w=[zt])
        for h in range(16):
            P.stt("dve", ob.t[:, h, :], ot.t[:, h, :], ss.t[:, 32 + h:33 + h], zt.t[:, h, :], ALU.mult, ALU.mult, r=[ot, ss, zt], w=[ob])
        for h4 in range(4):
            pt, ptap = C.ps(BF16)
            for hh in range(4):
                h = h4 * 4 + hh
                P.tr(ptap[:, hh * 128:(hh + 1) * 128], ob.t[:, h, :], identb.t[:], r=[ob, identb], w=[pt])
            P.copy(evrot.next(), obT.t[:, h4 * 4:(h4 + 1) * 4, :], ptap[:, 0:512].rearrange("p (a b) -> p a b", a=4), r=[pt], w=[obT])
        for h8 in range(2):
            P.st("sp", scr["obT"][h8 * 8:(h8 + 1) * 8, :, tile * 128:(tile + 1) * 128].rearrange("h d t -> d h t"), obT.t[:, h8 * 8:(h8 + 1) * 8, :], r=[obT], key="oT%d" % (obTs.i % 2))
    P.barrier()
    P.emit()
    C.close()


def phase_D(nc, P, ins, scr):
    C = Ctx(nc, P)
    C.init_psum()
    oaT = C.sb([128, 16, 512], BF16)
    obT = C.sb([128, 16, 512], BF16)
    mixT = C.sb([128, 16, 512], BF16)
    ws = WStream(P, C, 4096, 4, 4, "wd")
    sga = Rot(C.sbs(2, [128, 512], F32))
    sgb = Rot(C.sbs(2, [128, 512], F32))
    t1s = Rot(C.sbs(2, [128, 512], F32))
    t2s = Rot(C.sbs(2, [128, 512], F32))
    xts = Rot(C.sbs(3, [128, 256], F32))
    x = ins["x"]
    for ch in range(8):
        tok = slice(ch * 512, (ch + 1) * 512)
        for h8 in range(2):
            hs_ = slice(h8 * 8, (h8 + 1) * 8)
            P.ld("sp", oaT.t[:, hs_, :], scr["oaT"][hs_, :, tok].rearrange("h p t -> p h t"), w=[oaT], key="oa")
            P.ld("sp", obT.t[:, hs_, :], scr["obT"][hs_, :, tok].rearrange("h p t -> p h t"), w=[obT], key="ob")
        for g in range(8):
            wa, wav = ws.get(ins["wbra"][g], 16, 256)
            wb, wbv = ws.get(ins["wbrb"][g], 16, 256)
            for sub in range(2):
                ct = g * 2 + sub
                pa, paap = C.ps()
                for k in range(16):
                    P.mm(paap[:, 0:512], wav[:, k, sub * 128:(sub + 1) * 128], oaT.t[:, k, :], start=(k == 0), stop=(k == 15), r=[wa, oaT], w=[pa])
                pb, pbap = C.ps()
                for k in range(16):
                    P.mm(pbap[:, 0:512], wbv[:, k, sub * 128:(sub + 1) * 128], obT.t[:, k, :], start=(k == 0), stop=(k == 15), r=[wb, obT], w=[pb])
                ga = sga.next(); gb = sgb.next(); t1 = t1s.next(); t2 = t2s.next()
                P.ld("sp", ga.t[:], scr["sgA"][ct, :, tok], w=[ga], key="ga%d" % (sga.i % 2))
                P.ld("sp", gb.t[:], scr["sgB"][ct, :, tok], w=[gb], key="gb%d" % (sgb.i % 2))
                P.tt("dve", t1.t[:], paap[:, 0:512], ga.t[:], ALU.mult, r=[pa, ga], w=[t1])
                P.tt("dve", t2.t[:], pbap[:, 0:512], gb.t[:], ALU.mult, r=[pb, gb], w=[t2])
                P.tt("pool", mixT.t[:, ct, :], t1.t[:], t2.t[:], ALU.add, r=[t1, t2], w=[mixT])
        for g in range(8):
            wo, wov = ws.get(ins["wout"][g], 16, 256)
            for tt in range(4):
                rows = slice(ch * 512 + tt * 128, ch * 512 + (tt + 1) * 128)
                po, poap = C.ps()
                for k in range(16):
                    P.mm(poap[:, 0:256], mixT.t[:, k, tt * 128:(tt + 1) * 128], wov[:, k, :], start=(k == 0), stop=(k == 15), r=[wo, mixT], w=[po])
                xt = xts.next()
                kx = "dx%d" % (xts.i % 3)
                P.ld("sp", xt.t[:], x[rows, g * 256:(g + 1) * 256], w=[xt], key=kx)
                P.tt("dve", xt.t[:], poap[:, 0:256], xt.t[:], ALU.add, r=[po, xt], w=[xt])
                P.st("sp", scr["x1"][rows, g * 256:(g + 1) * 256], xt.t[:], r=[xt], key=kx + "s")
    P.barrier()
    P.emit()
    C.close()


def phase_E(nc, P, ins, scr, out):
    C = Ctx(nc, P)
    C.init_psum()
    h2T = C.sb([128, 16, 512], BF16)
    actT = C.sb([128, 44, 512], BF16)
    xts = Rot(C.sbs(1, [128, D], F32))
    hbs = Rot(C.sbs(1, [128, D], BF16))
    gain = C.sb([128, D], F32)
    ssr = Rot(C.sbs(2, [128, 4], F32))
    ident = C.sb([128, 128], BF16)
    gcf = C.sb([128, 8, 128], F32)
    ws = WStream(P, C, 5632, 4, 4, "we")
    sgs = Rot(C.sbs(2, [128, 512], F32))
    pos = Rot(C.sbs(2, [128, 512], F32))
    xcs = Rot(C.sbs(2, [128, 4, 128], F32))
    evrot = Rot(["act", "dve"])
    P.ld("sp", gain.t[:], ins["ffn_norm"][0:1, :].to_broadcast([128, D]), w=[gain], key="c0")
    P.ld("sp", gcf.t[:], ins["gc"], w=[gcf], key="c1")
    P.copy("dve", ident.t[:], gcf.t[:, GC_ID, :], r=[gcf], w=[ident])
    identf = gcf.t[:, GC_ID, :]
    for ch in range(8):
        t0 = ch * 512
        norm_transpose(P, C, scr["x1"][t0:t0 + 512, :], gain, ident, h2T, 0, 4, xts, hbs, hbs.items[0], ssr, evrot)
        for g in range(22):
            wg, wgv = ws.get(ins["wgate"][g], 16, 256)
            wu, wuv = ws.get(ins["wup"][g], 16, 256)
            for sub in range(2):
                ft = g * 2 + sub
                pg, pgap = C.ps()
                for k in range(16):
                    P.mm(pgap[:, 0:512], wgv[:, k, sub * 128:(sub + 1) * 128], h2T.t[:, k, :], start=(k == 0), stop=(k == 15), r=[wg, h2T], w=[pg])
                pu, puap = C.ps()
                for k in range(16):
                    P.mm(puap[:, 0:512], wuv[:, k, sub * 128:(sub + 1) * 128], h2T.t[:, k, :], start=(k == 0), stop=(k == 15), r=[wu, h2T], w=[pu])
                sg = sgs.next()
                P.act(sg.t[:], pgap[:, 0:512], AF.Silu, r=[pg], w=[sg])
                P.tt("dve", actT.t[:, ft, :], sg.t[:], puap[:, 0:512], ALU.mult, r=[sg, pu], w=[actT])
        for g in range(16):
            wd, wdv = ws.get(ins["wdown"][g], 44, 128)
            pd, pdap = C.ps()
            for k in range(44):
                P.mm(pdap[:, 0:512], wdv[:, k, :], actT.t[:, k, :], start=(k == 0), stop=(k == 43), r=[wd, actT], w=[pd])
            po = pos.next()
            P.copy("act", po.t[:], pdap[:, 0:512], r=[pd], w=[po])
            pt, ptap = C.ps()
            for tt in range(4):
                P.tr(ptap[:, tt * 128:(tt + 1) * 128], po.t[:, tt * 128:(tt + 1) * 128], identf, r=[po, gcf], w=[pt])
            xc = xcs.next()
            kx = "ex%d" % (xcs.i % 2)
            P.ld("sp", xc.t[:], scr["x1"][t0:t0 + 512, g * 128:(g + 1) * 128].rearrange("(t p) c -> p t c", p=128), w=[xc], key=kx)
            P.tt("dve", xc.t[:], ptap[:, 0:512].rearrange("p (t c) -> p t c", t=4), xc.t[:], ALU.add, r=[pt, xc], w=[xc])
            P.st("sp", out[t0:t0 + 512, g * 128:(g + 1) * 128].rearrange("(t p) c -> p t c", p=128), xc.t[:], r=[xc], key=kx + "s")
    P.barrier()
    P.emit()
    C.close()


_CACHE = {}


def prepare_inputs(inputs):
    f = lambda a: np.ascontiguousarray(np.asarray(a, dtype=np.float32))
    w_in = f(inputs["w_in"])[0]
    wfm, wtm = split_w_in(w_in)
    c = make_consts()
    common = {
        "attn_norm": f(inputs["attn_norm"])[0:1],
        "ffn_norm": f(inputs["ffn_norm"])[0:1],
        "wfm": wfm, "wtm": wtm,
        "qgain": f(inputs["nsa_q_norm"])[0].reshape(128, 1),
        "kgain": np.ascontiguousarray(f(inputs["nsa_k_norm"])[0].T),
        "posT": np.ascontiguousarray(f(inputs["cmp_pos"])[0].transpose(2, 0, 1)),
        "wcmp": np.ascontiguousarray(f(inputs["w_cmp"])[0].transpose(2, 0, 1, 3)),
        "convT": np.ascontiguousarray(f(inputs["gdn_conv"])[0].reshape(4, 48, 128).transpose(2, 1, 0)),
        "alog": f(inputs["gdn_a_log"])[0:1],
        "dtb": f(inputs["gdn_dt_bias"])[0:1],
        "ogain": f(inputs["gdn_out_norm"])[0:1],
        "wbra": pretile(f(inputs["w_branch_a"])[0], 256),
        "wbrb": pretile(f(inputs["w_branch_b"])[0], 256),
        "wout": pretile(f(inputs["w_out"])[0], 256),
        "wgate": pretile(f(inputs["w_gate"])[0], 256),
        "wup": pretile(f(inputs["w_up"])[0], 256),
        "wdown": pretile(f(inputs["w_down"])[0], 128),
        "agg": c["agg"], "cmpbias": c["cmpbias"], "tria": c["tria"], "trib": c["trib"],
        "expand": c["expand"], "gc": c["gc"], "selmul": c["selmul"], "seladd": c["seladd"],
    }
    x = f(inputs["x"])
    in_maps = []
    for core in range(8):
        m = dict(common)
        m["x"] = np.ascontiguousarray(x[core // 2])
        in_maps.append(m)
    return in_maps


LAST_RESULTS = None


def kernel(**inputs):
    global LAST_RESULTS
    in_maps = prepare_inputs(inputs)
    if "nc" not in _CACHE:
        _CACHE["nc"] = build_program()
    nc = _CACHE["nc"]
    res = run_bass_kernel_spmd(nc, in_maps, core_ids=list(range(8)))
    LAST_RESULTS = res.results
    outs = [np.asarray(res.results[2 * b]["out"]).reshape(S, D) for b in range(4)]
    return np.stack(outs, axis=0).astype(np.float32)
```

```python
import os
import numpy as np
import ml_dtypes
from contextlib import ExitStack
import concourse.bass as bass
import concourse.mybir as mybir
from concourse.bass_utils import run_bass_kernel_spmd

F32 = mybir.dt.float32
BF16 = mybir.dt.bfloat16
AF = mybir.ActivationFunctionType
ALU = mybir.AluOpType

S = 4096
D = 2048
NT = S // 128
DFF = 5632
EPS = 1e-6
EPOCH = 30000
NEG = -4096.0

PHASES = os.environ.get("MK_PHASES", "ABCDE")
EXT_IN = set(filter(None, os.environ.get("MK_EXT_IN", "").split(",")))
EXT_OUT = set(filter(None, os.environ.get("MK_EXT_OUT", "").split(",")))


class Buf:
    __slots__ = ("w", "r", "excl")

    def __init__(self):
        self.w = None
        self.r = []
        self.excl = False


class Tl:
    def __init__(self, t):
        self.t = t
        self.b = Buf()


class Prog:
    ENGS = ("pe", "act", "dve", "pool", "sp")

    def __init__(self, nc, es):
        self.nc = nc
        self.es = es
        self.streams = {e: [] for e in self.ENGS}
        self.count = {e: 0 for e in self.ENGS}
        self.seen = {e: {} for e in self.ENGS}
        self.sems = {}
        self.dcount = {}
        self.dgen = {}
        self.ninstr = 0

    def sem(self, key):
        if key not in self.sems:
            self.sems[key] = self.es.enter_context(self.nc.semaphore("s%d" % len(self.sems)))
        return self.sems[key]

    def _deps(self, eng, reads, writes, ident=None):
        ident = ident or eng
        deps = {}

        def add(tok, raw):
            if tok is None:
                return
            key, val, teng = tok
            if teng == ident and not raw:
                return
            if deps.get(key, 0) < val:
                deps[key] = val

        for b in reads:
            add(b.w, True)
            if b.excl:
                for t in b.r:
                    add(t, False)
        for b in writes:
            add(b.w, False)
            for t in b.r:
                add(t, False)
        waits = []
        seen = self.seen[eng]
        for key, val in deps.items():
            if seen.get(key, 0) < val:
                seen[key] = val
                waits.append((key, val))
        return waits

    def _commit(self, tok, reads, writes):
        for b in reads:
            if len(b.r) > 24:
                best = {}
                for t in b.r:
                    if best.get(t[0], (0,))[0] < t[1]:
                        best[t[0]] = (t[1], t)
                b.r = [v[1] for v in best.values()]
            b.r.append(tok)
        for b in writes:
            b.w = tok
            b.r = []

    def op(self, eng, fn, reads=(), writes=()):
        reads = [x.b if isinstance(x, Tl) else x for x in reads]
        writes = [x.b if isinstance(x, Tl) else x for x in writes]
        waits = self._deps(eng, reads, writes)
        c = self.count[eng]
        key = (eng, c // EPOCH)
        val = c % EPOCH + 1
        self.count[eng] = c + 1
        self.sem(key)
        self.streams[eng].append((waits, fn, key, 1))
        tok = (key, val, eng)
        self._commit(tok, reads, writes)
        self.ninstr += 1
        return tok

    def dma(self, eng, fn, reads=(), writes=(), key=None):
        reads = [x.b if isinstance(x, Tl) else x for x in reads]
        writes = [x.b if isinstance(x, Tl) else x for x in writes]
        w2 = self._deps(eng, reads, writes, ident="dma")
        gen = self.dgen.get(key, 0)
        k = ("dma", key, gen)
        if self.dcount.get(k, 0) + 16 > EPOCH:
            gen += 1
            self.dgen[key] = gen
            k = ("dma", key, gen)
        self.dcount[k] = self.dcount.get(k, 0) + 16
        self.sem(k)
        self.streams[eng].append((w2, fn, k, 16))
        tok = (k, self.dcount[k], "dma")
        self._commit(tok, reads, writes)
        self.ninstr += 1
        return tok

    def barrier(self):
        toks = []
        for e in self.ENGS:
            c = self.count[e]
            if c > 0:
                toks.append(((e, (c - 1) // EPOCH), (c - 1) % EPOCH + 1))
        for k, v in self.dcount.items():
            toks.append((k, v))
        for e in self.ENGS:
            seen = self.seen[e]
            waits = []
            for k, v in toks:
                if seen.get(k, 0) < v:
                    seen[k] = v
                    waits.append((k, v))
            self.streams[e].append((waits, None, None, 0))

    def emit(self):
        nc = self.nc

        def run(engname):
            stream = self.streams[engname]

            def body(e):
                for waits, fn, key, inc in stream:
                    for k, v in waits:
                        e.wait_ge(self.sems[k], v)
                    if fn is not None:
                        fn(e).then_inc(self.sems[key], inc)
            return body

        with nc.Block() as block:
            block.tensor(run("pe"))
            block.scalar(run("act"))
            block.vector(run("dve"))
            block.gpsimd(run("pool"))
            block.sync(run("sp"))
        self.streams = {e: [] for e in self.ENGS}

    def mm(self, out, lhsT, rhs, start=True, stop=True, r=(), w=()):
        return self.op("pe", lambda e: e.matmul(out, lhsT=lhsT, rhs=rhs, start=start, stop=stop), r, w)

    def tr(self, out, in_, ident, r=(), w=()):
        return self.op("pe", lambda e: e.transpose(out=out, in_=in_, identity=ident), r, w)

    def act(self, out, in_, func, r=(), w=(), bias=0.0, scale=1.0, accum=None):
        if accum is None:
            return self.op("act", lambda e: e.activation(out=out, in_=in_, func=func, bias=bias, scale=scale), r, w)
        return self.op("act", lambda e: e.activation(out=out, in_=in_, func=func, bias=bias, scale=scale, accum_out=accum), r, w)

    def copy(self, eng, out, in_, r=(), w=()):
        if eng == "act":
            return self.op("act", lambda e: e.copy(out=out, in_=in_), r, w)
        return self.op(eng, lambda e: e.tensor_copy(out=out, in_=in_), r, w)

    def ts(self, eng, out, in0, s1, s2, op0, op1=None, r=(), w=()):
        if op1 is None:
            return self.op(eng, lambda e: e.tensor_scalar(out=out, in0=in0, scalar1=s1, scalar2=None, op0=op0), r, w)
        return self.op(eng, lambda e: e.tensor_scalar(out=out, in0=in0, scalar1=s1, scalar2=s2, op0=op0, op1=op1), r, w)

    def stt(self, eng, out, in0, scalar, in1, op0, op1, r=(), w=()):
        return self.op(eng, lambda e: e.scalar_tensor_tensor(out=out, in0=in0, scalar=scalar, in1=in1, op0=op0, op1=op1), r, w)

    def tt(self, eng, out, in0, in1, op, r=(), w=()):
        return self.op(eng, lambda e: e.tensor_tensor(out=out, in0=in0, in1=in1, op=op), r, w)

    def recip(self, out, in_, r=(), w=()):
        return self.op("dve", lambda e: e.reciprocal(out=out, in_=in_), r, w)

    def memset(self, eng, ap, val, w=()):
        return self.op(eng, lambda e: e.memset(ap, val), (), w)

    def ld(self, eng, out, in_, w=(), key=None, r=()):
        return self.dma(eng, lambda e: e.dma_start(out=out, in_=in_), r, w, key)

    def st(self, eng, out, in_, r=(), key=None):
        return self.dma(eng, lambda e: e.dma_start(out=out, in_=in_), r, (), key)


_UID = [0]


def _uid():
    _UID[0] += 1
    return _UID[0]


class Ctx:
    def __init__(self, nc, P):
        self.nc = nc
        self.P = P
        self.es = ExitStack()
        self.n = 0
        self.banks = []
        self.bi = 0

    def sb(self, shape, dt=F32):
        self.n += 1
        return Tl(self.es.enter_context(self.nc.sbuf_tensor("t%d" % _uid(), shape, dt)))

    def sbs(self, n, shape, dt=F32):
        return [self.sb(shape, dt) for _ in range(n)]

    def init_psum(self):
        for i in range(8):
            self.n += 1
            self.banks.append(Tl(self.es.enter_context(self.nc.psum_tensor("p%d" % _uid(), [128, 2048], mybir.dt.uint8))))
            self.banks[-1].b.excl = True

    def take(self, dt=F32):
        b = self.banks.pop()
        return b, b.t[:].bitcast(dt)

    def ps(self, dt=F32):
        b = self.banks[self.bi % len(self.banks)]
        self.bi += 1
        return b, b.t[:].bitcast(dt)

    def close(self):
        self.es.close()


class Rot:
    def __init__(self, items):
        self.items = items
        self.i = 0

    def next(self):
        x = self.items[self.i % len(self.items)]
        self.i += 1
        return x


def pretile(w, gc):
    K, N = w.shape
    assert K % 128 == 0 and N % gc == 0
    return np.ascontiguousarray(w.reshape(K // 128, 128, N // gc, gc).transpose(2, 1, 0, 3))


def make_consts():
    c = {}
    n_cmp = 255
    cmp_start = np.arange(n_cmp) * 16
    sel_start = np.arange(64) * 64
    overlap = np.minimum(cmp_start[:, None] + 32, sel_start[None, :] + 64) - np.maximum(cmp_start[:, None], sel_start[None, :])
    agg = np.zeros((256, 64), np.float32)
    agg[:255] = np.clip(overlap, 0, None) / 32.0
    c["agg"] = agg
    n = np.arange(256)
    t = np.arange(S)
    valid = (16 * n[:, None] + 31 <= t[None, :]) & (n[:, None] < 255)
    cb = np.where(valid, 0.0, NEG).astype(np.float32).reshape(2, 128, NT, 128)
    cb = np.broadcast_to(cb.transpose(2, 0, 1, 3)[:, :, :, None, :], (NT, 2, 128, 4, 128))
    c["cmpbias"] = np.ascontiguousarray(cb).reshape(NT, 2, 128, 512).astype(ml_dtypes.bfloat16)
    kl = np.arange(128)[:, None]
    ql = np.arange(128)[None, :]
    tria = np.where(kl > ql, NEG, 0.0).astype(np.float32)
    trib = np.where(kl <= ql, NEG, 0.0).astype(np.float32)
    c["tria"] = np.ascontiguousarray(np.broadcast_to(tria[:, None, :], (128, 4, 128))).reshape(128, 512).astype(ml_dtypes.bfloat16)
    c["trib"] = np.ascontiguousarray(np.broadcast_to(trib[:, None, :], (128, 4, 128))).reshape(128, 512).astype(ml_dtypes.bfloat16)
    ex = np.zeros((64, NT, 128), np.float32)
    for kt in range(NT):
        for k in range(128):
            ex[2 * kt + k // 64, kt, k] = 1.0
    c["expand"] = ex.astype(ml_dtypes.bfloat16)
    i = np.arange(128)
    same = (i[:, None] // 64) == (i[None, :] // 64)
    g = {}
    g["trit"] = (same & (i[:, None] <= i[None, :])).astype(np.float32)
    g["ustr"] = (same & (i[:, None] > i[None, :])).astype(np.float32)
    g["bl"] = np.where(same & (i[None, :] < i[:, None]), 0.0, -30000.0).astype(np.float32)
    g["bu"] = np.where(same & (i[:, None] <= i[None, :]), 0.0, -30000.0).astype(np.float32)
    g["ci0"] = np.broadcast_to((i[:, None] < 64), (128, 128)).astype(np.float32)
    g["ci1"] = np.broadcast_to((i[:, None] >= 64), (128, 128)).astype(np.float32)
    g["ident"] = np.eye(128, dtype=np.float32)
    g["ones"] = np.ones((128, 128), np.float32)
    selmul = np.zeros((S, 64), np.float32)
    seladd = np.zeros((S, 64), np.float32)
    tt_ = np.arange(S)
    cur = tt_ // 64
    jb = np.arange(64)[None, :]
    noncausal = jb > cur[:, None]
    forced0 = (jb == 0) & ~noncausal
    forced1 = (jb == cur[:, None] - 1)
    forced2 = (jb == cur[:, None])
    free = ~(noncausal | forced0 | forced1 | forced2)
    selmul[free] = 1.0
    seladd = np.where(noncausal, -1e4 - jb, 0.0).astype(np.float32)
    seladd = np.where(forced0, 1e4, seladd)
    seladd = np.where(forced1, 1e4 + 1, seladd)
    seladd = np.where(forced2, 1e4 + 2, seladd).astype(np.float32)
    c["selmul"] = np.ascontiguousarray(selmul.reshape(NT, 128, 64))
    c["seladd"] = np.ascontiguousarray(seladd.reshape(NT, 128, 64))
    c["gc"] = np.ascontiguousarray(np.stack([g[k] for k in ("trit", "ustr", "bl", "bu", "ci0", "ci1", "ident", "ones")], axis=1))
    return c


GC_TRIT, GC_USTR, GC_BL, GC_BU, GC_CI0, GC_CI1, GC_ID, GC_ONES = range(8)

FM_GROUPS = 56
TM_GROUPS = 13


def split_w_in(w):
    q = w[:, 0:2048]
    kv = w[:, 2048:5120]
    gate = w[:, 5120:5168]
    gq = w[:, 5168:11312]
    z = w[:, 11312:13360]
    a = w[:, 13360:13376]
    b = w[:, 13376:13392]
    mA = w[:, 13392:15440]
    mB = w[:, 15440:17488]
    kc, vc, ks, vs, kw, vw = [kv[:, i * 512:(i + 1) * 512] for i in range(6)]
    fm = np.concatenate([q, kc, vc, ks, kw, gq, mA, mB], axis=1)
    misc = np.zeros((D, 256), np.float32)
    misc[:, 0:48] = gate
    misc[:, 48:64] = a
    misc[:, 64:80] = b
    tm = np.concatenate([vs, vw, z, misc], axis=1)
    return pretile(fm, 256), pretile(tm, 256)


def build_program():
    nc = bass.Bass("TRN2", target_bir_lowering=False)
    ins = {}

    def inp(name, shape, dt=F32):
        ins[name] = nc.dram_tensor(name, list(shape), dt, kind="ExternalInput").ap()
        return ins[name]

    scr = {}

    def scratch(name, shape, dt=F32):
        kind = "ExternalInput" if name in EXT_IN else ("ExternalOutput" if name in EXT_OUT else "Internal")
        scr[name] = nc.dram_tensor(name, list(shape), dt, kind=kind).ap()
        return scr[name]

    inp("x", [S, D])
    inp("xh", [S // 2, D])
    inp("selw", [128, 2])
    inp("attn_norm", [1, D])
    inp("ffn_norm", [1, D])
    inp("wfm", [FM_GROUPS, 128, 16, 256])
    inp("wtm", [TM_GROUPS, 128, 16, 256])
    inp("qgain", [128, 1])
    inp("kgain", [128, 3])
    inp("posT", [128, 2, 32])
    inp("wcmp", [128, 2, 32, 128])
    inp("convT", [128, 48, 4])
    inp("alog", [1, 16])
    inp("dtb", [1, 16])
    inp("ogain", [1, 128])
    inp("wbra", [8, 128, 16, 256])
    inp("wbrb", [8, 128, 16, 256])
    inp("wout", [8, 128, 16, 256])
    inp("wgate", [22, 128, 16, 256])
    inp("wup", [22, 128, 16, 256])
    inp("wdown", [16, 128, 44, 128])
    inp("agg", [256, 64])
    inp("cmpbias", [NT, 2, 128, 512], BF16)
    inp("tria", [128, 512], BF16)
    inp("trib", [128, 512], BF16)
    inp("expand", [64, NT, 128], BF16)
    inp("gc", [128, 8, 128])
    inp("selmul", [NT, 128, 64])
    inp("seladd", [NT, 128, 64])
    out = nc.dram_tensor("out", [S // 2, D], F32, kind="ExternalOutput").ap()

    scratch("qn", [16, 128, S], BF16)
    scratch("kc", [4, 128, S], BF16)
    scratch("vc", [4, 128, S], BF16)
    scratch("ks", [4, 128, S], BF16)
    scratch("kw", [4, 128, S], BF16)
    scratch("gq", [48, 128, S], F32)
    scratch("sgA", [16, 128, S], F32)
    scratch("sgB", [16, 128, S], F32)
    scratch("vs", [4, S, 128], BF16)
    scratch("vw", [4, S, 128], BF16)
    scratch("z", [S, D], F32)
    scratch("misc", [S, 80], F32)
    scratch("oaT", [16, 128, S], BF16)
    scratch("obT", [16, 128, S], BF16)
    scratch("gqT", [NT, 128, 16, 128], BF16)
    scratch("gkT", [NT, 128, 16, 128], BF16)
    scratch("gktm", [NT, 128, 16, 128], F32)
    scratch("gvtm", [NT, 128, 16, 128], F32)
    scratch("x1", [S // 2, D], F32)

    with ExitStack() as es:
        P = Prog(nc, es)
        if "A" in PHASES:
            phase_A(nc, P, ins, scr)
        if "B" in PHASES:
            phase_B(nc, P, ins, scr)
        if "C" in PHASES:
            phase_C(nc, P, ins, scr)
        if "D" in PHASES:
            phase_D(nc, P, ins, scr)
        if "E" in PHASES:
            phase_E(nc, P, ins, scr, out)
        print("instructions:", P.ninstr, "sems:", len(P.sems))
    return nc


def norm_transpose(P, C, src_rows, gain_bc, ident, hT, col0, ntiles, xts, hbs, junk, ssr, evrot):
    for tt in range(ntiles):
        xt = xts.next()
        hb = hbs.next()
        ss = ssr.next()
        P.ld("sp", xt.t[:], src_rows[tt * 128:(tt + 1) * 128, :], w=[xt], key="xt%d" % (xts.i % 2))
        P.act(junk.t[:], xt.t[:], AF.Square, r=[xt], w=[junk, ss], accum=ss.t[:, 0:1])
        P.act(ss.t[:, 1:2], ss.t[:, 0:1], AF.Sqrt, r=[ss], w=[ss], bias=EPS, scale=1.0 / D)
        P.recip(ss.t[:, 2:3], ss.t[:, 1:2], r=[ss], w=[ss])
        P.stt("dve", hb.t[:], xt.t[:], ss.t[:, 2:3], gain_bc.t[:], ALU.mult, ALU.mult, r=[xt, ss, gain_bc], w=[hb])
        for k4 in range(4):
            pb, pap = C.ps(BF16)
            for kk in range(4):
                k = k4 * 4 + kk
                P.tr(pap[:, kk * 128:(kk + 1) * 128], hb.t[:, k * 128:(k + 1) * 128], ident.t[:], r=[hb, ident], w=[pb])
            eng = evrot.next()
            P.copy(eng, hT.t[:, k4 * 4:(k4 + 1) * 4, col0 + tt * 128: col0 + (tt + 1) * 128],
                   pap[:, 0:512].rearrange("p (a b) -> p a b", a=4), r=[pb], w=[hT])


class WStream:
    def __init__(self, P, C, bf_elems, n_stage=3, n_bf=3, tag="w"):
        self.P = P
        self.st = Rot(C.sbs(n_stage, [128, 2048], F32))
        self.bf = Rot(C.sbs(n_bf, [128, bf_elems], BF16))
        self.tag = tag
        self.n = 0

    def get(self, src, nk, gc):
        P = self.P
        b = self.bf.next()
        bv = b.t[:, 0:nk * gc].rearrange("p (k c) -> p k c", k=nk)
        kstep = max(1, min(2048 // gc, 8))
        k0 = 0
        while k0 < nk:
            k1 = min(nk, k0 + kstep)
            s = self.st.next()
            self.n += 1
            sv = s.t[:, 0:(k1 - k0) * gc].rearrange("p (k c) -> p k c", k=k1 - k0)
            P.ld("sp", sv, src[:, k0:k1, :], w=[s], key="%s%d" % (self.tag, self.n % len(self.st.items)))
            P.copy("pool", bv[:, k0:k1, :], sv, r=[s], w=[b])
            k0 = k1
        return b, bv


def phase_A(nc, P, ins, scr):
    C = Ctx(nc, P)
    C.init_psum()
    hT = C.sb([128, 16, 2048], BF16)
    xts = Rot(C.sbs(2, [128, D], F32))
    hbs = Rot(C.sbs(2, [128, D], BF16))
    junk = C.sb([128, D], BF16)
    gain = C.sb([128, D], F32)
    ssr = Rot(C.sbs(2, [128, 4], F32))
    ident = C.sb([128, 128], BF16)
    ones = C.sb([128, 128], BF16)
    gcf = C.sb([128, 8, 128], F32)
    qg = C.sb([128, 1], F32)
    kg = C.sb([128, 3], F32)
    ws = WStream(P, C, 4096, 3, 3, "wa")
    evf = Rot(C.sbs(3, [128, 512], F32))
    evb = Rot(C.sbs(3, [128, 512], BF16))
    sqs = Rot(C.sbs(2, [128, 512], BF16))
    rts = Rot(C.sbs(2, [128, 512], F32))
    evrot = Rot(["act", "dve"])

    P.ld("sp", gain.t[:], ins["attn_norm"][0:1, :].to_broadcast([128, D]), w=[gain], key="c0")
    P.ld("sp", gcf.t[:], ins["gc"], w=[gcf], key="c1")
    P.ld("sp", qg.t[:], ins["qgain"], w=[qg], key="c2")
    P.ld("sp", kg.t[:], ins["kgain"], w=[kg], key="c3")
    P.copy("dve", ident.t[:], gcf.t[:, GC_ID, :], r=[gcf], w=[ident])
    P.copy("dve", ones.t[:], gcf.t[:, GC_ONES, :], r=[gcf], w=[ones])

    x = ins["x"]
    for half in range(2):
        t0 = half * 2048
        norm_transpose(P, C, x[t0:t0 + 2048, :], gain, ident, hT, 0, 16, xts, hbs, junk, ssr, evrot)
        for g in range(FM_GROUPS):
            wb, wv = ws.get(ins["wfm"][g], 16, 256)
            for sub in range(2):
                ct = g * 2 + sub
                for tg in range(4):
                    pb, pap = C.ps()
                    for k in range(16):
                        P.mm(pap[:, 0:512], wv[:, k, sub * 128:(sub + 1) * 128], hT.t[:, k, tg * 512:(tg + 1) * 512],
                             start=(k == 0), stop=(k == 15), r=[wb, hT], w=[pb])
                    tok = slice(t0 + tg * 512, t0 + (tg + 1) * 512)
                    if ct < 16 or 24 <= ct < 32:
                        if ct < 16:
                            gcol = qg.t[:, 0:1]; gt = qg
                            dst = scr["qn"][ct, :, tok]
                        elif ct < 28:
                            gcol = kg.t[:, 1:2]; gt = kg
                            dst = scr["ks"][ct - 24, :, tok]
                        else:
                            gcol = kg.t[:, 2:3]; gt = kg
                            dst = scr["kw"][ct - 28, :, tok]
                        sq = sqs.next(); rt = rts.next(); eb = evb.next()
                        P.act(sq.t[:], pap[:, 0:512], AF.Square, r=[pb], w=[sq])
                        p2, p2ap = C.ps()
                        P.mm(p2ap[:, 0:512], ones.t[:], sq.t[:], r=[ones, sq], w=[p2])
                        P.act(rt.t[:], p2ap[:, 0:512], AF.Sqrt, r=[p2], w=[rt], bias=EPS, scale=1.0 / 128)
                        P.recip(rt.t[:], rt.t[:], r=[rt], w=[rt])
                        P.stt("dve", eb.t[:], pap[:, 0:512], gcol, rt.t[:], ALU.mult, ALU.mult, r=[pb, gt, rt], w=[eb])
                        P.st("sp", dst, eb.t[:], r=[eb], key="eb%d" % (evb.i % 3))
                    elif ct < 24:
                        eb = evb.next()
                        P.copy(evrot.next(), eb.t[:], pap[:, 0:512], r=[pb], w=[eb])
                        dst = scr["kc"][ct - 16, :, tok] if ct < 20 else scr["vc"][ct - 20, :, tok]
                        P.st("sp", dst, eb.t[:], r=[eb], key="eb%d" % (evb.i % 3))
                    elif ct < 80:
                        ef = evf.next()
                        P.copy(evrot.next(), ef.t[:], pap[:, 0:512], r=[pb], w=[ef])
                        P.st("sp", scr["gq"][ct - 32, :, tok], ef.t[:], r=[ef], key="ef%d" % (evf.i % 3))
                    else:
                        ef = evf.next()
                        P.act(ef.t[:], pap[:, 0:512], AF.Sigmoid, r=[pb], w=[ef])
                        dst = scr["sgA"][ct - 80, :, tok] if ct < 96 else scr["sgB"][ct - 96, :, tok]
                        P.st("sp", dst, ef.t[:], r=[ef], key="ef%d" % (evf.i % 3))
        for g in range(TM_GROUPS):
            wb, wv = ws.get(ins["wtm"][g], 16, 256)
            for tt in range(16):
                pb, pap = C.ps()
                for k in range(16):
                    P.mm(pap[:, 0:256], hT.t[:, k, tt * 128:(tt + 1) * 128], wv[:, k, :],
                         start=(k == 0), stop=(k == 15), r=[wb, hT], w=[pb])
                rows = slice(t0 + tt * 128, t0 + (tt + 1) * 128)
                if g < 4:
                    eb = evb.next()
                    P.copy(evrot.next(), eb.t[:, 0:256], pap[:, 0:256], r=[pb], w=[eb])
                    name = "vs" if g < 2 else "vw"
                    g0 = (g % 2) * 2
                    P.st("sp", scr[name][g0:g0 + 2, rows, :].rearrange("g t d -> t g d"),
                         eb.t[:, 0:256].rearrange("p (g d) -> p g d", g=2), r=[eb], key="eb%d" % (evb.i % 3))
                elif g < 12:
                    ef = evf.next()
                    P.copy(evrot.next(), ef.t[:, 0:256], pap[:, 0:256], r=[pb], w=[ef])
                    P.st("sp", scr["z"][rows, (g - 4) * 256:(g - 3) * 256], ef.t[:, 0:256], r=[ef], key="ef%d" % (evf.i % 3))
                else:
                    ef = evf.next()
                    P.act(ef.t[:, 0:48], pap[:, 0:48], AF.Sigmoid, r=[pb], w=[ef])
                    P.copy("dve", ef.t[:, 48:80], pap[:, 48:80], r=[pb], w=[ef])
                    P.st("sp", scr["misc"][rows, :], ef.t[:, 0:80], r=[ef], key="ef%d" % (evf.i % 3))
    P.barrier()
    P.emit()
    C.close()


def phase_B(nc, P, ins, scr):
    C = Ctx(nc, P)
    C.init_psum()
    SC = 128.0 ** -0.5
    gcf = C.sb([128, 8, 128], F32)
    identb = C.sb([128, 128], BF16)
    onesb = C.sb([128, 128], BF16)
    tria = C.sb([128, 512], BF16)
    trib = C.sb([128, 512], BF16)
    expand = C.sb([64, NT, 128], BF16)
    wck = C.sb([128, 32, 128], BF16)
    wcv = C.sb([128, 32, 128], BF16)
    wst = Rot(C.sbs(2, [128, 8, 128], F32))
    posT = C.sb([128, 2, 32], F32)
    kg = C.sb([128, 3], F32)
    aggf = C.sb([128, 2, 64], F32)
    gates = C.sb([128, NT, 48], F32)
    selmul = C.sb([128, NT, 64], F32)
    seladd = C.sb([128, NT, 64], F32)
    ksT = C.sb([128, S], BF16)
    kwT = C.sb([128, S], BF16)
    vsa = C.sb([128, NT, 129], BF16)
    vwa = C.sb([128, NT, 129], BF16)
    kcr = C.sb([128, S], BF16)
    vcr = C.sb([128, S], BF16)
    kaug = C.sb([128, 32, 256], BF16)
    vaug = C.sb([128, 32, 256], BF16)
    kcmpT = C.sb([128, 256], BF16)
    vcmpa = C.sb([128, 2, 193], BF16)
    sqc = C.sb([128, 256], BF16)
    rtc = C.sb([128, 256], F32)
    qTs = Rot(C.sbs(2, [128, 512], BF16))
    cbs = Rot(C.sbs(2, [128, 2, 512], BF16))
    pTs = Rot(C.sbs(3, [128, 512], BF16))
    oaccs = Rot(C.sbs(2, [128, 4, 128], F32))
    oabs = Rot(C.sbs(2, [128, 4, 128], BF16))
    oTs = Rot(C.sbs(2, [128, 4, 128], BF16))
    smalls = Rot(C.sbs(4, [128, 16], F32))
    imps = Rot(C.sbs(2, [128, 64], F32))
    scs = Rot(C.sbs(2, [128, 64], F32))
    sc2s = Rot(C.sbs(2, [128, 64], F32))
    m8s = Rot(C.sbs(2, [128, 16], F32))
    brows = Rot(C.sbs(2, [128, 64], BF16))
    biasTs = Rot(C.sbs(2, [64, 512], BF16))
    evrot = Rot(["act", "dve"])
    dprot = Rot(["dve", "pool"])

    P.ld("sp", gcf.t[:], ins["gc"], w=[gcf], key="c0")
    P.copy("dve", identb.t[:], gcf.t[:, GC_ID, :], r=[gcf], w=[identb])
    P.copy("dve", onesb.t[:], gcf.t[:, GC_ONES, :], r=[gcf], w=[onesb])
    P.ld("sp", tria.t[:], ins["tria"], w=[tria], key="c1")
    P.ld("sp", trib.t[:], ins["trib"], w=[trib], key="c2")
    P.ld("sp", expand.t[:], ins["expand"], w=[expand], key="c3")
    P.ld("sp", posT.t[:], ins["posT"], w=[posT], key="c4")
    P.ld("sp", kg.t[:], ins["kgain"], w=[kg], key="c5")
    P.ld("sp", aggf.t[:], ins["agg"].rearrange("(j p) c -> p j c", p=128), w=[aggf], key="c6")
    for t8 in range(4):
        ts_ = slice(t8 * 8, (t8 + 1) * 8)
        P.ld("sp", gates.t[:, ts_, :], scr["misc"][t8 * 1024:(t8 + 1) * 1024, 0:48].rearrange("(t p) c -> p t c", p=128), w=[gates], key="c7")
        P.ld("sp", selmul.t[:, ts_, :], ins["selmul"][ts_].rearrange("t p c -> p t c"), w=[selmul], key="c8")
        P.ld("sp", seladd.t[:, ts_, :], ins["seladd"][ts_].rearrange("t p c -> p t c"), w=[seladd], key="c9")
    for kv, dstw in ((0, wck), (1, wcv)):
        for l4 in range(4):
            st = wst.next()
            P.ld("sp", st.t[:], ins["wcmp"][:, kv, l4 * 8:(l4 + 1) * 8, :], w=[st], key="wc%d" % (wst.i % 2))
            P.copy("pool", dstw.t[:, l4 * 8:(l4 + 1) * 8, :], st.t[:], r=[st], w=[dstw])
    P.memset("pool", kaug.t[:, :, 255:256], 0.0, w=[kaug])
    P.memset("pool", vaug.t[:, :, 255:256], 0.0, w=[vaug])
    P.memset("pool", vsa.t[:, :, 128:129], 1.0, w=[vsa])
    P.memset("pool", vwa.t[:, :, 128:129], 1.0, w=[vwa])
    P.memset("pool", vcmpa.t[:, :, 128:129], 1.0, w=[vcmpa])
    P.copy("dve", vcmpa.t[:, :, 129:193], aggf.t[:], r=[aggf], w=[vcmpa])

    oset = []
    for i in range(2):
        b0, a0 = C.take()
        b1, a1 = C.take()
        oset.append(((b0, a0), (b1, a1)))
    orot = Rot(oset)

    for g in range(4):
        P.ld("sp", kcr.t[:], scr["kc"][g], w=[kcr], key="g0")
        P.ld("sp", vcr.t[:], scr["vc"][g], w=[vcr], key="g1")
        P.ld("sp", ksT.t[:], scr["ks"][g], w=[ksT], key="g2")
        P.ld("sp", kwT.t[:], scr["kw"][g], w=[kwT], key="g3")
        for t8 in range(4):
            ts_ = slice(t8 * 8, (t8 + 1) * 8)
            P.ld("sp", vsa.t[:, ts_, 0:128], scr["vs"][g, t8 * 1024:(t8 + 1) * 1024, :].rearrange("(t p) d -> p t d", p=128), w=[vsa], key="g4")
            P.ld("sp", vwa.t[:, ts_, 0:128], scr["vw"][g, t8 * 1024:(t8 + 1) * 1024, :].rearrange("(t p) d -> p t d", p=128), w=[vwa], key="g5")
        for l in range(32):
            P.ts(dprot.next(), kaug.t[:, l, 0:255], kcr.t[:, l:l + 4065:16], posT.t[:, 0, l:l + 1], None, ALU.add, r=[kcr, posT], w=[kaug])
            P.ts(dprot.next(), vaug.t[:, l, 0:255], vcr.t[:, l:l + 4065:16], posT.t[:, 1, l:l + 1], None, ALU.add, r=[vcr, posT], w=[vaug])
        pk, pkap = C.ps()
        for l in range(32):
            P.mm(pkap[:, 0:256], wck.t[:, l, :], kaug.t[:, l, :], start=(l == 0), stop=(l == 31), r=[wck, kaug], w=[pk])
        P.act(sqc.t[:], pkap[:, 0:256], AF.Square, r=[pk], w=[sqc])
        p2, p2ap = C.ps()
        P.mm(p2ap[:, 0:256], onesb.t[:], sqc.t[:], r=[onesb, sqc], w=[p2])
        P.act(rtc.t[:], p2ap[:, 0:256], AF.Sqrt, r=[p2], w=[rtc], bias=EPS, scale=1.0 / 128)
        P.recip(rtc.t[:], rtc.t[:], r=[rtc], w=[rtc])
        P.stt("dve", kcmpT.t[:], pkap[:, 0:256], kg.t[:, 0:1], rtc.t[:], ALU.mult, ALU.mult, r=[pk, kg, rtc], w=[kcmpT])
        for j in range(2):
            pv, pvap = C.ps()
            for l in range(32):
                P.mm(pvap[:, 0:128], vaug.t[:, l, j * 128:(j + 1) * 128], wcv.t[:, l, :], start=(l == 0), stop=(l == 31), r=[wcv, vaug], w=[pv])
            P.copy("act", vcmpa.t[:, j, 0:128], pvap[:, 0:128], r=[pv], w=[vcmpa])

        for qt in range(NT):
            qT = qTs.next()
            cb = cbs.next()
            P.ld("sp", qT.t[:].rearrange("p (h t) -> p h t", h=4), scr["qn"][4 * g:4 * g + 4, :, qt * 128:(qt + 1) * 128].rearrange("h d t -> d h t"), w=[qT], key="q%d" % (qTs.i % 2))
            P.ld("sp", cb.t[:], ins["cmpbias"][qt].rearrange("j n c -> n j c"), w=[cb], key="cb%d" % (cbs.i % 2))
            oacc = oaccs.next()
            sm = smalls.next()
            (ob0, oa0), (ob1, oa1) = orot.next()
            obk = (ob0, ob1); oap = (oa0, oa1)
            njt = 1 if qt < 16 else 2
            for j in range(njt):
                sb_, sap = C.ps()
                P.mm(sap[:, 0:512], kcmpT.t[:, j * 128:(j + 1) * 128], qT.t[:], start=True, stop=False, r=[kcmpT, qT], w=[sb_])
                P.mm(sap[:, 0:512], identb.t[:], cb.t[:, j, :], start=False, stop=True, r=[identb, cb], w=[sb_])
                pt = pTs.next()
                P.act(pt.t[:], sap[:, 0:512], AF.Exp, r=[sb_], w=[pt], scale=SC)
                for r in range(4):
                    P.mm(oap[r // 2][:, (r % 2) * 193:(r % 2) * 193 + 193], pt.t[:, r * 128:(r + 1) * 128], vcmpa.t[:, j, :],
                         start=(j == 0 and r % 2 == 0), stop=(j == njt - 1), r=[pt, vcmpa], w=[obk[r // 2]])
            for r in range(4):
                zc = (r % 2) * 193 + 128
                P.ts("dve", sm.t[:, r:r + 1], oap[r // 2][:, zc:zc + 1], 1e-30, None, ALU.max, r=[obk[r // 2]], w=[sm])
            P.recip(sm.t[:, 4:8], sm.t[:, 0:4], r=[sm], w=[sm])
            P.tt("dve", sm.t[:, 8:12], sm.t[:, 4:8], gates.t[:, qt, 12 * g + 0:12 * g + 12:3], ALU.mult, r=[sm, gates], w=[sm])
            imp = imps.next()
            for r in range(4):
                ic = (r % 2) * 193 + 129
                if r == 0:
                    P.ts("dve", imp.t[:], oap[0][:, ic:ic + 64], sm.t[:, 4:5], None, ALU.mult, r=[obk[0], sm], w=[imp])
                else:
                    P.stt("dve", imp.t[:], oap[r // 2][:, ic:ic + 64], sm.t[:, 4 + r:5 + r], imp.t[:], ALU.mult, ALU.add, r=[obk[r // 2], sm, imp], w=[imp])
                oc = (r % 2) * 193
                P.ts("dve", oacc.t[:, r, :], oap[r // 2][:, oc:oc + 128], sm.t[:, 8 + r:9 + r], None, ALU.mult, r=[obk[r // 2], sm], w=[oacc])
            sc = scs.next(); sc2 = sc2s.next(); m8 = m8s.next(); brow = brows.next(); biasT = biasTs.next()
            P.tt("dve", sc.t[:], imp.t[:], selmul.t[:, qt, :], ALU.mult, r=[imp, selmul], w=[sc])
            P.tt("dve", sc.t[:], sc.t[:], seladd.t[:, qt, :], ALU.add, r=[sc, seladd], w=[sc])
            P.op("dve", lambda e, o=m8.t[:, 0:8], i=sc.t[:]: e.max(out=o, in_=i), [sc.b], [m8.b])
            P.op("dve", lambda e, o=sc2.t[:], a=m8.t[:, 0:8], v=sc.t[:]: e.match_replace(out=o, in_to_replace=a, in_values=v, imm_value=-1e30), [sc.b, m8.b], [sc2.b])
            P.op("dve", lambda e, o=m8.t[:, 8:16], i=sc2.t[:]: e.max(out=o, in_=i), [sc2.b], [m8.b])
            P.ts("dve", sc2.t[:], sc.t[:], m8.t[:, 15:16], None, ALU.is_ge, r=[sc, m8], w=[sc2])
            P.ts("dve", brow.t[:], sc2.t[:], -NEG, NEG, ALU.mult, ALU.add, r=[sc2], w=[brow])
            pbt, pbtap = C.ps(BF16)
            P.tr(pbtap[0:64, 0:128], brow.t[:], identb.t[:], r=[brow, identb], w=[pbt])
            for r in range(4):
                P.copy(evrot.next(), biasT.t[:, r * 128:(r + 1) * 128], pbtap[0:64, 0:128], r=[pbt], w=[biasT])
            for br in (1, 2):
                (ob0, oa0), (ob1, oa1) = orot.next()
                obk = (ob0, ob1); oap = (oa0, oa1)
                kT = ksT if br == 1 else kwT
                va = vsa if br == 1 else vwa
                kts = list(range(0, qt + 1)) if br == 1 else list(range(max(0, qt - 4), qt + 1))
                for idx, kt in enumerate(kts):
                    sb_, sap = C.ps()
                    extra = []
                    if br == 1:
                        extra.append((expand.t[:, kt, :], biasT.t[:], [expand, biasT]))
                    if kt == qt:
                        extra.append((identb.t[:], tria.t[:], [identb, tria]))
                    if br == 2 and kt == qt - 4:
                        extra.append((identb.t[:], trib.t[:], [identb, trib]))
                    P.mm(sap[:, 0:512], kT.t[:, kt * 128:(kt + 1) * 128], qT.t[:], start=True, stop=(len(extra) == 0), r=[kT, qT], w=[sb_])
                    for ei, (l_, r_, rd) in enumerate(extra):
                        P.mm(sap[:, 0:512], l_, r_, start=False, stop=(ei == len(extra) - 1), r=rd, w=[sb_])
                    pt = pTs.next()
                    P.act(pt.t[:], sap[:, 0:512], AF.Exp, r=[sb_], w=[pt], scale=SC)
                    for r in range(4):
                        P.mm(oap[r // 2][:, (r % 2) * 193:(r % 2) * 193 + 129], pt.t[:, r * 128:(r + 1) * 128], va.t[:, kt, :],
                             start=(idx == 0 and r % 2 == 0), stop=(idx == len(kts) - 1), r=[pt, va], w=[obk[r // 2]])
                sm2 = smalls.next()
                for r in range(4):
                    zc = (r % 2) * 193 + 128
                    P.copy("dve", sm2.t[:, r:r + 1], oap[r // 2][:, zc:zc + 1], r=[obk[r // 2]], w=[sm2])
                P.recip(sm2.t[:, 4:8], sm2.t[:, 0:4], r=[sm2], w=[sm2])
                P.tt("dve", sm2.t[:, 8:12], sm2.t[:, 4:8], gates.t[:, qt, 12 * g + br:12 * g + 12:3], ALU.mult, r=[sm2, gates], w=[sm2])
                for r in range(4):
                    oc = (r % 2) * 193
                    P.stt("dve", oacc.t[:, r, :], oap[r // 2][:, oc:oc + 128], sm2.t[:, 8 + r:9 + r], oacc.t[:, r, :], ALU.mult, ALU.add,
                          r=[obk[r // 2], sm2, oacc], w=[oacc])
            oab = oabs.next(); oT = oTs.next()
            P.copy("pool", oab.t[:], oacc.t[:], r=[oacc], w=[oab])
            po, poap = C.ps(BF16)
            for r in range(4):
                P.tr(poap[:, r * 128:(r + 1) * 128], oab.t[:, r, :], identb.t[:], r=[oab, identb], w=[po])
            P.copy("act", oT.t[:], poap[:, 0:512].rearrange("p (h t) -> p h t", h=4), r=[po], w=[oT])
            P.st("sp", scr["oaT"][4 * g:4 * g + 4, :, qt * 128:(qt + 1) * 128].rearrange("h d t -> d h t"), oT.t[:], r=[oT], key="oT%d" % (oTs.i % 2))
    P.barrier()
    P.emit()
    C.close()


CSUB = os.environ.get("MK_CSUB", "12")


def phase_C(nc, P, ins, scr):
    if "1" in CSUB:
        phase_C1(nc, P, ins, scr)
    if "2" in CSUB:
        phase_C2(nc, P, ins, scr)


def phase_C1(nc, P, ins, scr):
    C = Ctx(nc, P)
    C.init_psum()
    gcf = C.sb([128, 8, 128], F32)
    onesb = C.sb([128, 128], BF16)
    convw = C.sb([128, 48, 4], F32)
    ctmp = C.sb([128, S], F32)
    raws = Rot(C.sbs(2, [128, S], F32))
    ys = Rot(C.sbs(2, [128, S], F32))
    fbs = Rot(C.sbs(2, [128, S], BF16))
    tms = Rot(C.sbs(2, [128, NT, 128], F32))
    sqs = Rot(C.sbs(2, [128, 512], BF16))
    rts = Rot(C.sbs(2, [128, 512], F32))
    evrot = Rot(["act", "dve"])
    P.ld("sp", gcf.t[:], ins["gc"], w=[gcf], key="c0")
    P.ld("sp", convw.t[:], ins["convT"], w=[convw], key="c1")
    P.copy("dve", onesb.t[:], gcf.t[:, GC_ONES, :], r=[gcf], w=[onesb])
    identf = gcf.t[:, GC_ID, :]
    for ct in range(48):
        kind, h = ct // 16, ct % 16
        raw = raws.next(); y = ys.next()
        eng = "dve" if ct % 2 == 0 else "pool"
        P.ld("sp", raw.t[:], scr["gq"][ct], w=[raw], key="r%d" % (raws.i % 2))
        P.ts(eng, y.t[:], raw.t[:], convw.t[:, ct, 3:4], None, ALU.mult, r=[raw, convw], w=[y])
        for sh in (1, 2, 3):
            if eng == "dve":
                P.stt(eng, y.t[:, sh:S], raw.t[:, 0:S - sh], convw.t[:, ct, 3 - sh:4 - sh], y.t[:, sh:S], ALU.mult, ALU.add, r=[raw, convw, y], w=[y])
            else:
                P.ts(eng, ctmp.t[:, 0:S - sh], raw.t[:, 0:S - sh], convw.t[:, ct, 3 - sh:4 - sh], None, ALU.mult, r=[raw, convw], w=[ctmp])
                P.tt(eng, y.t[:, sh:S], y.t[:, sh:S], ctmp.t[:, 0:S - sh], ALU.add, r=[y, ctmp], w=[y])
        P.act(y.t[:], y.t[:], AF.Silu, r=[y], w=[y])
        if kind < 2:
            fb = fbs.next()
            for tg in range(8):
                sl = slice(tg * 512, (tg + 1) * 512)
                sq = sqs.next(); rt = rts.next()
                P.act(sq.t[:], y.t[:, sl], AF.Square, r=[y], w=[sq])
                pb, pap = C.ps()
                P.mm(pap[:, 0:512], onesb.t[:], sq.t[:], r=[onesb, sq], w=[pb])
                P.act(rt.t[:], pap[:, 0:512], AF.Sqrt, r=[pb], w=[rt], bias=EPS, scale=1.0)
                P.recip(rt.t[:], rt.t[:], r=[rt], w=[rt])
                if kind == 0:
                    P.stt("dve", fb.t[:, sl], y.t[:, sl], 128.0 ** -0.5, rt.t[:], ALU.mult, ALU.mult, r=[y, rt], w=[fb])
                else:
                    P.tt("dve", y.t[:, sl], y.t[:, sl], rt.t[:], ALU.mult, r=[y, rt], w=[y])
                    P.copy("pool", fb.t[:, sl], y.t[:, sl], r=[y], w=[fb])
            dst = scr["gqT"] if kind == 0 else scr["gkT"]
            for t8 in range(4):
                P.st("sp", dst[t8 * 8:(t8 + 1) * 8, :, h, :].rearrange("t d c -> d t c"), fb.t[:, t8 * 1024:(t8 + 1) * 1024].rearrange("p (t c) -> p t c", t=8), r=[fb], key="fb%d" % (fbs.i % 2))
        if kind >= 1:
            tm = tms.next()
            for t4 in range(NT // 4):
                pb, pap = C.ps()
                for tt in range(4):
                    tile = t4 * 4 + tt
                    P.tr(pap[:, tt * 128:(tt + 1) * 128], y.t[:, tile * 128:(tile + 1) * 128], identf, r=[y, gcf], w=[pb])
                P.copy(evrot.next(), tm.t[:, t4 * 4:(t4 + 1) * 4, :], pap[:, 0:512].rearrange("p (a b) -> p a b", a=4), r=[pb], w=[tm])
            dst = scr["gktm"] if kind == 1 else scr["gvtm"]
            for t8 in range(4):
                P.st("sp", dst[t8 * 8:(t8 + 1) * 8, :, h, :].rearrange("t c d -> c t d"), tm.t[:, t8 * 8:(t8 + 1) * 8, :], r=[tm], key="tm%d" % (tms.i % 2))
    P.barrier()
    P.emit()
    C.close()


def phase_C2(nc, P, ins, scr):
    C = Ctx(nc, P)
    C.init_psum()
    gcf = C.sb([128, 8, 128], F32)
    identb = C.sb([128, 128], BF16)
    misc = C.sb([128, NT, 32], F32)
    alog = C.sb([128, 16], F32)
    dtb = C.sb([128, 16], F32)
    og16 = C.sb([128, 16, 128], F32)
    gall = C.sb([128, NT, 16], F32)
    ball = C.sb([128, NT, 16], F32)
    nball = C.sb([128, NT, 16], F32)
    St = C.sb([128, 16, 128], F32)
    Sb = [C.sb([128, 128], BF16) for _ in range(16)]
    qTs = Rot(C.sbs(2, [128, 16, 128], BF16))
    kTs = Rot(C.sbs(2, [128, 16, 128], BF16))
    ktms = Rot(C.sbs(2, [128, 16, 128], F32))
    vtms = Rot(C.sbs(2, [128, 16, 128], F32))
    zts = Rot(C.sbs(1, [128, 16, 128], F32))
    scal = Rot(C.sbs(2, [128, 5, 16], F32))
    GUs = Rot(C.sbs(8, [128, 128], F32))
    decs = Rot(C.sbs(8, [128, 256], F32))
    Ms = Rot(C.sbs(12, [128, 2, 128], F32))
    Rs = Rot(C.sbs(12, [128, 128], F32))
    Rbs = Rot(C.sbs(8, [128, 128], BF16))
    vbs = Rot(C.sbs(8, [128, 128], BF16))
    kbgs = Rot(C.sbs(8, [128, 128], BF16))
    u16 = C.sb([128, 16, 128], F32)
    wT16 = [C.sb([128, 128], BF16) for _ in range(16)]
    qkd16 = [C.sb([128, 128], BF16) for _ in range(16)]
    kd16 = [C.sb([128, 128], BF16) for _ in range(16)]
    vn16 = [C.sb([128, 128], BF16) for _ in range(16)]
    o1s16 = [C.sb([128, 128], F32) for _ in range(16)]
    ots = Rot(C.sbs(1, [128, 16, 128], F32))
    obs = Rot(C.sbs(1, [128, 16, 128], BF16))
    obTs = Rot(C.sbs(2, [128, 16, 128], BF16))
    sss = Rot(C.sbs(2, [128, 48], F32))
    junk = C.sb([128, 128], F32)
    tmp16 = C.sb([128, NT, 16], F32)
    evrot = Rot(["act", "dve"])

    P.ld("sp", gcf.t[:], ins["gc"], w=[gcf], key="c0")
    P.copy("dve", identb.t[:], gcf.t[:, GC_ID, :], r=[gcf], w=[identb])
    for t8 in range(4):
        P.ld("sp", misc.t[:, t8 * 8:(t8 + 1) * 8, :], scr["misc"][t8 * 1024:(t8 + 1) * 1024, 48:80].rearrange("(t p) c -> p t c", p=128), w=[misc], key="c1")
    P.ld("sp", alog.t[:], ins["alog"][0:1, :].to_broadcast([128, 16]), w=[alog], key="c2")
    P.ld("sp", dtb.t[:], ins["dtb"][0:1, :].to_broadcast([128, 16]), w=[dtb], key="c3")
    for h in range(16):
        P.ld("sp", og16.t[:, h, :], ins["ogain"][0:1, :].to_broadcast([128, 128]), w=[og16], key="c4")
    P.act(alog.t[:], alog.t[:], AF.Exp, r=[alog], w=[alog])
    for t in range(NT):
        P.tt("dve", tmp16.t[:, t, :], misc.t[:, t, 0:16], dtb.t[:], ALU.add, r=[misc, dtb], w=[tmp16])
    P.act(tmp16.t[:], tmp16.t[:], AF.Exp, r=[tmp16], w=[tmp16])
    P.act(tmp16.t[:], tmp16.t[:], AF.Ln, r=[tmp16], w=[tmp16], bias=1.0, scale=1.0)
    for t in range(NT):
        P.stt("dve", gall.t[:, t, :], tmp16.t[:, t, :], -1.0, alog.t[:], ALU.mult, ALU.mult, r=[tmp16, alog], w=[gall])
    P.act(ball.t[:], misc.t[:, :, 16:32], AF.Sigmoid, r=[misc], w=[ball])
    P.ts("dve", nball.t[:], ball.t[:], -1.0, None, ALU.mult, r=[ball], w=[nball])
    P.memset("pool", St.t[:], 0.0, w=[St])
    for h in range(16):
        P.memset("pool", Sb[h].t[:], 0.0, w=[Sb[h]])

    TRIT = gcf.t[:, GC_TRIT, :]; USTR = gcf.t[:, GC_USTR, :]; BL = gcf.t[:, GC_BL, :]; BU = gcf.t[:, GC_BU, :]
    CI0 = gcf.t[:, GC_CI0, :]; CI1 = gcf.t[:, GC_CI1, :]; IDF = gcf.t[:, GC_ID, :]

    C2STAGE = int(os.environ.get("MK_C2STAGE", "3"))
    C2TILES = int(os.environ.get("MK_C2TILES", str(NT)))
    C2SUB = int(os.environ.get("MK_C2SUB", "9"))
    for tile in range(C2TILES):
        qT = qTs.next(); kT = kTs.next(); ktm = ktms.next(); vtm = vtms.next(); zt = zts.next()
        si = qTs.i % 2
        P.ld("sp", qT.t[:], scr["gqT"][tile], w=[qT], key="lq%d" % si)
        P.ld("sp", kT.t[:], scr["gkT"][tile], w=[kT], key="lk%d" % si)
        P.ld("sp", ktm.t[:], scr["gktm"][tile], w=[ktm], key="lkt%d" % si)
        P.ld("sp", vtm.t[:], scr["gvtm"][tile], w=[vtm], key="lvt%d" % si)
        P.ld("sp", zt.t[:].rearrange("p h d -> p (h d)"), scr["z"][tile * 128:(tile + 1) * 128, :], w=[zt], key="lz%d" % si)
        if C2SUB < 2:
            continue
        sc = scal.next()
        g_t = gall.t[:, tile, :]
        pb, pap = C.ps()
        P.mm(pap[:, 0:16], TRIT, g_t, r=[gcf, gall], w=[pb])
        P.mm(pap[:, 16:32], USTR, g_t, r=[gcf, gall], w=[pb])
        P.mm(pap[:, 32:48], CI0, g_t, r=[gcf, gall], w=[pb])
        P.mm(pap[:, 48:64], CI1, g_t, r=[gcf, gall], w=[pb])
        P.act(sc.t[:, 0:4, :].rearrange("p a b -> p (a b)"), pap[:, 0:64], AF.Exp, r=[pb], w=[sc])
        P.tt("dve", sc.t[:, 4, :], sc.t[:, 0, :], ball.t[:, tile, :], ALU.mult, r=[sc, ball], w=[sc])
        for hg in range(4 if C2SUB >= 3 else 0):
            hs = list(range(hg * 4, hg * 4 + 4))
            GU = {}; dec = {}; MN = {}; R = {}
            for h in hs:
                GU[h] = GUs.next()
                P.ts("pool", GU[h].t[:], USTR, gall.t[:, tile, h:h + 1], None, ALU.mult, r=[gcf, gall], w=[GU[h]])
            pd = {}
            for h in hs:
                pd[h] = C.ps()
                b_, a_ = pd[h]
                P.mm(a_[:, 0:128], TRIT, GU[h].t[:], start=True, stop=False, r=[gcf, GU[h]], w=[b_])
                P.mm(a_[:, 0:128], IDF, BL, start=False, stop=True, r=[gcf], w=[b_])
                P.mm(a_[:, 128:256], GU[h].t[:], TRIT, start=True, stop=False, r=[gcf, GU[h]], w=[b_])
                P.mm(a_[:, 128:256], IDF, BU, start=False, stop=True, r=[gcf], w=[b_])
            for h in hs:
                dec[h] = decs.next()
                b_, a_ = pd[h]
                P.act(dec[h].t[:], a_[:, 0:256], AF.Exp, r=[b_], w=[dec[h]])
            pk = {}
            for h in hs:
                pk[h] = C.ps()
                b_, a_ = pk[h]
                P.mm(a_[:, 0:128], kT.t[:, h, :], kT.t[:, h, :], r=[kT], w=[b_])
                P.mm(a_[:, 128:256], kT.t[:, h, :], qT.t[:, h, :], r=[kT, qT], w=[b_])
            for h in hs:
                MN[h] = Ms.next()
                b_, a_ = pk[h]
                P.stt("dve", MN[h].t[:, 0, :], a_[:, 0:128], nball.t[:, tile, h:h + 1], dec[h].t[:, 0:128], ALU.mult, ALU.mult, r=[b_, nball, dec[h]], w=[MN[h]])
                P.tt("dve", qkd16[h].t[:], a_[:, 128:256], dec[h].t[:, 128:256], ALU.mult, r=[b_, dec[h]], w=[qkd16[h]])
            pn = {}
            for h in hs:
                pn[h] = C.ps()
                b_, a_ = pn[h]
                P.tr(a_[:, 0:128], MN[h].t[:, 0, :], IDF, r=[MN[h], gcf], w=[b_])
            for h in hs:
                b_, a_ = pn[h]
                P.copy("act", MN[h].t[:, 1, :], a_[:, 0:128], r=[b_], w=[MN[h]])
            for h in hs:
                R[h] = Rs.next()
                P.tt("pool", R[h].t[:], MN[h].t[:, 1, :], IDF, ALU.add, r=[MN[h], gcf], w=[R[h]])
            for k in range(1, 6):
                pm = {}; MN2 = {}; pr = {}
                for h in hs:
                    pm[h] = C.ps()
                    b_, a_ = pm[h]
                    P.mm(a_[:, 0:128], MN[h].t[:, 1, :], MN[h].t[:, 0, :], r=[MN[h]], w=[b_])
                    if k < 5:
                        P.mm(a_[:, 128:256], MN[h].t[:, 0, :], MN[h].t[:, 1, :], r=[MN[h]], w=[b_])
                for h in hs:
                    MN2[h] = Ms.next()
                    b_, a_ = pm[h]
                    if k < 5:
                        P.copy("act", MN2[h].t[:].rearrange("p a b -> p (a b)"), a_[:, 0:256], r=[b_], w=[MN2[h]])
                    else:
                        P.copy("act", MN2[h].t[:, 0, :], a_[:, 0:128], r=[b_], w=[MN2[h]])
                for h in hs:
                    pr[h] = C.ps()
                    b_, a_ = pr[h]
                    P.mm(a_[:, 0:128], MN2[h].t[:, 0, :], R[h].t[:], r=[MN2[h], R[h]], w=[b_])
                for h in hs:
                    R2 = Rs.next()
                    b_, a_ = pr[h]
                    P.tt("dve", R2.t[:], a_[:, 0:128], R[h].t[:], ALU.add, r=[b_, R[h]], w=[R2])
                    R[h] = R2
                    MN[h] = MN2[h]
            Rb = {}; vb = {}; kbg = {}
            for h in hs:
                Rb[h] = Rbs.next(); vb[h] = vbs.next(); kbg[h] = kbgs.next()
                P.copy("pool", Rb[h].t[:], R[h].t[:], r=[R[h]], w=[Rb[h]])
                P.ts("dve", vb[h].t[:], vtm.t[:, h, :], ball.t[:, tile, h:h + 1], None, ALU.mult, r=[vtm, ball], w=[vb[h]])
                P.ts("dve", kbg[h].t[:], ktm.t[:, h, :], sc.t[:, 4, h:h + 1], None, ALU.mult, r=[ktm, sc], w=[kbg[h]])
                P.ts("dve", kd16[h].t[:], ktm.t[:, h, :], sc.t[:, 1, h:h + 1], None, ALU.mult, r=[ktm, sc], w=[kd16[h]])
            pu = {}
            for h in hs:
                pu[h] = C.ps()
                b_, a_ = pu[h]
                P.mm(a_[:, 0:128], Rb[h].t[:], vb[h].t[:], r=[Rb[h], vb[h]], w=[b_])
                P.mm(a_[:, 128:256], kbg[h].t[:], Rb[h].t[:], r=[Rb[h], kbg[h]], w=[b_])
            for h in hs:
                b_, a_ = pu[h]
                P.copy("act", u16.t[:, h, :], a_[:, 0:128], r=[b_], w=[u16])
                P.copy("act", wT16[h].t[:], a_[:, 128:256], r=[b_], w=[wT16[h]])
        for ci in range(2 if C2STAGE >= 2 else 0):
            rows = slice(ci * 64, (ci + 1) * 64)
            pend = None
            for i in range(17):
                if i < 16:
                    h = i
                    p1, p1ap = C.ps()
                    P.mm(p1ap[:, 0:128], wT16[h].t[:], Sb[h].t[:], r=[wT16[h], Sb[h]], w=[p1])
                    P.mm(p1ap[:, 128:256], qT.t[:, h, :], Sb[h].t[:], r=[qT, Sb[h]], w=[p1])
                    P.tt("dve", vn16[h].t[rows, :], u16.t[rows, h, :], p1ap[rows, 0:128], ALU.subtract, r=[u16, p1], w=[vn16[h]])
                    P.ts("dve", o1s16[h].t[rows, :], p1ap[rows, 128:256], sc.t[rows, 0, h:h + 1], None, ALU.mult, r=[p1, sc], w=[o1s16[h]])
                if i >= 1:
                    h = i - 1
                    p3, p3ap = C.ps()
                    P.mm(p3ap[:, 0:128], kd16[h].t[rows, :], vn16[h].t[rows, :], r=[kd16[h], vn16[h]], w=[p3])
                    P.stt("dve", St.t[:, h, :], St.t[:, h, :], sc.t[:, 2 + ci, h:h + 1], p3ap[:, 0:128], ALU.mult, ALU.add, r=[St, sc, p3], w=[St])
                    P.copy("pool", Sb[h].t[:], St.t[:, h, :], r=[St], w=[Sb[h]])
        if C2STAGE < 3:
            continue
        ot = ots.next(); ob = obs.next(); obT = obTs.next(); ss = sss.next()
        for h in range(16):
            p2, p2ap = C.ps()
            P.mm(p2ap[:, 0:128], qkd16[h].t[:], vn16[h].t[:], r=[qkd16[h], vn16[h]], w=[p2])
            P.tt("dve", ot.t[:, h, :], p2ap[:, 0:128], o1s16[h].t[:], ALU.add, r=[p2, o1s16[h]], w=[ot])
            P.act(junk.t[:], ot.t[:, h, :], AF.Square, r=[ot], w=[junk, ss], accum=ss.t[:, h:h + 1])
        P.act(ss.t[:, 16:32], ss.t[:, 0:16], AF.Sqrt, r=[ss], w=[ss], bias=EPS, scale=1.0 / 128)
        P.recip(ss.t[:, 32:48], ss.t[:, 16:32], r=[ss], w=[ss])
        P.act(zt.t[:], zt.t[:], AF.Silu, r=[zt], w=[zt])
        P.tt("pool", zt.t[:], zt.t[:], og16.t[:], ALU.mult, r=[zt, og16], w=[zt])
        for h in range(16):
            P.stt("dve", ob.t[:, h, :], ot.t[:, h, :], ss.t[:, 32 + h:33 + h], zt.t[:, h, :], ALU.mult, ALU.mult, r=[ot, ss, zt], w=[ob])
        for h4 in range(4):
            pt, ptap = C.ps(BF16)
            for hh in range(4):
                h = h4 * 4 + hh
                P.tr(ptap[:, hh * 128:(hh + 1) * 128], ob.t[:, h, :], identb.t[:], r=[ob, identb], w=[pt])
            P.copy(evrot.next(), obT.t[:, h4 * 4:(h4 + 1) * 4, :], ptap[:, 0:512].rearrange("p (a b) -> p a b", a=4), r=[pt], w=[obT])
        for h8 in range(2):
            P.st("sp", scr["obT"][h8 * 8:(h8 + 1) * 8, :, tile * 128:(tile + 1) * 128].rearrange("h d t -> d h t"), obT.t[:, h8 * 8:(h8 + 1) * 8, :], r=[obT], key="oT%d" % (obTs.i % 2))
    P.barrier()
    P.emit()
    C.close()


def phase_D(nc, P, ins, scr):
    C = Ctx(nc, P)
    C.init_psum()
    oaT = C.sb([128, 16, 512], BF16)
    obT = C.sb([128, 16, 512], BF16)
    mixT = C.sb([128, 16, 512], BF16)
    ws = WStream(P, C, 4096, 4, 4, "wd")
    sga = Rot(C.sbs(2, [128, 512], F32))
    sgb = Rot(C.sbs(2, [128, 512], F32))
    t1s = Rot(C.sbs(2, [128, 512], F32))
    t2s = Rot(C.sbs(2, [128, 512], F32))
    xts = Rot(C.sbs(3, [128, 256], F32))
    stg = Rot(C.sbs(2, [128, 16, 512], BF16))
    sgl = Rot(C.sbs(2, [128, 512], F32))
    selw = C.sb([128, 2], F32)
    P.ld("sp", selw.t[:], ins["selw"], w=[selw], key="sw")
    W0 = selw.t[:, 0:1]; W1 = selw.t[:, 1:2]
    x = ins["xh"]
    for ch in range(4):
        tok = slice(ch * 512, (ch + 1) * 512)
        tokh = slice(2048 + ch * 512, 2048 + (ch + 1) * 512)
        for name, dst in (("oaT", oaT), ("obT", obT)):
            lo = stg.next(); hi = stg.next()
            for h8 in range(2):
                hs_ = slice(h8 * 8, (h8 + 1) * 8)
                P.ld("sp", lo.t[:, hs_, :], scr[name][hs_, :, tok].rearrange("h p t -> p h t"), w=[lo], key="slo")
                P.ld("sp", hi.t[:, hs_, :], scr[name][hs_, :, tokh].rearrange("h p t -> p h t"), w=[hi], key="shi")
            P.ts("dve", dst.t[:], lo.t[:], W0, None, ALU.mult, r=[lo, selw], w=[dst])
            P.stt("dve", dst.t[:], hi.t[:], W1, dst.t[:], ALU.mult, ALU.add, r=[hi, selw, dst], w=[dst])
        for g in range(8):
            wa, wav = ws.get(ins["wbra"][g], 16, 256)
            wb, wbv = ws.get(ins["wbrb"][g], 16, 256)
            for sub in range(2):
                ct = g * 2 + sub
                pa, paap = C.ps()
                for k in range(16):
                    P.mm(paap[:, 0:512], wav[:, k, sub * 128:(sub + 1) * 128], oaT.t[:, k, :], start=(k == 0), stop=(k == 15), r=[wa, oaT], w=[pa])
                pb, pbap = C.ps()
                for k in range(16):
                    P.mm(pbap[:, 0:512], wbv[:, k, sub * 128:(sub + 1) * 128], obT.t[:, k, :], start=(k == 0), stop=(k == 15), r=[wb, obT], w=[pb])
                ga = sga.next(); gb = sgb.next(); t1 = t1s.next(); t2 = t2s.next()
                for nm_, gt_, kk_ in (("sgA", ga, "ga%d" % (sga.i % 2)), ("sgB", gb, "gb%d" % (sgb.i % 2))):
                    gh_ = sgl.next()
                    P.ld("sp", gt_.t[:], scr[nm_][ct, :, tok], w=[gt_], key=kk_)
                    P.ld("sp", gh_.t[:], scr[nm_][ct, :, tokh], w=[gh_], key="gh%d" % (sgl.i % 2))
                    P.ts("dve", gt_.t[:], gt_.t[:], W0, None, ALU.mult, r=[gt_, selw], w=[gt_])
                    P.stt("dve", gt_.t[:], gh_.t[:], W1, gt_.t[:], ALU.mult, ALU.add, r=[gh_, selw, gt_], w=[gt_])
                P.tt("dve", t1.t[:], paap[:, 0:512], ga.t[:], ALU.mult, r=[pa, ga], w=[t1])
                P.tt("dve", t2.t[:], pbap[:, 0:512], gb.t[:], ALU.mult, r=[pb, gb], w=[t2])
                P.tt("pool", mixT.t[:, ct, :], t1.t[:], t2.t[:], ALU.add, r=[t1, t2], w=[mixT])
        for g in range(8):
            wo, wov = ws.get(ins["wout"][g], 16, 256)
            for tt in range(4):
                rows = slice(ch * 512 + tt * 128, ch * 512 + (tt + 1) * 128)
                po, poap = C.ps()
                for k in range(16):
                    P.mm(poap[:, 0:256], mixT.t[:, k, tt * 128:(tt + 1) * 128], wov[:, k, :], start=(k == 0), stop=(k == 15), r=[wo, mixT], w=[po])
                xt = xts.next()
                kx = "dx%d" % (xts.i % 3)
                P.ld("sp", xt.t[:], x[rows, g * 256:(g + 1) * 256], w=[xt], key=kx)
                P.tt("dve", xt.t[:], poap[:, 0:256], xt.t[:], ALU.add, r=[po, xt], w=[xt])
                P.st("sp", scr["x1"][rows, g * 256:(g + 1) * 256], xt.t[:], r=[xt], key=kx + "s")
    P.barrier()
    P.emit()
    C.close()


def phase_E(nc, P, ins, scr, out):
    C = Ctx(nc, P)
    C.init_psum()
    h2T = C.sb([128, 16, 512], BF16)
    actT = C.sb([128, 44, 512], BF16)
    xts = Rot(C.sbs(1, [128, D], F32))
    hbs = Rot(C.sbs(1, [128, D], BF16))
    gain = C.sb([128, D], F32)
    ssr = Rot(C.sbs(2, [128, 4], F32))
    ident = C.sb([128, 128], BF16)
    gcf = C.sb([128, 8, 128], F32)
    ws = WStream(P, C, 5632, 4, 4, "we")
    sgs = Rot(C.sbs(2, [128, 512], F32))
    pos = Rot(C.sbs(2, [128, 512], F32))
    xcs = Rot(C.sbs(2, [128, 4, 128], F32))
    evrot = Rot(["act", "dve"])
    P.ld("sp", gain.t[:], ins["ffn_norm"][0:1, :].to_broadcast([128, D]), w=[gain], key="c0")
    P.ld("sp", gcf.t[:], ins["gc"], w=[gcf], key="c1")
    P.copy("dve", ident.t[:], gcf.t[:, GC_ID, :], r=[gcf], w=[ident])
    identf = gcf.t[:, GC_ID, :]
    for ch in range(4):
        t0 = ch * 512
        norm_transpose(P, C, scr["x1"][t0:t0 + 512, :], gain, ident, h2T, 0, 4, xts, hbs, hbs.items[0], ssr, evrot)
        for g in range(22):
            wg, wgv = ws.get(ins["wgate"][g], 16, 256)
            wu, wuv = ws.get(ins["wup"][g], 16, 256)
            for sub in range(2):
                ft = g * 2 + sub
                pg, pgap = C.ps()
                for k in range(16):
                    P.mm(pgap[:, 0:512], wgv[:, k, sub * 128:(sub + 1) * 128], h2T.t[:, k, :], start=(k == 0), stop=(k == 15), r=[wg, h2T], w=[pg])
                pu, puap = C.ps()
                for k in range(16):
                    P.mm(puap[:, 0:512], wuv[:, k, sub * 128:(sub + 1) * 128], h2T.t[:, k, :], start=(k == 0), stop=(k == 15), r=[wu, h2T], w=[pu])
                sg = sgs.next()
                P.act(sg.t[:], pgap[:, 0:512], AF.Silu, r=[pg], w=[sg])
                P.tt("dve", actT.t[:, ft, :], sg.t[:], puap[:, 0:512], ALU.mult, r=[sg, pu], w=[actT])
        for g in range(16):
            wd, wdv = ws.get(ins["wdown"][g], 44, 128)
            pd, pdap = C.ps()
            for k in range(44):
                P.mm(pdap[:, 0:512], wdv[:, k, :], actT.t[:, k, :], start=(k == 0), stop=(k == 43), r=[wd, actT], w=[pd])
            po = pos.next()
            P.copy("act", po.t[:], pdap[:, 0:512], r=[pd], w=[po])
            pt, ptap = C.ps()
            for tt in range(4):
                P.tr(ptap[:, tt * 128:(tt + 1) * 128], po.t[:, tt * 128:(tt + 1) * 128], identf, r=[po, gcf], w=[pt])
            xc = xcs.next()
            kx = "ex%d" % (xcs.i % 2)
            P.ld("sp", xc.t[:], scr["x1"][t0:t0 + 512, g * 128:(g + 1) * 128].rearrange("(t p) c -> p t c", p=128), w=[xc], key=kx)
            P.tt("dve", xc.t[:], ptap[:, 0:512].rearrange("p (t c) -> p t c", t=4), xc.t[:], ALU.add, r=[pt, xc], w=[xc])
            P.st("sp", out[t0:t0 + 512, g * 128:(g + 1) * 128].rearrange("(t p) c -> p t c", p=128), xc.t[:], r=[xc], key=kx + "s")
    P.barrier()
    P.emit()
    C.close()


_CACHE = {}


def prepare_inputs(inputs):
    f = lambda a: np.ascontiguousarray(np.asarray(a, dtype=np.float32))
    w_in = f(inputs["w_in"])[0]
    wfm, wtm = split_w_in(w_in)
    c = make_consts()
    common = {
        "attn_norm": f(inputs["attn_norm"])[0:1],
        "ffn_norm": f(inputs["ffn_norm"])[0:1],
        "wfm": wfm, "wtm": wtm,
        "qgain": f(inputs["nsa_q_norm"])[0].reshape(128, 1),
        "kgain": np.ascontiguousarray(f(inputs["nsa_k_norm"])[0].T),
        "posT": np.ascontiguousarray(f(inputs["cmp_pos"])[0].transpose(2, 0, 1)),
        "wcmp": np.ascontiguousarray(f(inputs["w_cmp"])[0].transpose(2, 0, 1, 3)),
        "convT": np.ascontiguousarray(f(inputs["gdn_conv"])[0].reshape(4, 48, 128).transpose(2, 1, 0)),
        "alog": f(inputs["gdn_a_log"])[0:1],
        "dtb": f(inputs["gdn_dt_bias"])[0:1],
        "ogain": f(inputs["gdn_out_norm"])[0:1],
        "wbra": pretile(f(inputs["w_branch_a"])[0], 256),
        "wbrb": pretile(f(inputs["w_branch_b"])[0], 256),
        "wout": pretile(f(inputs["w_out"])[0], 256),
        "wgate": pretile(f(inputs["w_gate"])[0], 256),
        "wup": pretile(f(inputs["w_up"])[0], 256),
        "wdown": pretile(f(inputs["w_down"])[0], 128),
        "agg": c["agg"], "cmpbias": c["cmpbias"], "tria": c["tria"], "trib": c["trib"],
        "expand": c["expand"], "gc": c["gc"], "selmul": c["selmul"], "seladd": c["seladd"],
    }
    x = f(inputs["x"])
    in_maps = []
    for core in range(8):
        m = dict(common)
        m["x"] = np.ascontiguousarray(x[core // 2])
        hf = core % 2
        m["xh"] = np.ascontiguousarray(x[core // 2, hf * 2048:(hf + 1) * 2048])
        sw = np.zeros((128, 2), np.float32)
        sw[:, hf] = 1.0
        m["selw"] = sw
        in_maps.append(m)
    return in_maps


LAST_RESULTS = None


def kernel(**inputs):
    global LAST_RESULTS
    in_maps = prepare_inputs(inputs)
    if "nc" not in _CACHE:
        _CACHE["nc"] = build_program()
    nc = _CACHE["nc"]
    res = run_bass_kernel_spmd(nc, in_maps, core_ids=list(range(8)))
    LAST_RESULTS = res.results
    outs = [np.concatenate([np.asarray(res.results[2 * b]["out"]).reshape(S // 2, D),
                            np.asarray(res.results[2 * b + 1]["out"]).reshape(S // 2, D)], axis=0) for b in range(4)]
    return np.stack(outs, axis=0).astype(np.float32)
```

```python
import os
import numpy as np
import ml_dtypes
from contextlib import ExitStack
import concourse.bass as bass
import concourse.mybir as mybir
from concourse.bass_utils import run_bass_kernel_spmd

F32 = mybir.dt.float32
BF16 = mybir.dt.bfloat16
AF = mybir.ActivationFunctionType
ALU = mybir.AluOpType

S = 4096
D = 2048
NT = S // 128
DFF = 5632
EPS = 1e-6
EPOCH = 30000
NEG = -4096.0

PHASES = os.environ.get("MK_PHASES", "ABCDE")
EXT_IN = set(filter(None, os.environ.get("MK_EXT_IN", "").split(",")))
EXT_OUT = set(filter(None, os.environ.get("MK_EXT_OUT", "").split(",")))


class Buf:
    __slots__ = ("w", "r", "excl")

    def __init__(self):
        self.w = None
        self.r = []
        self.excl = False


class Tl:
    def __init__(self, t):
        self.t = t
        self.b = Buf()


class Prog:
    ENGS = ("pe", "act", "dve", "pool", "sp")

    def __init__(self, nc, es):
        self.nc = nc
        self.es = es
        self.streams = {e: [] for e in self.ENGS}
        self.count = {e: 0 for e in self.ENGS}
        self.seen = {e: {} for e in self.ENGS}
        self.sems = {}
        self.dcount = {}
        self.dgen = {}
        self.ninstr = 0

    def sem(self, key):
        if key not in self.sems:
            self.sems[key] = self.es.enter_context(self.nc.semaphore("s%d" % len(self.sems)))
        return self.sems[key]

    def _deps(self, eng, reads, writes, ident=None):
        ident = ident or eng
        deps = {}

        def add(tok, raw):
            if tok is None:
                return
            key, val, teng = tok
            if teng == ident and not raw:
                return
            if deps.get(key, 0) < val:
                deps[key] = val

        for b in reads:
            add(b.w, True)
            if b.excl:
                for t in b.r:
                    add(t, False)
        for b in writes:
            add(b.w, False)
            for t in b.r:
                add(t, False)
        waits = []
        seen = self.seen[eng]
        for key, val in deps.items():
            if seen.get(key, 0) < val:
                seen[key] = val
                waits.append((key, val))
        return waits

    def _commit(self, tok, reads, writes):
        for b in reads:
            if len(b.r) > 24:
                best = {}
                for t in b.r:
                    if best.get(t[0], (0,))[0] < t[1]:
                        best[t[0]] = (t[1], t)
                b.r = [v[1] for v in best.values()]
            b.r.append(tok)
        for b in writes:
            b.w = tok
            b.r = []

    def op(self, eng, fn, reads=(), writes=()):
        reads = [x.b if isinstance(x, Tl) else x for x in reads]
        writes = [x.b if isinstance(x, Tl) else x for x in writes]
        waits = self._deps(eng, reads, writes)
        c = self.count[eng]
        key = (eng, c // EPOCH)
        val = c % EPOCH + 1
        self.count[eng] = c + 1
        self.sem(key)
        self.streams[eng].append((waits, fn, key, 1))
        tok = (key, val, eng)
        self._commit(tok, reads, writes)
        self.ninstr += 1
        return tok

    def dma(self, eng, fn, reads=(), writes=(), key=None):
        reads = [x.b if isinstance(x, Tl) else x for x in reads]
        writes = [x.b if isinstance(x, Tl) else x for x in writes]
        w2 = self._deps(eng, reads, writes, ident="dma")
        gen = self.dgen.get(key, 0)
        k = ("dma", key, gen)
        if self.dcount.get(k, 0) + 16 > EPOCH:
            gen += 1
            self.dgen[key] = gen
            k = ("dma", key, gen)
        self.dcount[k] = self.dcount.get(k, 0) + 16
        self.sem(k)
        self.streams[eng].append((w2, fn, k, 16))
        tok = (k, self.dcount[k], "dma")
        self._commit(tok, reads, writes)
        self.ninstr += 1
        return tok

    def barrier(self):
        toks = []
        for e in self.ENGS:
            c = self.count[e]
            if c > 0:
                toks.append(((e, (c - 1) // EPOCH), (c - 1) % EPOCH + 1))
        for k, v in self.dcount.items():
            toks.append((k, v))
        for e in self.ENGS:
            seen = self.seen[e]
            waits = []
            for k, v in toks:
                if seen.get(k, 0) < v:
                    seen[k] = v
                    waits.append((k, v))
            self.streams[e].append((waits, None, None, 0))

    def emit(self):
        nc = self.nc

        def run(engname):
            stream = self.streams[engname]

            def body(e):
                for waits, fn, key, inc in stream:
                    for k, v in waits:
                        e.wait_ge(self.sems[k], v)
                    if fn is not None:
                        fn(e).then_inc(self.sems[key], inc)
            return body

        with nc.Block() as block:
            block.tensor(run("pe"))
            block.scalar(run("act"))
            block.vector(run("dve"))
            block.gpsimd(run("pool"))
            block.sync(run("sp"))
        self.streams = {e: [] for e in self.ENGS}

    def mm(self, out, lhsT, rhs, start=True, stop=True, r=(), w=(), sgc=False):
        if sgc:
            return self.op("pe", lambda e: e.matmul(out, lhsT=lhsT, rhs=rhs, start=start, stop=stop, skip_group_check=True), r, w)
        return self.op("pe", lambda e: e.matmul(out, lhsT=lhsT, rhs=rhs, start=start, stop=stop), r, w)

    def tr(self, out, in_, ident, r=(), w=()):
        return self.op("pe", lambda e: e.transpose(out=out, in_=in_, identity=ident), r, w)

    def act(self, out, in_, func, r=(), w=(), bias=0.0, scale=1.0, accum=None):
        if accum is None:
            return self.op("act", lambda e: e.activation(out=out, in_=in_, func=func, bias=bias, scale=scale), r, w)
        return self.op("act", lambda e: e.activation(out=out, in_=in_, func=func, bias=bias, scale=scale, accum_out=accum), r, w)

    def copy(self, eng, out, in_, r=(), w=()):
        if eng == "act":
            return self.op("act", lambda e: e.copy(out=out, in_=in_), r, w)
        return self.op(eng, lambda e: e.tensor_copy(out=out, in_=in_), r, w)

    def ts(self, eng, out, in0, s1, s2, op0, op1=None, r=(), w=()):
        if op1 is None:
            return self.op(eng, lambda e: e.tensor_scalar(out=out, in0=in0, scalar1=s1, scalar2=None, op0=op0), r, w)
        return self.op(eng, lambda e: e.tensor_scalar(out=out, in0=in0, scalar1=s1, scalar2=s2, op0=op0, op1=op1), r, w)

    def stt(self, eng, out, in0, scalar, in1, op0, op1, r=(), w=()):
        return self.op(eng, lambda e: e.scalar_tensor_tensor(out=out, in0=in0, scalar=scalar, in1=in1, op0=op0, op1=op1), r, w)

    def tt(self, eng, out, in0, in1, op, r=(), w=()):
        return self.op(eng, lambda e: e.tensor_tensor(out=out, in0=in0, in1=in1, op=op), r, w)

    def recip(self, out, in_, r=(), w=()):
        return self.op("dve", lambda e: e.reciprocal(out=out, in_=in_), r, w)

    def memset(self, eng, ap, val, w=()):
        return self.op(eng, lambda e: e.memset(ap, val), (), w)

    def ld(self, eng, out, in_, w=(), key=None, r=()):
        return self.dma(eng, lambda e: e.dma_start(out=out, in_=in_), r, w, key)

    def st(self, eng, out, in_, r=(), key=None):
        return self.dma(eng, lambda e: e.dma_start(out=out, in_=in_), r, (), key)


_UID = [0]


def _uid():
    _UID[0] += 1
    return _UID[0]


class Ctx:
    def __init__(self, nc, P):
        self.nc = nc
        self.P = P
        self.es = ExitStack()
        self.n = 0
        self.banks = []
        self.bi = 0

    def sb(self, shape, dt=F32):
        self.n += 1
        return Tl(self.es.enter_context(self.nc.sbuf_tensor("t%d" % _uid(), shape, dt)))

    def sbs(self, n, shape, dt=F32):
        return [self.sb(shape, dt) for _ in range(n)]

    def init_psum(self):
        for i in range(8):
            self.n += 1
            self.banks.append(Tl(self.es.enter_context(self.nc.psum_tensor("p%d" % _uid(), [128, 2048], mybir.dt.uint8))))
            self.banks[-1].b.excl = True

    def take(self, dt=F32):
        b = self.banks.pop()
        return b, b.t[:].bitcast(dt)

    def ps(self, dt=F32):
        b = self.banks[self.bi % len(self.banks)]
        self.bi += 1
        return b, b.t[:].bitcast(dt)

    def close(self):
        self.es.close()


class Rot:
    def __init__(self, items):
        self.items = items
        self.i = 0

    def next(self):
        x = self.items[self.i % len(self.items)]
        self.i += 1
        return x


def pretile(w, gc):
    K, N = w.shape
    assert K % 128 == 0 and N % gc == 0
    return np.ascontiguousarray(w.reshape(K // 128, 128, N // gc, gc).transpose(2, 1, 0, 3))


def make_consts():
    c = {}
    n_cmp = 255
    cmp_start = np.arange(n_cmp) * 16
    sel_start = np.arange(64) * 64
    overlap = np.minimum(cmp_start[:, None] + 32, sel_start[None, :] + 64) - np.maximum(cmp_start[:, None], sel_start[None, :])
    agg = np.zeros((256, 64), np.float32)
    agg[:255] = np.clip(overlap, 0, None) / 32.0
    c["agg"] = agg
    n = np.arange(256)
    t = np.arange(S)
    valid = (16 * n[:, None] + 31 <= t[None, :]) & (n[:, None] < 255)
    cb = np.where(valid, 0.0, NEG).astype(np.float32).reshape(2, 128, NT, 128)
    cb = np.broadcast_to(cb.transpose(2, 0, 1, 3)[:, :, :, None, :], (NT, 2, 128, 4, 128))
    c["cmpbias"] = np.ascontiguousarray(cb).reshape(NT, 2, 128, 512).astype(ml_dtypes.bfloat16)
    kl = np.arange(128)[:, None]
    ql = np.arange(128)[None, :]
    tria = np.where(kl > ql, NEG, 0.0).astype(np.float32)
    trib = np.where(kl <= ql, NEG, 0.0).astype(np.float32)
    c["tria"] = np.ascontiguousarray(np.broadcast_to(tria[:, None, :], (128, 4, 128))).reshape(128, 512).astype(ml_dtypes.bfloat16)
    c["trib"] = np.ascontiguousarray(np.broadcast_to(trib[:, None, :], (128, 4, 128))).reshape(128, 512).astype(ml_dtypes.bfloat16)
    ex = np.zeros((64, NT, 128), np.float32)
    for kt in range(NT):
        for k in range(128):
            ex[2 * kt + k // 64, kt, k] = 1.0
    c["expand"] = ex.astype(ml_dtypes.bfloat16)
    i = np.arange(128)
    same = (i[:, None] // 64) == (i[None, :] // 64)
    g = {}
    g["trit"] = (same & (i[:, None] <= i[None, :])).astype(np.float32)
    g["ustr"] = (same & (i[:, None] > i[None, :])).astype(np.float32)
    g["bl"] = np.where(same & (i[None, :] < i[:, None]), 0.0, -30000.0).astype(np.float32)
    g["bu"] = np.where(same & (i[:, None] <= i[None, :]), 0.0, -30000.0).astype(np.float32)
    g["ci0"] = np.broadcast_to((i[:, None] < 64), (128, 128)).astype(np.float32)
    g["ci1"] = np.broadcast_to((i[:, None] >= 64), (128, 128)).astype(np.float32)
    g["ident"] = np.eye(128, dtype=np.float32)
    g["ones"] = np.ones((128, 128), np.float32)
    selmul = np.zeros((S, 64), np.float32)
    seladd = np.zeros((S, 64), np.float32)
    tt_ = np.arange(S)
    cur = tt_ // 64
    jb = np.arange(64)[None, :]
    noncausal = jb > cur[:, None]
    forced0 = (jb == 0) & ~noncausal
    forced1 = (jb == cur[:, None] - 1)
    forced2 = (jb == cur[:, None])
    free = ~(noncausal | forced0 | forced1 | forced2)
    selmul[free] = 1.0
    seladd = np.where(noncausal, -1e4 - jb, 0.0).astype(np.float32)
    seladd = np.where(forced0, 1e4, seladd)
    seladd = np.where(forced1, 1e4 + 1, seladd)
    seladd = np.where(forced2, 1e4 + 2, seladd).astype(np.float32)
    c["selmul"] = np.ascontiguousarray(selmul.reshape(NT, 128, 64))
    c["seladd"] = np.ascontiguousarray(seladd.reshape(NT, 128, 64))
    c["gc"] = np.ascontiguousarray(np.stack([g[k] for k in ("trit", "ustr", "bl", "bu", "ci0", "ci1", "ident", "ones")], axis=1))
    return c


GC_TRIT, GC_USTR, GC_BL, GC_BU, GC_CI0, GC_CI1, GC_ID, GC_ONES = range(8)

FM_GROUPS = 56
TM_GROUPS = 13


def split_w_in(w):
    q = w[:, 0:2048]
    kv = w[:, 2048:5120]
    gate = w[:, 5120:5168]
    gq = w[:, 5168:11312]
    z = w[:, 11312:13360]
    a = w[:, 13360:13376]
    b = w[:, 13376:13392]
    mA = w[:, 13392:15440]
    mB = w[:, 15440:17488]
    kc, vc, ks, vs, kw, vw = [kv[:, i * 512:(i + 1) * 512] for i in range(6)]
    fm = np.concatenate([q, kc, vc, ks, kw, gq, mA, mB], axis=1)
    misc = np.zeros((D, 256), np.float32)
    misc[:, 0:48] = gate
    misc[:, 48:64] = a
    misc[:, 64:80] = b
    tm = np.concatenate([vs, vw, z, misc], axis=1)
    return pretile(fm, 256), pretile(tm, 256)


def build_program():
    nc = bass.Bass("TRN2", target_bir_lowering=False)
    ins = {}

    def inp(name, shape, dt=F32):
        ins[name] = nc.dram_tensor(name, list(shape), dt, kind="ExternalInput").ap()
        return ins[name]

    scr = {}

    def scratch(name, shape, dt=F32):
        kind = "ExternalInput" if name in EXT_IN else ("ExternalOutput" if name in EXT_OUT else "Internal")
        scr[name] = nc.dram_tensor(name, list(shape), dt, kind=kind).ap()
        return scr[name]

    inp("x", [S, D])
    inp("xh", [S // 2, D])
    inp("selw", [128, 2])
    inp("attn_norm", [1, D])
    inp("ffn_norm", [1, D])
    inp("wfm", [FM_GROUPS, 128, 16, 256])
    inp("wtm", [TM_GROUPS, 128, 16, 256])
    inp("qgain", [128, 1])
    inp("kgain", [128, 3])
    inp("posT", [128, 2, 32])
    inp("wcmp", [128, 2, 32, 128])
    inp("convT", [128, 48, 4])
    inp("alog", [1, 16])
    inp("dtb", [1, 16])
    inp("ogain", [1, 128])
    inp("wbra", [8, 128, 16, 256])
    inp("wbrb", [8, 128, 16, 256])
    inp("wout", [8, 128, 16, 256])
    inp("wgate", [22, 128, 16, 256])
    inp("wup", [22, 128, 16, 256])
    inp("wdown", [16, 128, 44, 128])
    inp("agg", [256, 64])
    inp("cmpbias", [NT, 2, 128, 512], BF16)
    inp("tria", [128, 512], BF16)
    inp("trib", [128, 512], BF16)
    inp("expand", [64, NT, 128], BF16)
    inp("gc", [128, 8, 128])
    inp("selmul", [NT, 128, 64])
    inp("seladd", [NT, 128, 64])
    out = nc.dram_tensor("out", [S // 2, D], F32, kind="ExternalOutput").ap()

    scratch("qn", [16, 128, S], BF16)
    scratch("kc", [4, 128, S], BF16)
    scratch("vc", [4, 128, S], BF16)
    scratch("ks", [4, 128, S], BF16)
    scratch("kw", [4, 128, S], BF16)
    scratch("gq", [48, 128, S], F32)
    scratch("sgA", [16, 128, S], F32)
    scratch("sgB", [16, 128, S], F32)
    scratch("vs", [4, S, 128], BF16)
    scratch("vw", [4, S, 128], BF16)
    scratch("z", [S, D], F32)
    scratch("misc", [S, 80], F32)
    scratch("oaT", [16, 128, S], BF16)
    scratch("obT", [16, 128, S], BF16)
    scratch("gqT", [NT, 128, 16, 128], BF16)
    scratch("gkT", [NT, 128, 16, 128], BF16)
    scratch("gktm", [NT, 128, 16, 128], F32)
    scratch("gvtm", [NT, 128, 16, 128], F32)
    scratch("x1", [S // 2, D], F32)

    with ExitStack() as es:
        P = Prog(nc, es)
        if "A" in PHASES:
            phase_A(nc, P, ins, scr)
        if "B" in PHASES:
            phase_B(nc, P, ins, scr)
        if "C" in PHASES:
            phase_C(nc, P, ins, scr)
        if "D" in PHASES:
            phase_D(nc, P, ins, scr)
        if "E" in PHASES:
            phase_E(nc, P, ins, scr, out)
        print("instructions:", P.ninstr, "sems:", len(P.sems))
    return nc


def norm_transpose(P, C, src_rows, gain_bc, ident, hT, col0, ntiles, xts, hbs, junk, ssr, evrot):
    for tt in range(ntiles):
        xt = xts.next()
        hb = hbs.next()
        ss = ssr.next()
        P.ld("sp", xt.t[:], src_rows[tt * 128:(tt + 1) * 128, :], w=[xt], key="xt%d" % (xts.i % 2))
        P.act(junk.t[:], xt.t[:], AF.Square, r=[xt], w=[junk, ss], accum=ss.t[:, 0:1])
        P.act(ss.t[:, 1:2], ss.t[:, 0:1], AF.Sqrt, r=[ss], w=[ss], bias=EPS, scale=1.0 / D)
        P.recip(ss.t[:, 2:3], ss.t[:, 1:2], r=[ss], w=[ss])
        P.stt("dve", hb.t[:], xt.t[:], ss.t[:, 2:3], gain_bc.t[:], ALU.mult, ALU.mult, r=[xt, ss, gain_bc], w=[hb])
        for k4 in range(4):
            pb, pap = C.ps(BF16)
            for kk in range(4):
                k = k4 * 4 + kk
                P.tr(pap[:, kk * 128:(kk + 1) * 128], hb.t[:, k * 128:(k + 1) * 128], ident.t[:], r=[hb, ident], w=[pb])
            eng = evrot.next()
            P.copy(eng, hT.t[:, k4 * 4:(k4 + 1) * 4, col0 + tt * 128: col0 + (tt + 1) * 128],
                   pap[:, 0:512].rearrange("p (a b) -> p a b", a=4), r=[pb], w=[hT])


class WStream:
    def __init__(self, P, C, bf_elems, n_stage=3, n_bf=3, tag="w"):
        self.P = P
        self.st = Rot(C.sbs(n_stage, [128, 2048], F32))
        self.bf = Rot(C.sbs(n_bf, [128, bf_elems], BF16))
        self.tag = tag
        self.n = 0

    def get(self, src, nk, gc):
        P = self.P
        b = self.bf.next()
        bv = b.t[:, 0:nk * gc].rearrange("p (k c) -> p k c", k=nk)
        kstep = max(1, min(2048 // gc, 8))
        k0 = 0
        while k0 < nk:
            k1 = min(nk, k0 + kstep)
            s = self.st.next()
            self.n += 1
            sv = s.t[:, 0:(k1 - k0) * gc].rearrange("p (k c) -> p k c", k=k1 - k0)
            P.ld("sp", sv, src[:, k0:k1, :], w=[s], key="%s%d" % (self.tag, self.n % len(self.st.items)))
            P.copy("pool", bv[:, k0:k1, :], sv, r=[s], w=[b])
            k0 = k1
        return b, bv


def phase_A(nc, P, ins, scr):
    C = Ctx(nc, P)
    C.init_psum()
    hT = C.sb([128, 16, 2048], BF16)
    xts = Rot(C.sbs(2, [128, D], F32))
    hbs = Rot(C.sbs(2, [128, D], BF16))
    junk = C.sb([128, D], BF16)
    gain = C.sb([128, D], F32)
    ssr = Rot(C.sbs(2, [128, 4], F32))
    ident = C.sb([128, 128], BF16)
    ones = C.sb([128, 128], BF16)
    gcf = C.sb([128, 8, 128], F32)
    qg = C.sb([128, 1], F32)
    kg = C.sb([128, 3], F32)
    ws = WStream(P, C, 4096, 3, 3, "wa")
    evf = Rot(C.sbs(3, [128, 512], F32))
    evb = Rot(C.sbs(3, [128, 512], BF16))
    sqs = Rot(C.sbs(2, [128, 512], BF16))
    rts = Rot(C.sbs(2, [128, 512], F32))
    evrot = Rot(["act", "dve"])

    P.ld("sp", gain.t[:], ins["attn_norm"][0:1, :].to_broadcast([128, D]), w=[gain], key="c0")
    P.ld("sp", gcf.t[:], ins["gc"], w=[gcf], key="c1")
    P.ld("sp", qg.t[:], ins["qgain"], w=[qg], key="c2")
    P.ld("sp", kg.t[:], ins["kgain"], w=[kg], key="c3")
    P.copy("dve", ident.t[:], gcf.t[:, GC_ID, :], r=[gcf], w=[ident])
    P.copy("dve", ones.t[:], gcf.t[:, GC_ONES, :], r=[gcf], w=[ones])

    x = ins["x"]
    for half in range(2):
        t0 = half * 2048
        norm_transpose(P, C, x[t0:t0 + 2048, :], gain, ident, hT, 0, 16, xts, hbs, junk, ssr, evrot)
        for g in range(FM_GROUPS):
            wb, wv = ws.get(ins["wfm"][g], 16, 256)
            for sub in range(2):
                ct = g * 2 + sub
                for tg in range(4):
                    pb, pap = C.ps()
                    for k in range(16):
                        P.mm(pap[:, 0:512], wv[:, k, sub * 128:(sub + 1) * 128], hT.t[:, k, tg * 512:(tg + 1) * 512],
                             start=(k == 0), stop=(k == 15), r=[wb, hT], w=[pb])
                    tok = slice(t0 + tg * 512, t0 + (tg + 1) * 512)
                    if ct < 16 or 24 <= ct < 32:
                        if ct < 16:
                            gcol = qg.t[:, 0:1]; gt = qg
                            dst = scr["qn"][ct, :, tok]
                        elif ct < 28:
                            gcol = kg.t[:, 1:2]; gt = kg
                            dst = scr["ks"][ct - 24, :, tok]
                        else:
                            gcol = kg.t[:, 2:3]; gt = kg
                            dst = scr["kw"][ct - 28, :, tok]
                        sq = sqs.next(); rt = rts.next(); eb = evb.next()
                        P.act(sq.t[:], pap[:, 0:512], AF.Square, r=[pb], w=[sq])
                        p2, p2ap = C.ps()
                        P.mm(p2ap[:, 0:512], ones.t[:], sq.t[:], r=[ones, sq], w=[p2])
                        P.act(rt.t[:], p2ap[:, 0:512], AF.Sqrt, r=[p2], w=[rt], bias=EPS, scale=1.0 / 128)
                        P.recip(rt.t[:], rt.t[:], r=[rt], w=[rt])
                        P.stt("dve", eb.t[:], pap[:, 0:512], gcol, rt.t[:], ALU.mult, ALU.mult, r=[pb, gt, rt], w=[eb])
                        P.st("sp", dst, eb.t[:], r=[eb], key="eb%d" % (evb.i % 3))
                    elif ct < 24:
                        eb = evb.next()
                        P.copy(evrot.next(), eb.t[:], pap[:, 0:512], r=[pb], w=[eb])
                        dst = scr["kc"][ct - 16, :, tok] if ct < 20 else scr["vc"][ct - 20, :, tok]
                        P.st("sp", dst, eb.t[:], r=[eb], key="eb%d" % (evb.i % 3))
                    elif ct < 80:
                        ef = evf.next()
                        P.copy(evrot.next(), ef.t[:], pap[:, 0:512], r=[pb], w=[ef])
                        P.st("sp", scr["gq"][ct - 32, :, tok], ef.t[:], r=[ef], key="ef%d" % (evf.i % 3))
                    else:
                        ef = evf.next()
                        P.act(ef.t[:], pap[:, 0:512], AF.Sigmoid, r=[pb], w=[ef])
                        dst = scr["sgA"][ct - 80, :, tok] if ct < 96 else scr["sgB"][ct - 96, :, tok]
                        P.st("sp", dst, ef.t[:], r=[ef], key="ef%d" % (evf.i % 3))
        for g in range(TM_GROUPS):
            wb, wv = ws.get(ins["wtm"][g], 16, 256)
            for tt in range(16):
                pb, pap = C.ps()
                for k in range(16):
                    P.mm(pap[:, 0:256], hT.t[:, k, tt * 128:(tt + 1) * 128], wv[:, k, :],
                         start=(k == 0), stop=(k == 15), r=[wb, hT], w=[pb])
                rows = slice(t0 + tt * 128, t0 + (tt + 1) * 128)
                if g < 4:
                    eb = evb.next()
                    P.copy(evrot.next(), eb.t[:, 0:256], pap[:, 0:256], r=[pb], w=[eb])
                    name = "vs" if g < 2 else "vw"
                    g0 = (g % 2) * 2
                    P.st("sp", scr[name][g0:g0 + 2, rows, :].rearrange("g t d -> t g d"),
                         eb.t[:, 0:256].rearrange("p (g d) -> p g d", g=2), r=[eb], key="eb%d" % (evb.i % 3))
                elif g < 12:
                    ef = evf.next()
                    P.copy(evrot.next(), ef.t[:, 0:256], pap[:, 0:256], r=[pb], w=[ef])
                    P.st("sp", scr["z"][rows, (g - 4) * 256:(g - 3) * 256], ef.t[:, 0:256], r=[ef], key="ef%d" % (evf.i % 3))
                else:
                    ef = evf.next()
                    P.act(ef.t[:, 0:48], pap[:, 0:48], AF.Sigmoid, r=[pb], w=[ef])
                    P.copy("dve", ef.t[:, 48:80], pap[:, 48:80], r=[pb], w=[ef])
                    P.st("sp", scr["misc"][rows, :], ef.t[:, 0:80], r=[ef], key="ef%d" % (evf.i % 3))
    P.barrier()
    P.emit()
    C.close()


def phase_B(nc, P, ins, scr):
    C = Ctx(nc, P)
    C.init_psum()
    SC = 128.0 ** -0.5
    gcf = C.sb([128, 8, 128], F32)
    identb = C.sb([128, 128], BF16)
    onesb = C.sb([128, 128], BF16)
    tria = C.sb([128, 512], BF16)
    trib = C.sb([128, 512], BF16)
    expand = C.sb([64, NT, 128], BF16)
    wck = C.sb([128, 32, 128], BF16)
    wcv = C.sb([128, 32, 128], BF16)
    wst = Rot(C.sbs(2, [128, 8, 128], F32))
    posT = C.sb([128, 2, 32], F32)
    kg = C.sb([128, 3], F32)
    aggf = C.sb([128, 2, 64], F32)
    gates = C.sb([128, NT, 48], F32)
    selmul = C.sb([128, NT, 64], F32)
    seladd = C.sb([128, NT, 64], F32)
    ksT = C.sb([128, S], BF16)
    kwT = C.sb([128, S], BF16)
    vsa = C.sb([128, NT, 129], BF16)
    vwa = C.sb([128, NT, 129], BF16)
    kcr = C.sb([128, S], BF16)
    vcr = C.sb([128, S], BF16)
    kaug = C.sb([128, 32, 256], BF16)
    vaug = C.sb([128, 32, 256], BF16)
    kcmpT = C.sb([128, 256], BF16)
    vcmpa = C.sb([128, 2, 193], BF16)
    sqc = C.sb([128, 256], BF16)
    rtc = C.sb([128, 256], F32)
    qTs = Rot(C.sbs(2, [128, 512], BF16))
    cbs = Rot(C.sbs(2, [128, 2, 512], BF16))
    pTs = Rot(C.sbs(3, [128, 512], BF16))
    oaccs = Rot(C.sbs(2, [128, 4, 128], F32))
    oabs = Rot(C.sbs(2, [128, 4, 128], BF16))
    oTs = Rot(C.sbs(2, [128, 4, 128], BF16))
    smalls = Rot(C.sbs(4, [128, 16], F32))
    imps = Rot(C.sbs(2, [128, 64], F32))
    scs = Rot(C.sbs(2, [128, 64], F32))
    sc2s = Rot(C.sbs(2, [128, 64], F32))
    m8s = Rot(C.sbs(2, [128, 16], F32))
    brows = Rot(C.sbs(2, [128, 64], BF16))
    biasTs = Rot(C.sbs(2, [64, 512], BF16))
    evrot = Rot(["act", "dve"])
    dprot = Rot(["dve", "pool"])

    P.ld("sp", gcf.t[:], ins["gc"], w=[gcf], key="c0")
    P.copy("dve", identb.t[:], gcf.t[:, GC_ID, :], r=[gcf], w=[identb])
    P.copy("dve", onesb.t[:], gcf.t[:, GC_ONES, :], r=[gcf], w=[onesb])
    P.ld("sp", tria.t[:], ins["tria"], w=[tria], key="c1")
    P.ld("sp", trib.t[:], ins["trib"], w=[trib], key="c2")
    P.ld("sp", expand.t[:], ins["expand"], w=[expand], key="c3")
    P.ld("sp", posT.t[:], ins["posT"], w=[posT], key="c4")
    P.ld("sp", kg.t[:], ins["kgain"], w=[kg], key="c5")
    P.ld("sp", aggf.t[:], ins["agg"].rearrange("(j p) c -> p j c", p=128), w=[aggf], key="c6")
    for t8 in range(4):
        ts_ = slice(t8 * 8, (t8 + 1) * 8)
        P.ld("sp", gates.t[:, ts_, :], scr["misc"][t8 * 1024:(t8 + 1) * 1024, 0:48].rearrange("(t p) c -> p t c", p=128), w=[gates], key="c7")
        P.ld("sp", selmul.t[:, ts_, :], ins["selmul"][ts_].rearrange("t p c -> p t c"), w=[selmul], key="c8")
        P.ld("sp", seladd.t[:, ts_, :], ins["seladd"][ts_].rearrange("t p c -> p t c"), w=[seladd], key="c9")
    for kv, dstw in ((0, wck), (1, wcv)):
        for l4 in range(4):
            st = wst.next()
            P.ld("sp", st.t[:], ins["wcmp"][:, kv, l4 * 8:(l4 + 1) * 8, :], w=[st], key="wc%d" % (wst.i % 2))
            P.copy("pool", dstw.t[:, l4 * 8:(l4 + 1) * 8, :], st.t[:], r=[st], w=[dstw])
    P.memset("pool", kaug.t[:, :, 255:256], 0.0, w=[kaug])
    P.memset("pool", vaug.t[:, :, 255:256], 0.0, w=[vaug])
    P.memset("pool", vsa.t[:, :, 128:129], 1.0, w=[vsa])
    P.memset("pool", vwa.t[:, :, 128:129], 1.0, w=[vwa])
    P.memset("pool", vcmpa.t[:, :, 128:129], 1.0, w=[vcmpa])
    P.copy("dve", vcmpa.t[:, :, 129:193], aggf.t[:], r=[aggf], w=[vcmpa])

    oset = []
    for i in range(2):
        b0, a0 = C.take()
        b1, a1 = C.take()
        oset.append(((b0, a0), (b1, a1)))
    orot = Rot(oset)

    for g in range(4):
        P.ld("sp", kcr.t[:], scr["kc"][g], w=[kcr], key="g0")
        P.ld("sp", vcr.t[:], scr["vc"][g], w=[vcr], key="g1")
        P.ld("sp", ksT.t[:], scr["ks"][g], w=[ksT], key="g2")
        P.ld("sp", kwT.t[:], scr["kw"][g], w=[kwT], key="g3")
        for t8 in range(4):
            ts_ = slice(t8 * 8, (t8 + 1) * 8)
            P.ld("sp", vsa.t[:, ts_, 0:128], scr["vs"][g, t8 * 1024:(t8 + 1) * 1024, :].rearrange("(t p) d -> p t d", p=128), w=[vsa], key="g4")
            P.ld("sp", vwa.t[:, ts_, 0:128], scr["vw"][g, t8 * 1024:(t8 + 1) * 1024, :].rearrange("(t p) d -> p t d", p=128), w=[vwa], key="g5")
        for l in range(32):
            P.ts(dprot.next(), kaug.t[:, l, 0:255], kcr.t[:, l:l + 4065:16], posT.t[:, 0, l:l + 1], None, ALU.add, r=[kcr, posT], w=[kaug])
            P.ts(dprot.next(), vaug.t[:, l, 0:255], vcr.t[:, l:l + 4065:16], posT.t[:, 1, l:l + 1], None, ALU.add, r=[vcr, posT], w=[vaug])
        pk, pkap = C.ps()
        for l in range(32):
            P.mm(pkap[:, 0:256], wck.t[:, l, :], kaug.t[:, l, :], start=(l == 0), stop=(l == 31), r=[wck, kaug], w=[pk])
        P.act(sqc.t[:], pkap[:, 0:256], AF.Square, r=[pk], w=[sqc])
        p2, p2ap = C.ps()
        P.mm(p2ap[:, 0:256], onesb.t[:], sqc.t[:], r=[onesb, sqc], w=[p2])
        P.act(rtc.t[:], p2ap[:, 0:256], AF.Sqrt, r=[p2], w=[rtc], bias=EPS, scale=1.0 / 128)
        P.recip(rtc.t[:], rtc.t[:], r=[rtc], w=[rtc])
        P.stt("dve", kcmpT.t[:], pkap[:, 0:256], kg.t[:, 0:1], rtc.t[:], ALU.mult, ALU.mult, r=[pk, kg, rtc], w=[kcmpT])
        for j in range(2):
            pv, pvap = C.ps()
            for l in range(32):
                P.mm(pvap[:, 0:128], vaug.t[:, l, j * 128:(j + 1) * 128], wcv.t[:, l, :], start=(l == 0), stop=(l == 31), r=[wcv, vaug], w=[pv])
            P.copy("act", vcmpa.t[:, j, 0:128], pvap[:, 0:128], r=[pv], w=[vcmpa])

        for qt in range(NT):
            qT = qTs.next()
            cb = cbs.next()
            P.ld("sp", qT.t[:].rearrange("p (h t) -> p h t", h=4), scr["qn"][4 * g:4 * g + 4, :, qt * 128:(qt + 1) * 128].rearrange("h d t -> d h t"), w=[qT], key="q%d" % (qTs.i % 2))
            P.ld("sp", cb.t[:], ins["cmpbias"][qt].rearrange("j n c -> n j c"), w=[cb], key="cb%d" % (cbs.i % 2))
            oacc = oaccs.next()
            sm = smalls.next()
            (ob0, oa0), (ob1, oa1) = orot.next()
            obk = (ob0, ob1); oap = (oa0, oa1)
            njt = 1 if qt < 16 else 2
            for j in range(njt):
                sb_, sap = C.ps()
                P.mm(sap[:, 0:512], kcmpT.t[:, j * 128:(j + 1) * 128], qT.t[:], start=True, stop=False, r=[kcmpT, qT], w=[sb_])
                P.mm(sap[:, 0:512], identb.t[:], cb.t[:, j, :], start=False, stop=True, r=[identb, cb], w=[sb_])
                pt = pTs.next()
                P.act(pt.t[:], sap[:, 0:512], AF.Exp, r=[sb_], w=[pt], scale=SC)
                for r in range(4):
                    P.mm(oap[r // 2][:, (r % 2) * 193:(r % 2) * 193 + 193], pt.t[:, r * 128:(r + 1) * 128], vcmpa.t[:, j, :],
                         start=(j == 0 and r % 2 == 0), stop=(j == njt - 1), r=[pt, vcmpa], w=[obk[r // 2]], sgc=True)
            for r in range(4):
                zc = (r % 2) * 193 + 128
                P.ts("dve", sm.t[:, r:r + 1], oap[r // 2][:, zc:zc + 1], 1e-30, None, ALU.max, r=[obk[r // 2]], w=[sm])
            P.recip(sm.t[:, 4:8], sm.t[:, 0:4], r=[sm], w=[sm])
            P.tt("dve", sm.t[:, 8:12], sm.t[:, 4:8], gates.t[:, qt, 12 * g + 0:12 * g + 12:3], ALU.mult, r=[sm, gates], w=[sm])
            imp = imps.next()
            for r in range(4):
                ic = (r % 2) * 193 + 129
                if r == 0:
                    P.ts("dve", imp.t[:], oap[0][:, ic:ic + 64], sm.t[:, 4:5], None, ALU.mult, r=[obk[0], sm], w=[imp])
                else:
                    P.stt("dve", imp.t[:], oap[r // 2][:, ic:ic + 64], sm.t[:, 4 + r:5 + r], imp.t[:], ALU.mult, ALU.add, r=[obk[r // 2], sm, imp], w=[imp])
                oc = (r % 2) * 193
                P.ts("dve", oacc.t[:, r, :], oap[r // 2][:, oc:oc + 128], sm.t[:, 8 + r:9 + r], None, ALU.mult, r=[obk[r // 2], sm], w=[oacc])
            sc = scs.next(); sc2 = sc2s.next(); m8 = m8s.next(); brow = brows.next(); biasT = biasTs.next()
            P.tt("dve", sc.t[:], imp.t[:], selmul.t[:, qt, :], ALU.mult, r=[imp, selmul], w=[sc])
            P.tt("dve", sc.t[:], sc.t[:], seladd.t[:, qt, :], ALU.add, r=[sc, seladd], w=[sc])
            P.op("dve", lambda e, o=m8.t[:, 0:8], i=sc.t[:]: e.max(out=o, in_=i), [sc.b], [m8.b])
            P.op("dve", lambda e, o=sc2.t[:], a=m8.t[:, 0:8], v=sc.t[:]: e.match_replace(out=o, in_to_replace=a, in_values=v, imm_value=-1e30), [sc.b, m8.b], [sc2.b])
            P.op("dve", lambda e, o=m8.t[:, 8:16], i=sc2.t[:]: e.max(out=o, in_=i), [sc2.b], [m8.b])
            P.ts("dve", sc2.t[:], sc.t[:], m8.t[:, 15:16], None, ALU.is_ge, r=[sc, m8], w=[sc2])
            P.ts("dve", brow.t[:], sc2.t[:], -NEG, NEG, ALU.mult, ALU.add, r=[sc2], w=[brow])
            pbt, pbtap = C.ps(BF16)
            P.tr(pbtap[0:64, 0:128], brow.t[:], identb.t[:], r=[brow, identb], w=[pbt])
            for r in range(4):
                P.copy(evrot.next(), biasT.t[:, r * 128:(r + 1) * 128], pbtap[0:64, 0:128], r=[pbt], w=[biasT])
            for br in (1, 2):
                (ob0, oa0), (ob1, oa1) = orot.next()
                obk = (ob0, ob1); oap = (oa0, oa1)
                kT = ksT if br == 1 else kwT
                va = vsa if br == 1 else vwa
                kts = list(range(0, qt + 1)) if br == 1 else list(range(max(0, qt - 4), qt + 1))
                for idx, kt in enumerate(kts):
                    sb_, sap = C.ps()
                    extra = []
                    if br == 1:
                        extra.append((expand.t[:, kt, :], biasT.t[:], [expand, biasT]))
                    if kt == qt:
                        extra.append((identb.t[:], tria.t[:], [identb, tria]))
                    if br == 2 and kt == qt - 4:
                        extra.append((identb.t[:], trib.t[:], [identb, trib]))
                    P.mm(sap[:, 0:512], kT.t[:, kt * 128:(kt + 1) * 128], qT.t[:], start=True, stop=(len(extra) == 0), r=[kT, qT], w=[sb_])
                    for ei, (l_, r_, rd) in enumerate(extra):
                        P.mm(sap[:, 0:512], l_, r_, start=False, stop=(ei == len(extra) - 1), r=rd, w=[sb_])
                    pt = pTs.next()
                    P.act(pt.t[:], sap[:, 0:512], AF.Exp, r=[sb_], w=[pt], scale=SC)
                    for r in range(4):
                        P.mm(oap[r // 2][:, (r % 2) * 193:(r % 2) * 193 + 129], pt.t[:, r * 128:(r + 1) * 128], va.t[:, kt, :],
                             start=(idx == 0 and r % 2 == 0), stop=(idx == len(kts) - 1), r=[pt, va], w=[obk[r // 2]], sgc=True)
                sm2 = smalls.next()
                for r in range(4):
                    zc = (r % 2) * 193 + 128
                    P.copy("dve", sm2.t[:, r:r + 1], oap[r // 2][:, zc:zc + 1], r=[obk[r // 2]], w=[sm2])
                P.recip(sm2.t[:, 4:8], sm2.t[:, 0:4], r=[sm2], w=[sm2])
                P.tt("dve", sm2.t[:, 8:12], sm2.t[:, 4:8], gates.t[:, qt, 12 * g + br:12 * g + 12:3], ALU.mult, r=[sm2, gates], w=[sm2])
                for r in range(4):
                    oc = (r % 2) * 193
                    P.stt("dve", oacc.t[:, r, :], oap[r // 2][:, oc:oc + 128], sm2.t[:, 8 + r:9 + r], oacc.t[:, r, :], ALU.mult, ALU.add,
                          r=[obk[r // 2], sm2, oacc], w=[oacc])
            oab = oabs.next(); oT = oTs.next()
            P.copy("pool", oab.t[:], oacc.t[:], r=[oacc], w=[oab])
            po, poap = C.ps(BF16)
            for r in range(4):
                P.tr(poap[:, r * 128:(r + 1) * 128], oab.t[:, r, :], identb.t[:], r=[oab, identb], w=[po])
            P.copy("act", oT.t[:], poap[:, 0:512].rearrange("p (h t) -> p h t", h=4), r=[po], w=[oT])
            P.st("sp", scr["oaT"][4 * g:4 * g + 4, :, qt * 128:(qt + 1) * 128].rearrange("h d t -> d h t"), oT.t[:], r=[oT], key="oT%d" % (oTs.i % 2))
    P.barrier()
    P.emit()
    C.close()


CSUB = os.environ.get("MK_CSUB", "12")


def phase_C(nc, P, ins, scr):
    if "1" in CSUB:
        phase_C1(nc, P, ins, scr)
    if "2" in CSUB:
        phase_C2(nc, P, ins, scr)


def phase_C1(nc, P, ins, scr):
    C = Ctx(nc, P)
    C.init_psum()
    gcf = C.sb([128, 8, 128], F32)
    onesb = C.sb([128, 128], BF16)
    convw = C.sb([128, 48, 4], F32)
    ctmp = C.sb([128, S], F32)
    raws = Rot(C.sbs(2, [128, S], F32))
    ys = Rot(C.sbs(2, [128, S], F32))
    fbs = Rot(C.sbs(2, [128, S], BF16))
    tms = Rot(C.sbs(2, [128, NT, 128], F32))
    sqs = Rot(C.sbs(2, [128, 512], BF16))
    rts = Rot(C.sbs(2, [128, 512], F32))
    evrot = Rot(["act", "dve"])
    P.ld("sp", gcf.t[:], ins["gc"], w=[gcf], key="c0")
    P.ld("sp", convw.t[:], ins["convT"], w=[convw], key="c1")
    P.copy("dve", onesb.t[:], gcf.t[:, GC_ONES, :], r=[gcf], w=[onesb])
    identf = gcf.t[:, GC_ID, :]
    for ct in range(48):
        kind, h = ct // 16, ct % 16
        raw = raws.next(); y = ys.next()
        eng = "dve" if ct % 2 == 0 else "pool"
        P.ld("sp", raw.t[:], scr["gq"][ct], w=[raw], key="r%d" % (raws.i % 2))
        P.ts(eng, y.t[:], raw.t[:], convw.t[:, ct, 3:4], None, ALU.mult, r=[raw, convw], w=[y])
        for sh in (1, 2, 3):
            if eng == "dve":
                P.stt(eng, y.t[:, sh:S], raw.t[:, 0:S - sh], convw.t[:, ct, 3 - sh:4 - sh], y.t[:, sh:S], ALU.mult, ALU.add, r=[raw, convw, y], w=[y])
            else:
                P.ts(eng, ctmp.t[:, 0:S - sh], raw.t[:, 0:S - sh], convw.t[:, ct, 3 - sh:4 - sh], None, ALU.mult, r=[raw, convw], w=[ctmp])
                P.tt(eng, y.t[:, sh:S], y.t[:, sh:S], ctmp.t[:, 0:S - sh], ALU.add, r=[y, ctmp], w=[y])
        P.act(y.t[:], y.t[:], AF.Silu, r=[y], w=[y])
        if kind < 2:
            fb = fbs.next()
            for tg in range(8):
                sl = slice(tg * 512, (tg + 1) * 512)
                sq = sqs.next(); rt = rts.next()
                P.act(sq.t[:], y.t[:, sl], AF.Square, r=[y], w=[sq])
                pb, pap = C.ps()
                P.mm(pap[:, 0:512], onesb.t[:], sq.t[:], r=[onesb, sq], w=[pb])
                P.act(rt.t[:], pap[:, 0:512], AF.Sqrt, r=[pb], w=[rt], bias=EPS, scale=1.0)
                P.recip(rt.t[:], rt.t[:], r=[rt], w=[rt])
                if kind == 0:
                    P.stt("dve", fb.t[:, sl], y.t[:, sl], 128.0 ** -0.5, rt.t[:], ALU.mult, ALU.mult, r=[y, rt], w=[fb])
                else:
                    P.tt("dve", y.t[:, sl], y.t[:, sl], rt.t[:], ALU.mult, r=[y, rt], w=[y])
                    P.copy("pool", fb.t[:, sl], y.t[:, sl], r=[y], w=[fb])
            dst = scr["gqT"] if kind == 0 else scr["gkT"]
            for t8 in range(4):
                P.st("sp", dst[t8 * 8:(t8 + 1) * 8, :, h, :].rearrange("t d c -> d t c"), fb.t[:, t8 * 1024:(t8 + 1) * 1024].rearrange("p (t c) -> p t c", t=8), r=[fb], key="fb%d" % (fbs.i % 2))
        if kind >= 1:
            tm = tms.next()
            for t4 in range(NT // 4):
                pb, pap = C.ps()
                for tt in range(4):
                    tile = t4 * 4 + tt
                    P.tr(pap[:, tt * 128:(tt + 1) * 128], y.t[:, tile * 128:(tile + 1) * 128], identf, r=[y, gcf], w=[pb])
                P.copy(evrot.next(), tm.t[:, t4 * 4:(t4 + 1) * 4, :], pap[:, 0:512].rearrange("p (a b) -> p a b", a=4), r=[pb], w=[tm])
            dst = scr["gktm"] if kind == 1 else scr["gvtm"]
            for t8 in range(4):
                P.st("sp", dst[t8 * 8:(t8 + 1) * 8, :, h, :].rearrange("t c d -> c t d"), tm.t[:, t8 * 8:(t8 + 1) * 8, :], r=[tm], key="tm%d" % (tms.i % 2))
    P.barrier()
    P.emit()
    C.close()


def phase_C2(nc, P, ins, scr):
    C = Ctx(nc, P)
    C.init_psum()
    gcf = C.sb([128, 8, 128], F32)
    identb = C.sb([128, 128], BF16)
    misc = C.sb([128, NT, 32], F32)
    alog = C.sb([128, 16], F32)
    dtb = C.sb([128, 16], F32)
    og16 = C.sb([128, 16, 128], F32)
    gall = C.sb([128, NT, 16], F32)
    ball = C.sb([128, NT, 16], F32)
    nball = C.sb([128, NT, 16], F32)
    St = C.sb([128, 16, 128], F32)
    Sb = [C.sb([128, 128], BF16) for _ in range(16)]
    qTs = Rot(C.sbs(2, [128, 16, 128], BF16))
    kTs = Rot(C.sbs(2, [128, 16, 128], BF16))
    ktms = Rot(C.sbs(2, [128, 16, 128], F32))
    vtms = Rot(C.sbs(2, [128, 16, 128], F32))
    zts = Rot(C.sbs(1, [128, 16, 128], F32))
    scal = Rot(C.sbs(2, [128, 5, 16], F32))
    GUs = Rot(C.sbs(8, [128, 128], F32))
    decs = Rot(C.sbs(8, [128, 256], F32))
    Ms = Rot(C.sbs(12, [128, 2, 128], F32))
    Rs = Rot(C.sbs(12, [128, 128], F32))
    Rbs = Rot(C.sbs(8, [128, 128], BF16))
    vbs = Rot(C.sbs(8, [128, 128], BF16))
    kbgs = Rot(C.sbs(8, [128, 128], BF16))
    u16 = C.sb([128, 16, 128], F32)
    wT16 = [C.sb([128, 128], BF16) for _ in range(16)]
    qkd16 = [C.sb([128, 128], BF16) for _ in range(16)]
    kd16 = [C.sb([128, 128], BF16) for _ in range(16)]
    vn16 = [C.sb([128, 128], BF16) for _ in range(16)]
    o1s16 = [C.sb([128, 128], F32) for _ in range(16)]
    ots = Rot(C.sbs(1, [128, 16, 128], F32))
    obs = Rot(C.sbs(1, [128, 16, 128], BF16))
    obTs = Rot(C.sbs(2, [128, 16, 128], BF16))
    sss = Rot(C.sbs(2, [128, 48], F32))
    junk = C.sb([128, 128], F32)
    tmp16 = C.sb([128, NT, 16], F32)
    evrot = Rot(["act", "dve"])

    P.ld("sp", gcf.t[:], ins["gc"], w=[gcf], key="c0")
    P.copy("dve", identb.t[:], gcf.t[:, GC_ID, :], r=[gcf], w=[identb])
    for t8 in range(4):
        P.ld("sp", misc.t[:, t8 * 8:(t8 + 1) * 8, :], scr["misc"][t8 * 1024:(t8 + 1) * 1024, 48:80].rearrange("(t p) c -> p t c", p=128), w=[misc], key="c1")
    P.ld("sp", alog.t[:], ins["alog"][0:1, :].to_broadcast([128, 16]), w=[alog], key="c2")
    P.ld("sp", dtb.t[:], ins["dtb"][0:1, :].to_broadcast([128, 16]), w=[dtb], key="c3")
    for h in range(16):
        P.ld("sp", og16.t[:, h, :], ins["ogain"][0:1, :].to_broadcast([128, 128]), w=[og16], key="c4")
    P.act(alog.t[:], alog.t[:], AF.Exp, r=[alog], w=[alog])
    for t in range(NT):
        P.tt("dve", tmp16.t[:, t, :], misc.t[:, t, 0:16], dtb.t[:], ALU.add, r=[misc, dtb], w=[tmp16])
    P.act(tmp16.t[:], tmp16.t[:], AF.Exp, r=[tmp16], w=[tmp16])
    P.act(tmp16.t[:], tmp16.t[:], AF.Ln, r=[tmp16], w=[tmp16], bias=1.0, scale=1.0)
    for t in range(NT):
        P.stt("dve", gall.t[:, t, :], tmp16.t[:, t, :], -1.0, alog.t[:], ALU.mult, ALU.mult, r=[tmp16, alog], w=[gall])
    P.act(ball.t[:], misc.t[:, :, 16:32], AF.Sigmoid, r=[misc], w=[ball])
    P.ts("dve", nball.t[:], ball.t[:], -1.0, None, ALU.mult, r=[ball], w=[nball])
    P.memset("pool", St.t[:], 0.0, w=[St])
    for h in range(16):
        P.memset("pool", Sb[h].t[:], 0.0, w=[Sb[h]])

    TRIT = gcf.t[:, GC_TRIT, :]; USTR = gcf.t[:, GC_USTR, :]; BL = gcf.t[:, GC_BL, :]; BU = gcf.t[:, GC_BU, :]
    CI0 = gcf.t[:, GC_CI0, :]; CI1 = gcf.t[:, GC_CI1, :]; IDF = gcf.t[:, GC_ID, :]

    C2STAGE = int(os.environ.get("MK_C2STAGE", "3"))
    C2TILES = int(os.environ.get("MK_C2TILES", str(NT)))
    C2SUB = int(os.environ.get("MK_C2SUB", "9"))
    for tile in range(C2TILES):
        qT = qTs.next(); kT = kTs.next(); ktm = ktms.next(); vtm = vtms.next(); zt = zts.next()
        si = qTs.i % 2
        P.ld("sp", qT.t[:], scr["gqT"][tile], w=[qT], key="lq%d" % si)
        P.ld("sp", kT.t[:], scr["gkT"][tile], w=[kT], key="lk%d" % si)
        P.ld("sp", ktm.t[:], scr["gktm"][tile], w=[ktm], key="lkt%d" % si)
        P.ld("sp", vtm.t[:], scr["gvtm"][tile], w=[vtm], key="lvt%d" % si)
        P.ld("sp", zt.t[:].rearrange("p h d -> p (h d)"), scr["z"][tile * 128:(tile + 1) * 128, :], w=[zt], key="lz%d" % si)
        if C2SUB < 2:
            continue
        sc = scal.next()
        g_t = gall.t[:, tile, :]
        pb, pap = C.ps()
        P.mm(pap[:, 0:16], TRIT, g_t, r=[gcf, gall], w=[pb])
        P.mm(pap[:, 16:32], USTR, g_t, r=[gcf, gall], w=[pb])
        P.mm(pap[:, 32:48], CI0, g_t, r=[gcf, gall], w=[pb])
        P.mm(pap[:, 48:64], CI1, g_t, r=[gcf, gall], w=[pb])
        P.act(sc.t[:, 0:4, :].rearrange("p a b -> p (a b)"), pap[:, 0:64], AF.Exp, r=[pb], w=[sc])
        P.tt("dve", sc.t[:, 4, :], sc.t[:, 0, :], ball.t[:, tile, :], ALU.mult, r=[sc, ball], w=[sc])
        for hg in range(4 if C2SUB >= 3 else 0):
            hs = list(range(hg * 4, hg * 4 + 4))
            GU = {}; dec = {}; MN = {}; R = {}
            for h in hs:
                GU[h] = GUs.next()
                P.ts("pool", GU[h].t[:], USTR, gall.t[:, tile, h:h + 1], None, ALU.mult, r=[gcf, gall], w=[GU[h]])
            pd = {}
            for h in hs:
                pd[h] = C.ps()
                b_, a_ = pd[h]
                P.mm(a_[:, 0:128], TRIT, GU[h].t[:], start=True, stop=False, r=[gcf, GU[h]], w=[b_])
                P.mm(a_[:, 0:128], IDF, BL, start=False, stop=True, r=[gcf], w=[b_])
                P.mm(a_[:, 128:256], GU[h].t[:], TRIT, start=True, stop=False, r=[gcf, GU[h]], w=[b_])
                P.mm(a_[:, 128:256], IDF, BU, start=False, stop=True, r=[gcf], w=[b_])
            for h in hs:
                dec[h] = decs.next()
                b_, a_ = pd[h]
                P.act(dec[h].t[:], a_[:, 0:256], AF.Exp, r=[b_], w=[dec[h]])
            pk = {}
            for h in hs:
                pk[h] = C.ps()
                b_, a_ = pk[h]
                P.mm(a_[:, 0:128], kT.t[:, h, :], kT.t[:, h, :], r=[kT], w=[b_])
                P.mm(a_[:, 128:256], kT.t[:, h, :], qT.t[:, h, :], r=[kT, qT], w=[b_])
            for h in hs:
                MN[h] = Ms.next()
                b_, a_ = pk[h]
                P.stt("dve", MN[h].t[:, 0, :], a_[:, 0:128], nball.t[:, tile, h:h + 1], dec[h].t[:, 0:128], ALU.mult, ALU.mult, r=[b_, nball, dec[h]], w=[MN[h]])
                P.tt("dve", qkd16[h].t[:], a_[:, 128:256], dec[h].t[:, 128:256], ALU.mult, r=[b_, dec[h]], w=[qkd16[h]])
            pn = {}
            for h in hs:
                pn[h] = C.ps()
                b_, a_ = pn[h]
                P.tr(a_[:, 0:128], MN[h].t[:, 0, :], IDF, r=[MN[h], gcf], w=[b_])
            for h in hs:
                b_, a_ = pn[h]
                P.copy("act", MN[h].t[:, 1, :], a_[:, 0:128], r=[b_], w=[MN[h]])
            for h in hs:
                R[h] = Rs.next()
                P.tt("pool", R[h].t[:], MN[h].t[:, 1, :], IDF, ALU.add, r=[MN[h], gcf], w=[R[h]])
            for k in range(1, 6):
                pm = {}; MN2 = {}; pr = {}
                for h in hs:
                    pm[h] = C.ps()
                    b_, a_ = pm[h]
                    P.mm(a_[:, 0:128], MN[h].t[:, 1, :], MN[h].t[:, 0, :], r=[MN[h]], w=[b_])
                    if k < 5:
                        P.mm(a_[:, 128:256], MN[h].t[:, 0, :], MN[h].t[:, 1, :], r=[MN[h]], w=[b_])
                for h in hs:
                    MN2[h] = Ms.next()
                    b_, a_ = pm[h]
                    if k < 5:
                        P.copy("act", MN2[h].t[:].rearrange("p a b -> p (a b)"), a_[:, 0:256], r=[b_], w=[MN2[h]])
                    else:
                        P.copy("act", MN2[h].t[:, 0, :], a_[:, 0:128], r=[b_], w=[MN2[h]])
                for h in hs:
                    pr[h] = C.ps()
                    b_, a_ = pr[h]
                    P.mm(a_[:, 0:128], MN2[h].t[:, 0, :], R[h].t[:], r=[MN2[h], R[h]], w=[b_])
                for h in hs:
                    R2 = Rs.next()
                    b_, a_ = pr[h]
                    P.tt("dve", R2.t[:], a_[:, 0:128], R[h].t[:], ALU.add, r=[b_, R[h]], w=[R2])
                    R[h] = R2
                    MN[h] = MN2[h]
            Rb = {}; vb = {}; kbg = {}
            for h in hs:
                Rb[h] = Rbs.next(); vb[h] = vbs.next(); kbg[h] = kbgs.next()
                P.copy("pool", Rb[h].t[:], R[h].t[:], r=[R[h]], w=[Rb[h]])
                P.ts("dve", vb[h].t[:], vtm.t[:, h, :], ball.t[:, tile, h:h + 1], None, ALU.mult, r=[vtm, ball], w=[vb[h]])
                P.ts("dve", kbg[h].t[:], ktm.t[:, h, :], sc.t[:, 4, h:h + 1], None, ALU.mult, r=[ktm, sc], w=[kbg[h]])
                P.ts("dve", kd16[h].t[:], ktm.t[:, h, :], sc.t[:, 1, h:h + 1], None, ALU.mult, r=[ktm, sc], w=[kd16[h]])
            pu = {}
            for h in hs:
                pu[h] = C.ps()
                b_, a_ = pu[h]
                P.mm(a_[:, 0:128], Rb[h].t[:], vb[h].t[:], r=[Rb[h], vb[h]], w=[b_])
                P.mm(a_[:, 128:256], kbg[h].t[:], Rb[h].t[:], r=[Rb[h], kbg[h]], w=[b_])
            for h in hs:
                b_, a_ = pu[h]
                P.copy("act", u16.t[:, h, :], a_[:, 0:128], r=[b_], w=[u16])
                P.copy("act", wT16[h].t[:], a_[:, 128:256], r=[b_], w=[wT16[h]])
        for ci in range(2 if C2STAGE >= 2 else 0):
            rows = slice(ci * 64, (ci + 1) * 64)
            pend = None
            for i in range(17):
                if i < 16:
                    h = i
                    p1, p1ap = C.ps()
                    P.mm(p1ap[:, 0:128], wT16[h].t[:], Sb[h].t[:], r=[wT16[h], Sb[h]], w=[p1])
                    P.mm(p1ap[:, 128:256], qT.t[:, h, :], Sb[h].t[:], r=[qT, Sb[h]], w=[p1])
                    P.tt("dve", vn16[h].t[rows, :], u16.t[rows, h, :], p1ap[rows, 0:128], ALU.subtract, r=[u16, p1], w=[vn16[h]])
                    P.ts("dve", o1s16[h].t[rows, :], p1ap[rows, 128:256], sc.t[rows, 0, h:h + 1], None, ALU.mult, r=[p1, sc], w=[o1s16[h]])
                if i >= 1:
                    h = i - 1
                    p3, p3ap = C.ps()
                    P.mm(p3ap[:, 0:128], kd16[h].t[rows, :], vn16[h].t[rows, :], r=[kd16[h], vn16[h]], w=[p3])
                    P.stt("dve", St.t[:, h, :], St.t[:, h, :], sc.t[:, 2 + ci, h:h + 1], p3ap[:, 0:128], ALU.mult, ALU.add, r=[St, sc, p3], w=[St])
                    P.copy("pool", Sb[h].t[:], St.t[:, h, :], r=[St], w=[Sb[h]])
        if C2STAGE < 3:
            continue
        ot = ots.next(); ob = obs.next(); obT = obTs.next(); ss = sss.next()
        for h in range(16):
            p2, p2ap = C.ps()
            P.mm(p2ap[:, 0:128], qkd16[h].t[:], vn16[h].t[:], r=[qkd16[h], vn16[h]], w=[p2])
            P.tt("dve", ot.t[:, h, :], p2ap[:, 0:128], o1s16[h].t[:], ALU.add, r=[p2, o1s16[h]], w=[ot])
            P.act(junk.t[:], ot.t[:, h, :], AF.Square, r=[ot], w=[junk, ss], accum=ss.t[:, h:h + 1])
        P.act(ss.t[:, 16:32], ss.t[:, 0:16], AF.Sqrt, r=[ss], w=[ss], bias=EPS, scale=1.0 / 128)
        P.recip(ss.t[:, 32:48], ss.t[:, 16:32], r=[ss], w=[ss])
        P.act(zt.t[:], zt.t[:], AF.Silu, r=[zt], w=[zt])
        P.tt("pool", zt.t[:], zt.t[:], og16.t[:], ALU.mult, r=[zt, og16], w=[zt])
        for h in range(16):
            P.stt("dve", ob.t[:, h, :], ot.t[:, h, :], ss.t[:, 32 + h:33 + h], zt.t[:, h, :], ALU.mult, ALU.mult, r=[ot, ss, zt], w=[ob])
        for h4 in range(4):
            pt, ptap = C.ps(BF16)
            for hh in range(4):
                h = h4 * 4 + hh
                P.tr(ptap[:, hh * 128:(hh + 1) * 128], ob.t[:, h, :], identb.t[:], r=[ob, identb], w=[pt])
            P.copy(evrot.next(), obT.t[:, h4 * 4:(h4 + 1) * 4, :], ptap[:, 0:512].rearrange("p (a b) -> p a b", a=4), r=[pt], w=[obT])
        for h8 in range(2):
            P.st("sp", scr["obT"][h8 * 8:(h8 + 1) * 8, :, tile * 128:(tile + 1) * 128].rearrange("h d t -> d h t"), obT.t[:, h8 * 8:(h8 + 1) * 8, :], r=[obT], key="oT%d" % (obTs.i % 2))
    P.barrier()
    P.emit()
    C.close()


def phase_D(nc, P, ins, scr):
    C = Ctx(nc, P)
    C.init_psum()
    oaT = C.sb([128, 16, 512], BF16)
    obT = C.sb([128, 16, 512], BF16)
    mixT = C.sb([128, 16, 512], BF16)
    ws = WStream(P, C, 4096, 4, 4, "wd")
    sga = Rot(C.sbs(2, [128, 512], F32))
    sgb = Rot(C.sbs(2, [128, 512], F32))
    t1s = Rot(C.sbs(2, [128, 512], F32))
    t2s = Rot(C.sbs(2, [128, 512], F32))
    xts = Rot(C.sbs(3, [128, 256], F32))
    stg = Rot(C.sbs(2, [128, 16, 512], BF16))
    sgl = Rot(C.sbs(2, [128, 512], F32))
    selw = C.sb([128, 2], F32)
    P.ld("sp", selw.t[:], ins["selw"], w=[selw], key="sw")
    W0 = selw.t[:, 0:1]; W1 = selw.t[:, 1:2]
    x = ins["xh"]
    for ch in range(4):
        tok = slice(ch * 512, (ch + 1) * 512)
        tokh = slice(2048 + ch * 512, 2048 + (ch + 1) * 512)
        for name, dst in (("oaT", oaT), ("obT", obT)):
            lo = stg.next(); hi = stg.next()
            for h8 in range(2):
                hs_ = slice(h8 * 8, (h8 + 1) * 8)
                P.ld("sp", lo.t[:, hs_, :], scr[name][hs_, :, tok].rearrange("h p t -> p h t"), w=[lo], key="slo")
                P.ld("sp", hi.t[:, hs_, :], scr[name][hs_, :, tokh].rearrange("h p t -> p h t"), w=[hi], key="shi")
            P.ts("dve", dst.t[:], lo.t[:], W0, None, ALU.mult, r=[lo, selw], w=[dst])
            P.stt("dve", dst.t[:], hi.t[:], W1, dst.t[:], ALU.mult, ALU.add, r=[hi, selw, dst], w=[dst])
        for g in range(8):
            wa, wav = ws.get(ins["wbra"][g], 16, 256)
            wb, wbv = ws.get(ins["wbrb"][g], 16, 256)
            for sub in range(2):
                ct = g * 2 + sub
                pa, paap = C.ps()
                for k in range(16):
                    P.mm(paap[:, 0:512], wav[:, k, sub * 128:(sub + 1) * 128], oaT.t[:, k, :], start=(k == 0), stop=(k == 15), r=[wa, oaT], w=[pa])
                pb, pbap = C.ps()
                for k in range(16):
                    P.mm(pbap[:, 0:512], wbv[:, k, sub * 128:(sub + 1) * 128], obT.t[:, k, :], start=(k == 0), stop=(k == 15), r=[wb, obT], w=[pb])
                ga = sga.next(); gb = sgb.next(); t1 = t1s.next(); t2 = t2s.next()
                for nm_, gt_, kk_ in (("sgA", ga, "ga%d" % (sga.i % 2)), ("sgB", gb, "gb%d" % (sgb.i % 2))):
                    gh_ = sgl.next()
                    P.ld("sp", gt_.t[:], scr[nm_][ct, :, tok], w=[gt_], key=kk_)
                    P.ld("sp", gh_.t[:], scr[nm_][ct, :, tokh], w=[gh_], key="gh%d" % (sgl.i % 2))
                    P.ts("dve", gt_.t[:], gt_.t[:], W0, None, ALU.mult, r=[gt_, selw], w=[gt_])
                    P.stt("dve", gt_.t[:], gh_.t[:], W1, gt_.t[:], ALU.mult, ALU.add, r=[gh_, selw, gt_], w=[gt_])
                P.tt("dve", t1.t[:], paap[:, 0:512], ga.t[:], ALU.mult, r=[pa, ga], w=[t1])
                P.tt("dve", t2.t[:], pbap[:, 0:512], gb.t[:], ALU.mult, r=[pb, gb], w=[t2])
                P.tt("pool", mixT.t[:, ct, :], t1.t[:], t2.t[:], ALU.add, r=[t1, t2], w=[mixT])
        for g in range(8):
            wo, wov = ws.get(ins["wout"][g], 16, 256)
            for tt in range(4):
                rows = slice(ch * 512 + tt * 128, ch * 512 + (tt + 1) * 128)
                po, poap = C.ps()
                for k in range(16):
                    P.mm(poap[:, 0:256], mixT.t[:, k, tt * 128:(tt + 1) * 128], wov[:, k, :], start=(k == 0), stop=(k == 15), r=[wo, mixT], w=[po])
                xt = xts.next()
                kx = "dx%d" % (xts.i % 3)
                P.ld("sp", xt.t[:], x[rows, g * 256:(g + 1) * 256], w=[xt], key=kx)
                P.tt("dve", xt.t[:], poap[:, 0:256], xt.t[:], ALU.add, r=[po, xt], w=[xt])
                P.st("sp", scr["x1"][rows, g * 256:(g + 1) * 256], xt.t[:], r=[xt], key=kx + "s")
    P.barrier()
    P.emit()
    C.close()


def phase_E(nc, P, ins, scr, out):
    C = Ctx(nc, P)
    C.init_psum()
    h2T = C.sb([128, 16, 512], BF16)
    actT = C.sb([128, 44, 512], BF16)
    xts = Rot(C.sbs(1, [128, D], F32))
    hbs = Rot(C.sbs(1, [128, D], BF16))
    gain = C.sb([128, D], F32)
    ssr = Rot(C.sbs(2, [128, 4], F32))
    ident = C.sb([128, 128], BF16)
    gcf = C.sb([128, 8, 128], F32)
    ws = WStream(P, C, 5632, 4, 4, "we")
    sgs = Rot(C.sbs(2, [128, 512], F32))
    pos = Rot(C.sbs(2, [128, 512], F32))
    xcs = Rot(C.sbs(2, [128, 4, 128], F32))
    evrot = Rot(["act", "dve"])
    P.ld("sp", gain.t[:], ins["ffn_norm"][0:1, :].to_broadcast([128, D]), w=[gain], key="c0")
    P.ld("sp", gcf.t[:], ins["gc"], w=[gcf], key="c1")
    P.copy("dve", ident.t[:], gcf.t[:, GC_ID, :], r=[gcf], w=[ident])
    identf = gcf.t[:, GC_ID, :]
    for ch in range(4):
        t0 = ch * 512
        norm_transpose(P, C, scr["x1"][t0:t0 + 512, :], gain, ident, h2T, 0, 4, xts, hbs, hbs.items[0], ssr, evrot)
        for g in range(22):
            wg, wgv = ws.get(ins["wgate"][g], 16, 256)
            wu, wuv = ws.get(ins["wup"][g], 16, 256)
            for sub in range(2):
                ft = g * 2 + sub
                pg, pgap = C.ps()
                for k in range(16):
                    P.mm(pgap[:, 0:512], wgv[:, k, sub * 128:(sub + 1) * 128], h2T.t[:, k, :], start=(k == 0), stop=(k == 15), r=[wg, h2T], w=[pg])
                pu, puap = C.ps()
                for k in range(16):
                    P.mm(puap[:, 0:512], wuv[:, k, sub * 128:(sub + 1) * 128], h2T.t[:, k, :], start=(k == 0), stop=(k == 15), r=[wu, h2T], w=[pu])
                sg = sgs.next()
                P.act(sg.t[:], pgap[:, 0:512], AF.Silu, r=[pg], w=[sg])
                P.tt("dve", actT.t[:, ft, :], sg.t[:], puap[:, 0:512], ALU.mult, r=[sg, pu], w=[actT])
        for g in range(16):
            wd, wdv = ws.get(ins["wdown"][g], 44, 128)
            pd, pdap = C.ps()
            for k in range(44):
                P.mm(pdap[:, 0:512], wdv[:, k, :], actT.t[:, k, :], start=(k == 0), stop=(k == 43), r=[wd, actT], w=[pd])
            po = pos.next()
            P.copy("act", po.t[:], pdap[:, 0:512], r=[pd], w=[po])
            pt, ptap = C.ps()
            for tt in range(4):
                P.tr(ptap[:, tt * 128:(tt + 1) * 128], po.t[:, tt * 128:(tt + 1) * 128], identf, r=[po, gcf], w=[pt])
            xc = xcs.next()
            kx = "ex%d" % (xcs.i % 2)
            P.ld("sp", xc.t[:], scr["x1"][t0:t0 + 512, g * 128:(g + 1) * 128].rearrange("(t p) c -> p t c", p=128), w=[xc], key=kx)
            P.tt("dve", xc.t[:], ptap[:, 0:512].rearrange("p (t c) -> p t c", t=4), xc.t[:], ALU.add, r=[pt, xc], w=[xc])
            P.st("sp", out[t0:t0 + 512, g * 128:(g + 1) * 128].rearrange("(t p) c -> p t c", p=128), xc.t[:], r=[xc], key=kx + "s")
    P.barrier()
    P.emit()
    C.close()


_CACHE = {}


def prepare_inputs(inputs):
    f = lambda a: np.ascontiguousarray(np.asarray(a, dtype=np.float32))
    w_in = f(inputs["w_in"])[0]
    wfm, wtm = split_w_in(w_in)
    c = make_consts()
    common = {
        "attn_norm": f(inputs["attn_norm"])[0:1],
        "ffn_norm": f(inputs["ffn_norm"])[0:1],
        "wfm": wfm, "wtm": wtm,
        "qgain": f(inputs["nsa_q_norm"])[0].reshape(128, 1),
        "kgain": np.ascontiguousarray(f(inputs["nsa_k_norm"])[0].T),
        "posT": np.ascontiguousarray(f(inputs["cmp_pos"])[0].transpose(2, 0, 1)),
        "wcmp": np.ascontiguousarray(f(inputs["w_cmp"])[0].transpose(2, 0, 1, 3)),
        "convT": np.ascontiguousarray(f(inputs["gdn_conv"])[0].reshape(4, 48, 128).transpose(2, 1, 0)),
        "alog": f(inputs["gdn_a_log"])[0:1],
        "dtb": f(inputs["gdn_dt_bias"])[0:1],
        "ogain": f(inputs["gdn_out_norm"])[0:1],
        "wbra": pretile(f(inputs["w_branch_a"])[0], 256),
        "wbrb": pretile(f(inputs["w_branch_b"])[0], 256),
        "wout": pretile(f(inputs["w_out"])[0], 256),
        "wgate": pretile(f(inputs["w_gate"])[0], 256),
        "wup": pretile(f(inputs["w_up"])[0], 256),
        "wdown": pretile(f(inputs["w_down"])[0], 128),
        "agg": c["agg"], "cmpbias": c["cmpbias"], "tria": c["tria"], "trib": c["trib"],
        "expand": c["expand"], "gc": c["gc"], "selmul": c["selmul"], "seladd": c["seladd"],
    }
    x = f(inputs["x"])
    in_maps = []
    for core in range(8):
        m = dict(common)
        m["x"] = np.ascontiguousarray(x[core // 2])
        hf = core % 2
        m["xh"] = np.ascontiguousarray(x[core // 2, hf * 2048:(hf + 1) * 2048])
        sw = np.zeros((128, 2), np.float32)
        sw[:, hf] = 1.0
        m["selw"] = sw
        in_maps.append(m)
    return in_maps


LAST_RESULTS = None


def kernel(**inputs):
    global LAST_RESULTS
    in_maps = prepare_inputs(inputs)
    if "nc" not in _CACHE:
        _CACHE["nc"] = build_program()
    nc = _CACHE["nc"]
    res = run_bass_kernel_spmd(nc, in_maps, core_ids=list(range(8)))
    LAST_RESULTS = res.results
    outs = [np.concatenate([np.asarray(res.results[2 * b]["out"]).reshape(S // 2, D),
                            np.asarray(res.results[2 * b + 1]["out"]).reshape(S // 2, D)], axis=0) for b in range(4)]
    return np.stack(outs, axis=0).astype(np.float32)
```
